# Optimizing a Trainium2 kernel written in Bass

```python
import math
import jax, jax.numpy as jnp
from jax import lax
import numpy as np

D_MODEL = 1024
BATCH = 4
SEQ = 4096
DEPTH = 1

MEM_LEN = 256
D_FF = 2752
MLA_HEADS = 4
MLA_Q_RANK = 384
MLA_KV_RANK = 256
MLA_NOPE = 128
MLA_ROPE = 64
MLA_V = 128
MLA_QK = MLA_NOPE + MLA_ROPE
MLA_WIDTH = MLA_HEADS * MLA_V
SSM_WIDTH = D_MODEL - MLA_WIDTH
SSM_GROUP = 16
SSM_GROUPS = SSM_WIDTH // SSM_GROUP
SSM_STATE = 64
DT_MIN = 1e-3
DT_MAX = 1e-1
XATTN_HEADS = 4
XATTN_HEAD_DIM = 128
XATTN_WIDTH = XATTN_HEADS * XATTN_HEAD_DIM
IN_SPLITS = [MLA_Q_RANK, MLA_Q_RANK + MLA_KV_RANK, MLA_Q_RANK + MLA_KV_RANK + MLA_ROPE]
IN_WIDTH = MLA_Q_RANK + MLA_KV_RANK + MLA_ROPE + SSM_WIDTH
Q_BLOCK = 128
ROPE_THETA = 10000.0
EPS = 1e-6

kernel_name = "hymba_mla_s5_macaron_memxattn"


def rms_norm(x, g):
    xf = x.astype(jnp.float32)
    y = xf * lax.rsqrt(jnp.mean(xf * xf, axis=-1, keepdims=True) + EPS)
    return (y * g.astype(jnp.float32)).astype(x.dtype)


def swiglu(h, w_gate, w_up, w_down):
    return (jax.nn.silu(h @ w_gate) * (h @ w_up)) @ w_down


def rope(x, pos):
    half = x.shape[-1] // 2
    inv = ROPE_THETA ** (-jnp.arange(half, dtype=jnp.float32) / half)
    ang = (pos.astype(jnp.float32)[..., None] * inv)[:, :, None, :]
    cos, sin = jnp.cos(ang), jnp.sin(ang)
    x1 = x[..., :half].astype(jnp.float32)
    x2 = x[..., half:].astype(jnp.float32)
    return jnp.concatenate([x1 * cos - x2 * sin, x2 * cos + x1 * sin], axis=-1).astype(x.dtype)


def causal_block_attention(q, k, v):
    B, S, H, Dk = q.shape
    Dv = v.shape[-1]
    nb = S // Q_BLOCK
    scale = Dk ** -0.5
    qb = q.reshape(B, nb, Q_BLOCK, H, Dk).transpose(1, 0, 2, 3, 4)
    kpos = jnp.arange(S)

    def one_block(args):
        q_blk, i = args
        s = jnp.einsum('bqhd,bkhd->bhqk', q_blk, k).astype(jnp.float32) * scale
        qpos = i * Q_BLOCK + jnp.arange(Q_BLOCK)
        s = jnp.where(qpos[:, None] >= kpos[None, :], s, -jnp.inf)
        p = jax.nn.softmax(s, axis=-1).astype(v.dtype)
        return jnp.einsum('bhqk,bkhd->bqhd', p, v)

    out = lax.map(one_block, (qb, jnp.arange(nb)))
    return out.transpose(1, 0, 2, 3, 4).reshape(B, S, H, Dv)


def mla_mixer(c_q_in, c_kv_in, k_r, pos, q_norm, w_uq, kv_norm, w_ukv, qk_norm_q, qk_norm_k):
    B, S, _ = c_q_in.shape
    c_q = rms_norm(c_q_in, q_norm)
    q = (c_q @ w_uq).reshape(B, S, MLA_HEADS, MLA_QK)
    c_kv = rms_norm(c_kv_in, kv_norm)
    kv = (c_kv @ w_ukv).reshape(B, S, MLA_HEADS, MLA_NOPE + MLA_V)
    k_nope, v = kv[..., :MLA_NOPE], kv[..., MLA_NOPE:]
    k_rope = jnp.broadcast_to(k_r[:, :, None, :], (B, S, MLA_HEADS, MLA_ROPE))
    k = jnp.concatenate([k_nope, k_rope], axis=-1)
    q = rms_norm(q, qk_norm_q)
    k = rms_norm(k, qk_norm_k)
    q = jnp.concatenate([q[..., :MLA_NOPE], rope(q[..., MLA_NOPE:], pos)], axis=-1)
    k = jnp.concatenate([k[..., :MLA_NOPE], rope(k[..., MLA_NOPE:], pos)], axis=-1)
    o = causal_block_attention(q, k, v)
    return o.reshape(B, S, MLA_WIDTH)


def _complex_linear_combine(e1, e2):
    a1r, a1i, b1r, b1i = e1
    a2r, a2i, b2r, b2i = e2
    ar = a2r * a1r - a2i * a1i
    ai = a2r * a1i + a2i * a1r
    br = a2r * b1r - a2i * b1i + b2r
    bi = a2r * b1i + a2i * b1r + b2i
    return (ar, ai, br, bi)


def s5_mixer(u, a_re, a_im, log_dt, b_re, b_im, c_re, c_im, d, w_glu, b_glu):
    B, S, _ = u.shape
    f32 = jnp.float32
    uf = u.astype(f32).reshape(B, S, SSM_GROUPS, SSM_GROUP)
    lr, li = a_re.astype(f32), a_im.astype(f32)
    dt = jnp.exp(log_dt.astype(f32))[:, None]
    decay = jnp.exp(lr * dt)
    ar = decay * jnp.cos(li * dt)
    ai = decay * jnp.sin(li * dt)
    den = lr * lr + li * li
    nr = ar - 1.0
    coef_r = (nr * lr + ai * li) / den
    coef_i = (ai * lr - nr * li) / den
    br, bi = b_re.astype(f32), b_im.astype(f32)
    bbar_r = coef_r[..., None] * br - coef_i[..., None] * bi
    bbar_i = coef_r[..., None] * bi + coef_i[..., None] * br
    bu_r = jnp.einsum('bsgh,gph->bsgp', uf, bbar_r)
    bu_i = jnp.einsum('bsgh,gph->bsgp', uf, bbar_i)
    ar_t = jnp.broadcast_to(ar, bu_r.shape)
    ai_t = jnp.broadcast_to(ai, bu_r.shape)
    _, _, xr, xi = lax.associative_scan(_complex_linear_combine, (ar_t, ai_t, bu_r, bu_i), axis=1)
    y = (jnp.einsum('bsgp,ghp->bsgh', xr, c_re.astype(f32))
         - jnp.einsum('bsgp,ghp->bsgh', xi, c_im.astype(f32))
         + d.astype(f32) * uf)
    y = y.reshape(B, S, SSM_WIDTH)
    g = jax.nn.gelu(y)
    out = g * jax.nn.sigmoid(g @ w_glu.astype(f32) + b_glu.astype(f32))
    return out.astype(u.dtype)


def memory_cross_attention(h, mem, mem_norm, w_q, w_kv, qn, kn, w_o):
    B, S, _ = h.shape
    M = mem.shape[1]
    q = (h @ w_q).reshape(B, S, XATTN_HEADS, XATTN_HEAD_DIM)
    m = rms_norm(mem, mem_norm)
    kv = (m @ w_kv).reshape(B, M, 2, XATTN_HEADS, XATTN_HEAD_DIM)
    k, v = kv[:, :, 0], kv[:, :, 1]
    q = rms_norm(q, qn)
    k = rms_norm(k, kn)
    s = jnp.einsum('bshd,bmhd->bhsm', q, k).astype(jnp.float32) * (XATTN_HEAD_DIM ** -0.5)
    p = jax.nn.softmax(s, axis=-1).astype(v.dtype)
    o = jnp.einsum('bhsm,bmhd->bshd', p, v).reshape(B, S, XATTN_WIDTH)
    return o @ w_o


def setup_inputs(seed: int = 0) -> dict:
    key = jax.random.key(seed)
    ks = iter(jax.random.split(key, 48))
    f32 = jnp.float32

    def nrm(shape):
        return jax.random.normal(next(ks), (DEPTH,) + shape, f32)

    def w(shape, fan_in):
        return nrm(shape) * (fan_in ** -0.5)

    def gain(dim):
        return 1.0 + 0.02 * nrm((dim,))

    x = jax.random.normal(next(ks), (BATCH, SEQ, D_MODEL), f32)
    mem = jax.random.normal(next(ks), (BATCH, MEM_LEN, D_MODEL), f32)
    positions = (jax.random.randint(next(ks), (BATCH, 1), 0, 1024, jnp.int32)
                 + jnp.arange(SEQ, dtype=jnp.int32)[None, :])
    ffn1_norm = gain(D_MODEL)
    ffn1_w_gate = w((D_MODEL, D_FF), D_MODEL)
    ffn1_w_up = w((D_MODEL, D_FF), D_MODEL)
    ffn1_w_down = w((D_FF, D_MODEL), D_FF)
    mix_norm = gain(D_MODEL)
    w_in = w((D_MODEL, IN_WIDTH), D_MODEL)
    mla_q_norm = gain(MLA_Q_RANK)
    mla_w_uq = w((MLA_Q_RANK, MLA_HEADS * MLA_QK), MLA_Q_RANK)
    mla_kv_norm = gain(MLA_KV_RANK)
    mla_w_ukv = w((MLA_KV_RANK, MLA_HEADS * (MLA_NOPE + MLA_V)), MLA_KV_RANK)
    mla_qk_norm_q = gain(MLA_QK)
    mla_qk_norm_k = gain(MLA_QK)
    n = jnp.arange(SSM_STATE, dtype=f32)
    ssm_a_re = -0.5 + 0.01 * nrm((SSM_GROUPS, SSM_STATE))
    ssm_a_im = math.pi * n + 0.01 * nrm((SSM_GROUPS, SSM_STATE))
    ssm_log_dt = jax.random.uniform(next(ks), (DEPTH, SSM_GROUPS), f32, math.log(DT_MIN), math.log(DT_MAX))
    ssm_b_re = nrm((SSM_GROUPS, SSM_STATE, SSM_GROUP)) * (0.5 / SSM_GROUP) ** 0.5
    ssm_b_im = nrm((SSM_GROUPS, SSM_STATE, SSM_GROUP)) * (0.5 / SSM_GROUP) ** 0.5
    ssm_c_re = nrm((SSM_GROUPS, SSM_GROUP, SSM_STATE)) * (0.5 / SSM_STATE) ** 0.5
    ssm_c_im = nrm((SSM_GROUPS, SSM_GROUP, SSM_STATE)) * (0.5 / SSM_STATE) ** 0.5
    ssm_d = nrm((SSM_GROUPS, SSM_GROUP))
    ssm_w_glu = w((SSM_WIDTH, SSM_WIDTH), SSM_WIDTH)
    ssm_b_glu = 0.01 * nrm((SSM_WIDTH,))
    out_norm_mla = gain(MLA_WIDTH)
    out_norm_ssm = gain(SSM_WIDTH)
    w_o = w((MLA_WIDTH + SSM_WIDTH, D_MODEL), MLA_WIDTH + SSM_WIDTH)
    xattn_norm = gain(D_MODEL)
    mem_norm = gain(D_MODEL)
    xattn_w_q = w((D_MODEL, XATTN_WIDTH), D_MODEL)
    xattn_w_kv = w((D_MODEL, 2 * XATTN_WIDTH), D_MODEL)
    xattn_q_norm = gain(XATTN_HEAD_DIM)
    xattn_k_norm = gain(XATTN_HEAD_DIM)
    xattn_w_o = w((XATTN_WIDTH, D_MODEL), XATTN_WIDTH)
    ffn2_norm = gain(D_MODEL)
    ffn2_w_gate = w((D_MODEL, D_FF), D_MODEL)
    ffn2_w_up = w((D_MODEL, D_FF), D_MODEL)
    ffn2_w_down = w((D_FF, D_MODEL), D_FF)
    return {
        'x': x, 'mem': mem, 'positions': positions,
        'ffn1_norm': ffn1_norm, 'ffn1_w_gate': ffn1_w_gate, 'ffn1_w_up': ffn1_w_up, 'ffn1_w_down': ffn1_w_down,
        'mix_norm': mix_norm, 'w_in': w_in,
        'mla_q_norm': mla_q_norm, 'mla_w_uq': mla_w_uq, 'mla_kv_norm': mla_kv_norm, 'mla_w_ukv': mla_w_ukv,
        'mla_qk_norm_q': mla_qk_norm_q, 'mla_qk_norm_k': mla_qk_norm_k,
        'ssm_a_re': ssm_a_re, 'ssm_a_im': ssm_a_im, 'ssm_log_dt': ssm_log_dt,
        'ssm_b_re': ssm_b_re, 'ssm_b_im': ssm_b_im, 'ssm_c_re': ssm_c_re, 'ssm_c_im': ssm_c_im,
        'ssm_d': ssm_d, 'ssm_w_glu': ssm_w_glu, 'ssm_b_glu': ssm_b_glu,
        'out_norm_mla': out_norm_mla, 'out_norm_ssm': out_norm_ssm, 'w_o': w_o,
        'xattn_norm': xattn_norm, 'mem_norm': mem_norm, 'xattn_w_q': xattn_w_q, 'xattn_w_kv': xattn_w_kv,
        'xattn_q_norm': xattn_q_norm, 'xattn_k_norm': xattn_k_norm, 'xattn_w_o': xattn_w_o,
        'ffn2_norm': ffn2_norm, 'ffn2_w_gate': ffn2_w_gate, 'ffn2_w_up': ffn2_w_up, 'ffn2_w_down': ffn2_w_down,
    }


def reference(x, mem, positions,
              ffn1_norm, ffn1_w_gate, ffn1_w_up, ffn1_w_down,
              mix_norm, w_in,
              mla_q_norm, mla_w_uq, mla_kv_norm, mla_w_ukv, mla_qk_norm_q, mla_qk_norm_k,
              ssm_a_re, ssm_a_im, ssm_log_dt, ssm_b_re, ssm_b_im, ssm_c_re, ssm_c_im,
              ssm_d, ssm_w_glu, ssm_b_glu,
              out_norm_mla, out_norm_ssm, w_o,
              xattn_norm, mem_norm, xattn_w_q, xattn_w_kv, xattn_q_norm, xattn_k_norm, xattn_w_o,
              ffn2_norm, ffn2_w_gate, ffn2_w_up, ffn2_w_down):
    for l in range(DEPTH):
        x = x + 0.5 * swiglu(rms_norm(x, ffn1_norm[l]), ffn1_w_gate[l], ffn1_w_up[l], ffn1_w_down[l])
        h = rms_norm(x, mix_norm[l])
        proj = h @ w_in[l]
        c_q_in, c_kv_in, k_r, u = jnp.split(proj, IN_SPLITS, axis=-1)
        y_mla = mla_mixer(c_q_in, c_kv_in, k_r, positions, mla_q_norm[l], mla_w_uq[l],
                          mla_kv_norm[l], mla_w_ukv[l], mla_qk_norm_q[l], mla_qk_norm_k[l])
        y_ssm = s5_mixer(u, ssm_a_re[l], ssm_a_im[l], ssm_log_dt[l], ssm_b_re[l], ssm_b_im[l],
                         ssm_c_re[l], ssm_c_im[l], ssm_d[l], ssm_w_glu[l], ssm_b_glu[l])
        y = jnp.concatenate([rms_norm(y_mla, out_norm_mla[l]), rms_norm(y_ssm, out_norm_ssm[l])], axis=-1)
        x = x + y @ w_o[l]
        x = x + memory_cross_attention(rms_norm(x, xattn_norm[l]), mem, mem_norm[l], xattn_w_q[l],
                                       xattn_w_kv[l], xattn_q_norm[l], xattn_k_norm[l], xattn_w_o[l])
        x = x + 0.5 * swiglu(rms_norm(x, ffn2_norm[l]), ffn2_w_gate[l], ffn2_w_up[l], ffn2_w_down[l])
    return x
```

```python
import math
import contextlib
import numpy as np
import ml_dtypes
import concourse.bass as bass
import concourse.mybir as mybir
from concourse.bass_utils import run_bass_kernel_spmd

F32 = mybir.dt.float32
BF16 = mybir.dt.bfloat16
I32 = mybir.dt.int32
U8 = mybir.dt.uint8
ALU = mybir.AluOpType
AF = mybir.ActivationFunctionType

D = 1024
T = 4096
NOWN = 2048
DFF = 2752
NF = 22
EPS = 1e-6
L = 16
NJ = T // L
TWO_PI = 2.0 * math.pi


class Buf:
    __slots__ = ("w", "r")

    def __init__(self, w=None):
        self.w = w
        self.r = []


class Op:
    __slots__ = ("idx", "eng", "fn", "deps", "dma", "slot", "ticket", "inc", "waits")

    def __init__(self, idx, eng, fn, dma):
        self.idx = idx
        self.eng = eng
        self.fn = fn
        self.deps = {}
        self.dma = dma
        self.slot = None
        self.ticket = None
        self.inc = False
        self.waits = []


class Sched:
    NSLOT = 12

    def __init__(self, nc):
        self.nc = nc
        self.ops = []
        self.slot_ctr = {}
        self.bar = None
        self.touched = set()

    def add(self, eng, fn, reads=(), writes=(), dma=False):
        op = Op(len(self.ops), eng, fn, dma)
        for b in reads:
            if b.w is not None:
                op.deps[b.w] = "raw"
        for b in writes:
            if b.w is not None and b.w not in op.deps:
                op.deps[b.w] = "waw"
            for r in b.r:
                if r not in op.deps:
                    op.deps[r] = "war"
        for b in reads:
            b.r.append(op.idx)
            self.touched.add(b)
        for b in writes:
            b.w = op.idx
            b.r = []
            self.touched.add(b)
        if dma:
            c = self.slot_ctr.get(eng, 0)
            op.slot = (eng, c % self.NSLOT)
            self.slot_ctr[eng] = c + 1
        self.ops.append(op)
        return op

    def emit(self):
        nc = self.nc
        ops = self.ops
        for op in ops:
            best = {}
            for p, kind in op.deps.items():
                P = ops[p]
                if P.dma:
                    key = ("dma",) + P.slot
                else:
                    if P.eng == op.eng and not op.dma:
                        if P.eng == "pe":
                            continue
                    key = ("eng", P.eng)
                if key not in best or best[key] < p:
                    best[key] = p
            op.waits = sorted(best.items(), key=lambda kv: kv[1])
            for _, p in op.waits:
                ops[p].inc = True
        cnt = {}
        for op in ops:
            if op.dma:
                key = ("dma",) + op.slot
                cnt[key] = cnt.get(key, 0) + 1
                op.ticket = 16 * cnt[key]
            elif op.inc:
                key = ("eng", op.eng)
                cnt[key] = cnt.get(key, 0) + 1
                op.ticket = cnt[key]
        keys = sorted(set(cnt.keys()), key=str)
        sems = {}
        with contextlib.ExitStack() as es:
            for k in keys:
                sems[k] = es.enter_context(nc.semaphore("s_" + "_".join(str(x) for x in k)))
            block = es.enter_context(nc.Block())
            by_eng = {}
            for op in ops:
                by_eng.setdefault(op.eng, []).append(op)
            engmap = {"pe": "tensor", "act": "scalar", "dve": "vector", "pool": "gpsimd", "sp": "sync"}

            def make(elist):
                def body(e):
                    seen = {}
                    for op in elist:
                        for key, p in op.waits:
                            v = ops[p].ticket
                            if seen.get(key, 0) < v:
                                e.wait_ge(sems[key], v)
                                seen[key] = v
                        if op.dma:
                            key = ("dma",) + op.slot
                            prev = op.ticket - 16
                            if prev > 0 and seen.get(key, 0) < prev:
                                e.wait_ge(sems[key], prev)
                                seen[key] = prev
                            op.fn(e).then_inc(sems[key], 16)
                        else:
                            ins = op.fn(e)
                            if op.inc:
                                ins.then_inc(sems[("eng", op.eng)], 1)
                    for op in elist:
                        if op.dma:
                            key = ("dma",) + op.slot
                            if seen.get(key, 0) < op.ticket:
                                e.wait_ge(sems[key], op.ticket)
                                seen[key] = op.ticket
                return body

            for engname, elist in by_eng.items():
                getattr(block, engmap[engname])(make(elist))


class Tl:
    def __init__(self, ap, b):
        self.ap = ap
        self.b = b

    def __getitem__(self, k):
        return self.ap[k]


def _dtsize(dt):
    return {F32: 4, BF16: 2, I32: 4, U8: 1}[dt]


class Ctx:
    ARENA = 212480

    def __init__(self, nc):
        self.nc = nc
        self.S = Sched(nc)
        self.arena = nc.alloc_sbuf_tensor("arena", [128, self.ARENA], U8)
        self.sp = 0
        self.barrier_idx = None
        self.psum = [Tl(nc.alloc_psum_tensor("pb%d" % i, [128, 512], F32)[:, :], Buf()) for i in range(8)]
        self.pi = 0
        self.reserved = set()

    def alloc(self, shape, dt, nb=1):
        n = 1
        for s in shape[1:]:
            n *= s
        nbytes = (n * _dtsize(dt) + 63) // 64 * 64
        assert self.sp + nbytes <= self.ARENA, ("SBUF overflow", self.sp, nbytes)
        ap = self.arena[:, self.sp:self.sp + n * _dtsize(dt)].bitcast(dt)
        self.sp += nbytes
        if len(shape) == 3:
            ap = ap.rearrange("p (a b) -> p a b", b=shape[2])
        elif len(shape) == 4:
            ap = ap.rearrange("p (a b c) -> p a b c", b=shape[2], c=shape[3])
        if shape[0] < 128:
            ap = ap[0:shape[0]]
        if nb == 1:
            return Tl(ap, Buf(self.barrier_idx))
        return Tl(ap, [Buf(self.barrier_idx) for _ in range(nb)])

    def ps(self):
        while (self.pi % 8) in self.reserved:
            self.pi += 1
        t = self.psum[self.pi % 8]
        self.pi += 1
        return t

    def barrier(self):
        S = self.S
        bufs = list(S.touched)
        dummy = self._dummy
        op = S.add("dve", lambda e: e.memset(dummy.ap, 0.0), reads=bufs, writes=bufs + [dummy.b])
        self.barrier_idx = op.idx
        S.touched = set()
        for t in self.psum:
            t.b.w = op.idx
            t.b.r = []
        return op.idx

    def mm(self, out, lhsT, rhs, start, stop, reads, writes, **kw):
        self.S.add("pe", lambda e: e.matmul(out, lhsT=lhsT, rhs=rhs, start=start, stop=stop, **kw),
                   reads=reads, writes=writes)

    def act(self, out, in_, func, reads, writes, bias=None, scale=None):
        kw = {}
        if bias is not None:
            kw["bias"] = bias
        if scale is not None:
            kw["scale"] = scale
        self.S.add("act", lambda e: e.activation(out=out, in_=in_, func=func, **kw), reads=reads, writes=writes)

    def ts(self, out, in0, s1, s2, op0, op1, reads, writes, eng="dve"):
        if op1 is None:
            self.S.add(eng, lambda e: e.tensor_scalar(out=out, in0=in0, scalar1=s1, scalar2=None, op0=op0),
                       reads=reads, writes=writes)
        else:
            self.S.add(eng, lambda e: e.tensor_scalar(out=out, in0=in0, scalar1=s1, scalar2=s2, op0=op0, op1=op1),
                       reads=reads, writes=writes)

    def stt(self, out, in0, scalar, in1, op0, op1, reads, writes, eng="dve"):
        self.S.add(eng, lambda e: e.scalar_tensor_tensor(out=out, in0=in0, scalar=scalar, in1=in1, op0=op0, op1=op1),
                   reads=reads, writes=writes)

    def tt(self, out, in0, in1, op, reads, writes, eng="dve"):
        self.S.add(eng, lambda e: e.tensor_tensor(out=out, in0=in0, in1=in1, op=op), reads=reads, writes=writes)

    def cp(self, out, in_, reads, writes, eng="dve"):
        self.S.add(eng, lambda e: e.tensor_copy(out=out, in_=in_), reads=reads, writes=writes)

    def memset(self, out, val, writes, eng="dve"):
        self.S.add(eng, lambda e: e.memset(out, val), writes=writes)

    def acopy(self, out, in_, reads, writes):
        self.S.add("act", lambda e: e.copy(out=out, in_=in_), reads=reads, writes=writes)

    def clamp_pi(self, ap, b):
        self.ts(ap, ap, math.pi, -math.pi, ALU.min, ALU.max, [b], [b])

    def recip(self, out, in_, reads, writes):
        self.S.add("dve", lambda e: e.reciprocal(out=out, in_=in_), reads=reads, writes=writes)

    def dma(self, out, in_, reads, writes, q="sp"):
        self.S.add(q, lambda e: e.dma_start(out=out, in_=in_), reads=reads, writes=writes, dma=True)


def _bl(x):
    return x if isinstance(x, (list, tuple)) else [x]


def build(debug=False, stop=None):
    nc = bass.Bass("TRN2", target_bir_lowering=False)
    C = Ctx(nc)
    S = C.S

    def din(name, shape, dt=F32):
        return nc.dram_tensor(name, list(shape), dt, kind="ExternalInput").ap()

    def dscr(name, shape, dt):
        return Tl(nc.dram_tensor(name, list(shape), dt, kind="Internal").ap(), Buf())

    xT = din("xT", [D, T])
    memT = din("memT", [D, 256])
    pos_all = din("pos_all", [64, T], I32)
    pos_own = din("pos_own", [64, NOWN], I32)
    sel_d = din("sel", [128, 2])
    amask_d = din("amask", [128, 8, 512])
    invf_d = din("invf", [64, 1])
    sgn_d = din("sgn", [64, 1])
    iota_d = din("iota", [128, NJ])
    maskE_d = din("maskE", [128, 2])
    kvec_d = din("kvec", [128, L + 1])
    W = {}
    for nm, shp in [("ffn1_wg", [D, DFF]), ("ffn1_wu", [D, DFF]), ("ffn1_wd", [DFF, D]),
                    ("ffn2_wg", [D, DFF]), ("ffn2_wu", [D, DFF]), ("ffn2_wd", [DFF, D]),
                    ("w_in", [D, 1280]), ("w_uq", [384, 1024]), ("w_ukv", [256, 1024]),
                    ("w_glu", [512, 512]), ("w_o", [D, D]), ("w_xq", [D, 512]), ("w_xkv", [D, 1024]),
                    ("w_xo", [512, D])]:
        W[nm] = din(nm, shp)
    G = {}
    for nm, shp in [("g_ffn1", [128, 8]), ("g_mix", [128, 8]), ("g_q", [128, 3]), ("g_kv", [128, 2]),
                    ("g_qn", [128, 1]), ("g_qr", [64, 1]), ("g_qrs", [64, 1]),
                    ("g_kn", [128, 1]), ("g_kr", [64, 1]), ("g_krs", [64, 1]),
                    ("g_om", [128, 4]), ("g_os", [128, 4]), ("g_xa", [128, 8]), ("g_mem", [128, 8]),
                    ("g_xq", [128, 1]), ("g_xk", [128, 1]), ("g_ffn2", [128, 8]),
                    ("b_glu", [128, 4]), ("d_fm", [128, 4]),
                    ("are_in", [128, 4, 64]), ("aim_in", [128, 4, 64]), ("ldt_in", [128, 4, 64]),
                    ("bre_in", [128, 4, 64]), ("bim_in", [128, 4, 64]),
                    ("are_out", [128, 16]), ("aim_out", [128, 16]), ("ldt_out", [128, 16]),
                    ("cre_out", [128, 16, 16]), ("cim_out", [128, 16, 16]),
                    ("bre_out", [128, 16, 16]), ("bim_out", [128, 16, 16])]:
        G[nm] = din(nm, shp)
    outT = nc.dram_tensor("outT", [D, NOWN], F32, kind="ExternalOutput").ap()
    dbg = {}
    if debug:
        for nm, shp in [("d_x1own", [D, NOWN]), ("d_yssm", [512, NOWN]), ("d_ymla", [512, NOWN]), ("d_x2", [D, NOWN]),
                        ("d_x3", [D, NOWN])]:
            dbg[nm] = Tl(nc.dram_tensor(nm, shp, F32, kind="ExternalOutput").ap(), Buf())

    def wscr(tag):
        a = Tl(nc.dram_tensor("WgB" + tag, [128, 8, DFF], BF16, kind="Internal").ap(), [Buf() for _ in range(11)])
        b = Tl(nc.dram_tensor("WuB" + tag, [128, 8, DFF], BF16, kind="Internal").ap(), [Buf() for _ in range(11)])
        c = Tl(nc.dram_tensor("WdB" + tag, [128, 8, NF, 128], BF16, kind="Internal").ap(), [Buf() for _ in range(8)])
        ct = Tl(c.ap, [Buf() for _ in range(8)])
        return (a, b, c, ct)
    scr1 = wscr("1")
    scr2 = wscr("2")
    KnD = dscr("KnD", [128, 4, T], BF16)
    krrD = dscr("krrD", [64, T], BF16)
    VD = dscr("VD", [128, 32, 512], BF16)
    rstdkD = dscr("rstdkD", [128, 32, 4], F32)
    uD = dscr("uD", [128, 4, T], BF16)
    QnD = dscr("QnD", [128, 4, NOWN], BF16)
    QrD = dscr("QrD", [64, 4, NOWN], BF16)
    x1D = dscr("x1D", [128, 8, NOWN], F32)
    ysD = dscr("ysD", [128, 4, NOWN], BF16)

    C._dummy = C.alloc([128, 8], F32)
    ones_bf = C.alloc([128, 128], BF16)
    eps_t = C.alloc([128, 1], F32)
    sel = C.alloc([128, 2], F32)
    gt = {}
    for nm in ["g_ffn1", "g_mix", "g_q", "g_kv", "g_qn", "g_qr", "g_qrs", "g_kn", "g_kr", "g_krs", "g_om", "g_os",
               "g_xa", "g_mem", "g_xq", "g_xk", "g_ffn2", "b_glu", "d_fm"]:
        shp = [int(s) for s in G[nm].shape]
        gt[nm] = C.alloc(shp, F32)
        C.dma(gt[nm].ap, G[nm], [], [gt[nm].b])
    invf = C.alloc([64, 1], F32)
    sgn = C.alloc([64, 1], F32)
    C.dma(invf.ap, invf_d, [], [invf.b])
    C.dma(sgn.ap, sgn_d, [], [sgn.b])
    C.dma(sel.ap, sel_d, [], [sel.b])
    C.memset(ones_bf.ap, 1.0, [ones_bf.b])
    C.memset(eps_t.ap, EPS, [eps_t.b])
    base_sp = C.sp

    def rstd_from(srcs, N, Dn, sq_t, rstd_t):
        ps = C.ps()
        n = len(srcs)
        for i, (ap, K, bufs) in enumerate(srcs):
            C.act(sq_t.ap[0:K, i, 0:N], ap, AF.Square, _bl(bufs), [sq_t.b])
        for i, (ap, K, bufs) in enumerate(srcs):
            C.mm(ps.ap[:, 0:N], ones_bf.ap[0:K, :], sq_t.ap[0:K, i, 0:N], i == 0, i == n - 1, [ones_bf.b, sq_t.b], [ps.b])
        C.act(rstd_t.ap[:, 0:N], ps.ap[:, 0:N], AF.Sqrt, [ps.b, eps_t.b], [rstd_t.b], bias=eps_t.ap, scale=1.0 / Dn)
        C.recip(rstd_t.ap[:, 0:N], rstd_t.ap[:, 0:N], [rstd_t.b], [rstd_t.b])

    def rope_tables(pos_ap_dram, col0, N, cosT, sinS, tmpi, tmpf, tmpk):
        C.dma(tmpi.ap[:, 0:N], pos_ap_dram[:, col0:col0 + N], [], [tmpi.b])
        C.cp(tmpf.ap[:, 0:N], tmpi.ap[:, 0:N], [tmpi.b], [tmpf.b])
        C.ts(tmpf.ap[:, 0:N], tmpf.ap[:, 0:N], invf.ap, None, ALU.mult, None, [tmpf.b, invf.b], [tmpf.b])
        C.ts(tmpk.ap[:, 0:N], tmpf.ap[:, 0:N], 1.0 / TWO_PI, None, ALU.mult, None, [tmpf.b], [tmpk.b])
        C.cp(tmpi.ap[:, 0:N], tmpk.ap[:, 0:N], [tmpk.b], [tmpi.b])
        C.cp(tmpk.ap[:, 0:N], tmpi.ap[:, 0:N], [tmpi.b], [tmpk.b])
        C.stt(tmpk.ap[:, 0:N], tmpk.ap[:, 0:N], -TWO_PI, tmpf.ap[:, 0:N], ALU.mult, ALU.add, [tmpk.b, tmpf.b], [tmpk.b])
        C.clamp_pi(tmpk.ap[:, 0:N], tmpk.b)
        C.act(sinS.ap[:, 0:N], tmpk.ap[:, 0:N], AF.Sin, [tmpk.b], [sinS.b])
        C.ts(sinS.ap[:, 0:N], sinS.ap[:, 0:N], sgn.ap, None, ALU.mult, None, [sinS.b, sgn.b], [sinS.b])
        C.ts(tmpf.ap[:, 0:N], tmpf.ap[:, 0:N], math.pi / 2, None, ALU.add, None, [tmpf.b], [tmpf.b])
        C.ts(tmpk.ap[:, 0:N], tmpf.ap[:, 0:N], 1.0 / TWO_PI, None, ALU.mult, None, [tmpf.b], [tmpk.b])
        C.cp(tmpi.ap[:, 0:N], tmpk.ap[:, 0:N], [tmpk.b], [tmpi.b])
        C.cp(tmpk.ap[:, 0:N], tmpi.ap[:, 0:N], [tmpi.b], [tmpk.b])
        C.stt(tmpk.ap[:, 0:N], tmpk.ap[:, 0:N], -TWO_PI, tmpf.ap[:, 0:N], ALU.mult, ALU.add, [tmpk.b, tmpf.b], [tmpk.b])
        C.clamp_pi(tmpk.ap[:, 0:N], tmpk.b)
        C.act(cosT.ap[:, 0:N], tmpk.ap[:, 0:N], AF.Sin, [tmpk.b], [cosT.b])

    def ffn_norm_gen(xs, hbf, gname, sq_t, rstd_t):
        N = 512
        rstd_from([(xs.ap[:, k, :], 128, xs.b) for k in range(8)], N, D, sq_t, rstd_t)
        yield
        for k in range(8):
            C.stt(hbf.ap[:, k, :], xs.ap[:, k, :], gt[gname].ap[:, k:k + 1], rstd_t.ap[:, 0:N], ALU.mult, ALU.mult,
                  [xs.b, gt[gname].b, rstd_t.b], [hbf.b])
            if k % 2 == 1:
                yield

    def ffn(xs, hbf, wg_d, wu_d, wd_d, gname, sq_t, rstd_t, actT, wgs, wus, wds, silt, scr, first, bg=None, bgB=None, do_norm=True, nstep=2, bgB_first=False):
        N = 512
        if do_norm:
            for _ in ffn_norm_gen(xs, hbf, gname, sq_t, rstd_t):
                pass
        wgv = wg_d.rearrange("(k p) f -> p k f", p=128)
        wuv = wu_d.rearrange("(k p) f -> p k f", p=128)
        NG = 11
        WgB, WuB, WdB, WdBt = scr

        def load_wd(d):
            slot = d % 3
            if first:
                C.dma(wds.ap[:, slot, 0:21, :], wd_d[0:21 * 128, d * 128:(d + 1) * 128].rearrange("(f p) d -> p f d", p=128),
                      [], [wds.b[slot]], q="pool")
                C.dma(wds.ap[0:64, slot, 21, :], wd_d[21 * 128:DFF, d * 128:(d + 1) * 128], [], [wds.b[slot + 3]], q="pool")
                C.dma(WdB.ap[:, d, 0:21, :], wds.ap[:, slot, 0:21, :], [wds.b[slot]], [WdB.b[d]])
                C.dma(WdB.ap[0:64, d, 21, :], wds.ap[0:64, slot, 21, :], [wds.b[slot + 3]], [WdBt.b[d]])
            else:
                C.dma(wds.ap[:, slot, 0:21, :], WdB.ap[:, d, 0:21, :], [WdB.b[d]], [wds.b[slot]], q="sp")
                C.dma(wds.ap[0:64, slot, 21, :], WdB.ap[0:64, d, 21, :], [WdBt.b[d]], [wds.b[slot + 3]], q="sp")

        for g in range(NG):
            if g in (5, 7, 9):
                load_wd((g - 5) // 2)
            c0 = g * 256
            cw = min(256, DFF - c0)
            slot = g % 3
            if first:
                C.dma(wgs.ap[:, slot, :, 0:cw], wgv[:, :, c0:c0 + cw], [], [wgs.b[slot]], q="pool")
                C.dma(wus.ap[:, slot, :, 0:cw], wuv[:, :, c0:c0 + cw], [], [wus.b[slot]], q="pool")
                C.dma(WgB.ap[:, :, c0:c0 + cw], wgs.ap[:, slot, :, 0:cw], [wgs.b[slot]], [WgB.b[g]])
                C.dma(WuB.ap[:, :, c0:c0 + cw], wus.ap[:, slot, :, 0:cw], [wus.b[slot]], [WuB.b[g]])
            else:
                C.dma(wgs.ap[:, slot, :, 0:cw], WgB.ap[:, :, c0:c0 + cw], [WgB.b[g]], [wgs.b[slot]], q="sp")
                C.dma(wus.ap[:, slot, :, 0:cw], WuB.ap[:, :, c0:c0 + cw], [WuB.b[g]], [wus.b[slot]], q="sp")
            for ff in range(2):
                f = 2 * g + ff
                if f >= NF:
                    break
                fs = min(128, DFF - f * 128)
                pg = C.ps()
                pu = C.ps()
                for k in range(8):
                    C.mm(pg.ap[0:fs, :], wgs.ap[:, slot, k, ff * 128:ff * 128 + fs], hbf.ap[:, k, :], k == 0, k == 7,
                         [wgs.b[slot], hbf.b], [pg.b])
                for k in range(8):
                    C.mm(pu.ap[0:fs, :], wus.ap[:, slot, k, ff * 128:ff * 128 + fs], hbf.ap[:, k, :], k == 0, k == 7,
                         [wus.b[slot], hbf.b], [pu.b])
                C.act(silt.ap[0:fs, f % 2, :], pg.ap[0:fs, :], AF.Silu, [pg.b], [silt.b[f % 2]])
                C.tt(actT.ap[0:fs, f, :], pu.ap[0:fs, :], silt.ap[0:fs, f % 2, :], ALU.mult, [pu.b, silt.b[f % 2]], [actT.b[f]])
                if bg is not None:
                    for _ in range(nstep):
                        next(bg, None)
        for d in range(8):
            slot = d % 3
            if d + 3 < 8:
                pass
            po = C.ps()
            for f in range(NF):
                fs = min(128, DFF - f * 128)
                C.mm(po.ap, wds.ap[0:fs, slot, f, :], actT.ap[0:fs, f, :], f == 0, f == NF - 1,
                     [wds.b[slot + (3 if f == NF - 1 else 0)], actT.b[f]], [po.b])
            C.stt(xs.ap[:, d, :], po.ap, 0.5, xs.ap[:, d, :], ALU.mult, ALU.add, [po.b, xs.b], [xs.b])
            if d + 3 < 8:
                load_wd(d + 3)
            if bgB is not None and bgB_first:
                next(bgB, None)
            bg_done = True
            if bg is not None:
                for _ in range(2 * nstep):
                    if next(bg, "END") == "END":
                        bg = None
                        break
                bg_done = bg is None
            if bgB is not None and bg_done and not bgB_first:
                next(bgB, None)
        if bg is not None:
            for _ in bg:
                pass
        if bgB is not None:
            for _ in bgB:
                pass

    def select_own(out3, src_fn, reads, writes):
        C.ts(out3[:, :, 0:128], src_fn(0), sel.ap[:, 0:1], None, ALU.mult, None, reads + [sel.b], writes)
        C.stt(out3[:, :, 0:128], src_fn(1), sel.ap[:, 1:2], out3[:, :, 0:128], ALU.mult, ALU.add, reads + [sel.b] + writes, writes)
        C.ts(out3[:, :, 128:256], src_fn(3), sel.ap[:, 0:1], None, ALU.mult, None, reads + [sel.b], writes)
        C.stt(out3[:, :, 128:256], src_fn(2), sel.ap[:, 1:2], out3[:, :, 128:256], ALU.mult, ALU.add, reads + [sel.b] + writes, writes)

    mK = C.alloc([128, 4, 256], BF16)
    mV = C.alloc([128, 2, 512], BF16)
    base_sp = C.sp
    C.barrier()
    if True:
        mx = C.alloc([128, 8, 256], F32)
        mh = C.alloc([128, 8, 256], BF16)
        sq_t = C.alloc([128, 8, 512], BF16)
        rstd_t = C.alloc([128, 512], F32)
        wkv = C.alloc([128, 8, 1024], BF16)
        C.dma(mx.ap, memT.rearrange("(k p) t -> p k t", p=128), [], [mx.b])
        C.dma(wkv.ap, W["w_xkv"].rearrange("(k p) f -> p k f", p=128), [], [wkv.b], q="pool")
        rstd_from([(mx.ap[:, k, :], 128, mx.b) for k in range(8)], 256, D, sq_t, rstd_t)
        for k in range(8):
            C.stt(mh.ap[:, k, :], mx.ap[:, k, :], gt["g_mem"].ap[:, k:k + 1], rstd_t.ap[:, 0:256], ALU.mult, ALU.mult,
                  [mx.b, gt["g_mem"].b, rstd_t.b], [mh.b])
        rk = C.alloc([128, 512], F32)
        for h in range(4):
            pk = C.ps()
            for k in range(8):
                C.mm(pk.ap[:, 0:256], wkv.ap[:, k, h * 128:(h + 1) * 128], mh.ap[:, k, :], k == 0, k == 7, [wkv.b, mh.b], [pk.b])
            rstd_from([(pk.ap[:, 0:256], 128, pk.b)], 256, 128, sq_t, rk)
            C.stt(mK.ap[:, h, :], pk.ap[:, 0:256], gt["g_xk"].ap[:, 0:1], rk.ap[:, 0:256], ALU.mult, ALU.mult,
                  [pk.b, gt["g_xk"].b, rk.b], [mK.b])
        for j in range(2):
            pv = C.ps()
            for k in range(8):
                C.mm(pv.ap, mh.ap[:, k, j * 128:(j + 1) * 128], wkv.ap[:, k, 512:1024], k == 0, k == 7, [wkv.b, mh.b], [pv.b])
            C.acopy(mV.ap[:, j, :], pv.ap, [pv.b], [mV.b])

    if stop == "M":
        S.emit()
        return nc
    C.barrier()
    C.sp = base_sp
    xs = C.alloc([128, 8, 512], F32)
    xs_b = C.alloc([128, 8, 512], F32)
    hbf = C.alloc([128, 8, 512], BF16)
    hbf2 = C.alloc([128, 8, 512], BF16)
    sq_t = C.alloc([128, 8, 512], BF16)
    rstd_t = C.alloc([128, 512], F32)
    actT = C.alloc([128, NF, 512], BF16, nb=NF)
    wgs = C.alloc([128, 3, 8, 256], BF16, nb=3)
    wus = C.alloc([128, 3, 8, 256], BF16, nb=3)
    wds = C.alloc([128, 3, NF, 128], BF16, nb=6)
    silt = C.alloc([128, 2, 512], F32, nb=2)
    w_in = C.alloc([128, 8, 1280], BF16)
    w_ukv = C.alloc([128, 2, 1024], BF16)
    x1own = C.alloc([128, 8, 128], F32)
    cqo = C.alloc([128, 3, 256], F32)
    cqn = C.alloc([128, 3, 256], BF16)
    w_uq = C.alloc([128, 3, 1024], BF16)
    C.dma(w_uq.ap, W["w_uq"].rearrange("(k p) f -> p k f", p=128), [], [w_uq.b], q="pool")
    Qn_c = C.alloc([128, 4, 256], BF16)
    Qr_c = C.alloc([64, 4, 256], BF16)
    cosO = C.alloc([64, 256], F32)
    sinO = C.alloc([64, 256], F32)
    ckvn = C.alloc([128, 2, 512], BF16)
    r2 = C.alloc([128, 512], F32)
    cosT = C.alloc([64, 512], F32)
    sinS = C.alloc([64, 512], F32)
    tmpi = C.alloc([64, 512], I32)
    tmpf = C.alloc([64, 512], F32)
    tmpk = C.alloc([64, 512], F32)
    t1 = C.alloc([64, 512], F32)
    t2 = C.alloc([64, 512], F32)
    krr = C.alloc([64, 512], BF16)
    krsq = C.alloc([64, 512], BF16)
    t3 = C.alloc([64, 512], F32)
    t4 = C.alloc([64, 512], F32)
    knb = None
    knsq = C.alloc([128, 4, 512], BF16)
    onehot = C.alloc([128, 4, 4], BF16)
    ub = C.alloc([128, 4, 512], BF16)
    vb = ub
    knb = ub
    rk = C.alloc([128, 4, 4], F32)
    w_in_v = W["w_in"].rearrange("(k p) f -> p k f", p=128)
    for cc in range(2):
        C.dma(w_in.ap[:, :, cc * 640:(cc + 1) * 640], w_in_v[:, :, cc * 640:(cc + 1) * 640], [], [w_in.b], q="pool")
    C.dma(w_ukv.ap, W["w_ukv"].rearrange("(k p) f -> p k f", p=128), [], [w_ukv.b], q="pool")
    onehot_f = C.alloc([128, 4, 4], F32)
    C.memset(onehot_f.ap, 0.0, [onehot_f.b])
    for h in range(4):
        C.memset(onehot_f.ap[:, h, h:h + 1], 1.0, [onehot_f.b])
    C.cp(onehot.ap, onehot_f.ap, [onehot_f.b], [onehot.b])
    xTv = xT.rearrange("(k p) t -> p k t", p=128)
    if stop == "A0":
        S.emit()
        return nc
    def post_gen(c, xs):
        t0 = c * 512
        for (ob, ba, bb) in ((0, 0, 1), (1, 3, 2)):
            C.ts(x1own.ap, xs.ap[:, :, ba * 128:(ba + 1) * 128], sel.ap[:, 0:1], None, ALU.mult, None, [xs.b, sel.b], [x1own.b])
            C.stt(x1own.ap, xs.ap[:, :, bb * 128:(bb + 1) * 128], sel.ap[:, 1:2], x1own.ap, ALU.mult, ALU.add, [xs.b, sel.b, x1own.b], [x1own.b])
            C.dma(x1D.ap[:, :, c * 256 + ob * 128:c * 256 + (ob + 1) * 128], x1own.ap, [x1own.b], [x1D.b], q="pool")
            yield
        rstd_from([(xs.ap[:, k, :], 128, xs.b) for k in range(8)], 512, D, sq_t, rstd_t)
        yield
        for k in range(8):
            C.stt(hbf2.ap[:, k, :], xs.ap[:, k, :], gt["g_mix"].ap[:, k:k + 1], rstd_t.ap, ALU.mult, ALU.mult,
                  [xs.b, gt["g_mix"].b, rstd_t.b], [hbf2.b])
            yield
        for i in range(3):
            p = C.ps()
            for k in range(8):
                C.mm(p.ap, w_in.ap[:, k, i * 128:(i + 1) * 128], hbf2.ap[:, k, :], k == 0, k == 7, [w_in.b, hbf2.b], [p.b])
            pv3 = p.ap.rearrange("p (a t) -> p a t", a=1)
            select_own(cqo.ap[:, i:i + 1, :], lambda blk: pv3[:, :, blk * 128:(blk + 1) * 128], [p.b], [cqo.b])
            yield
        rstd_from([(cqo.ap[:, i, :], 128, cqo.b) for i in range(3)], 256, 384, sq_t, r2)
        yield
        for i in range(3):
            C.stt(cqn.ap[:, i, :], cqo.ap[:, i, :], gt["g_q"].ap[:, i:i + 1], r2.ap[:, 0:256], ALU.mult, ALU.mult,
                  [cqo.b, gt["g_q"].b, r2.b], [cqn.b])
            yield
        rope_tables(pos_own, c * 256, 256, cosO, sinO, tmpi, tmpf, tmpk)
        yield
        for h in range(4):
            pn = C.ps()
            pr = C.ps()
            prs = C.ps()
            for k in range(3):
                C.mm(pn.ap[:, 0:256], w_uq.ap[:, k, h * 256:h * 256 + 128], cqn.ap[:, k, :], k == 0, k == 2, [w_uq.b, cqn.b], [pn.b])
            for k in range(3):
                C.mm(pr.ap[0:64, 0:256], w_uq.ap[:, k, h * 256 + 128:h * 256 + 192], cqn.ap[:, k, :], k == 0, k == 2, [w_uq.b, cqn.b], [pr.b])
            for k in range(3):
                C.mm(prs.ap[0:64, 0:256], w_uq.ap[:, k, h * 256 + 192:h * 256 + 256], cqn.ap[:, k, :], k == 0, k == 2, [w_uq.b, cqn.b], [prs.b])
            rstd_from([(pn.ap[:, 0:256], 128, pn.b), (pr.ap[0:64, 0:256], 64, pr.b)], 256, 192, sq_t, r2)
            C.stt(Qn_c.ap[:, h, :], pn.ap[:, 0:256], gt["g_qn"].ap[:, 0:1], r2.ap[:, 0:256], ALU.mult, ALU.mult, [pn.b, gt["g_qn"].b, r2.b], [Qn_c.b])
            C.stt(t1.ap[:, 0:256], pr.ap[0:64, 0:256], gt["g_qr"].ap, cosO.ap, ALU.mult, ALU.mult, [pr.b, gt["g_qr"].b, cosO.b, r2.b], [t1.b])
            C.stt(t2.ap[:, 0:256], prs.ap[0:64, 0:256], gt["g_qrs"].ap, sinO.ap, ALU.mult, ALU.mult, [prs.b, gt["g_qrs"].b, sinO.b, r2.b], [t2.b])
            C.tt(t1.ap[:, 0:256], t1.ap[:, 0:256], t2.ap[:, 0:256], ALU.add, [t1.b, t2.b], [t1.b])
            C.tt(Qr_c.ap[:, h, :], t1.ap[:, 0:256], r2.ap[0:64, 0:256], ALU.mult, [t1.b, r2.b], [Qr_c.b])
            yield
        C.dma(QnD.ap[:, :, c * 256:(c + 1) * 256], Qn_c.ap, [Qn_c.b], [QnD.b], q="pool")
        C.dma(QrD.ap[:, :, c * 256:(c + 1) * 256], Qr_c.ap, [Qr_c.b], [QrD.b], q="pool")
        yield
        pkv = [C.ps(), C.ps()]
        for i in range(2):
            for k in range(8):
                C.mm(pkv[i].ap, w_in.ap[:, k, 384 + i * 128:384 + (i + 1) * 128], hbf2.ap[:, k, :], k == 0, k == 7, [w_in.b, hbf2.b], [pkv[i].b])
        rstd_from([(pkv[i].ap, 128, pkv[i].b) for i in range(2)], 512, 256, sq_t, r2)
        for i in range(2):
            C.stt(ckvn.ap[:, i, :], pkv[i].ap, gt["g_kv"].ap[:, i:i + 1], r2.ap, ALU.mult, ALU.mult,
                  [pkv[i].b, gt["g_kv"].b, r2.b], [ckvn.b])
        yield
        rope_tables(pos_all, t0, 512, cosT, sinS, tmpi, tmpf, tmpk)
        yield
        pk1 = C.ps()
        pk2 = C.ps()
        for k in range(8):
            C.mm(pk1.ap[0:64, :], w_in.ap[:, k, 640:704], hbf2.ap[:, k, :], k == 0, k == 7, [w_in.b, hbf2.b], [pk1.b])
        for k in range(8):
            C.mm(pk2.ap[0:64, :], w_in.ap[:, k, 704:768], hbf2.ap[:, k, :], k == 0, k == 7, [w_in.b, hbf2.b], [pk2.b])
        C.acopy(t3.ap, pk1.ap[0:64, :], [pk1.b], [t3.b])
        C.acopy(t4.ap, pk2.ap[0:64, :], [pk2.b], [t4.b])
        yield
        C.act(krsq.ap, t3.ap, AF.Square, [t3.b], [krsq.b])
        yield
        C.stt(t1.ap, t3.ap, gt["g_kr"].ap, cosT.ap, ALU.mult, ALU.mult, [t3.b, gt["g_kr"].b, cosT.b], [t1.b])
        yield
        C.stt(t2.ap, t4.ap, gt["g_krs"].ap, sinS.ap, ALU.mult, ALU.mult, [t4.b, gt["g_krs"].b, sinS.b], [t2.b])
        yield
        C.tt(krr.ap, t1.ap, t2.ap, ALU.add, [t1.b, t2.b], [krr.b])
        yield
        C.dma(krrD.ap[:, t0:t0 + 512], krr.ap, [krr.b], [krrD.b], q="pool")
        yield
        for i in range(4):
            p = C.ps()
            for k in range(8):
                C.mm(p.ap, w_in.ap[:, k, 768 + i * 128:768 + (i + 1) * 128], hbf2.ap[:, k, :], k == 0, k == 7, [w_in.b, hbf2.b], [p.b])
            C.acopy(ub.ap[:, i, :], p.ap, [p.b], [ub.b])
            yield
        C.dma(uD.ap[:, :, t0:t0 + 512], ub.ap, [ub.b], [uD.b], q="pool")
        yield
        for h in range(4):
            p = C.ps()
            for k in range(2):
                C.mm(p.ap, w_ukv.ap[:, k, h * 128:(h + 1) * 128], ckvn.ap[:, k, :], k == 0, k == 1, [w_ukv.b, ckvn.b], [p.b])
            C.act(knsq.ap[:, h, :], p.ap, AF.Square, [p.b], [knsq.b])
            C.ts(knb.ap[:, h, :], p.ap, gt["g_kn"].ap[:, 0:1], None, ALU.mult, None, [p.b, gt["g_kn"].b, knsq.b], [knb.b])
            yield
        C.dma(KnD.ap[:, :, t0:t0 + 512], knb.ap, [knb.b], [KnD.b], q="pool")
        yield
        pq = C.ps()
        for blk in range(4):
            for h in range(4):
                C.mm(pq.ap[:, blk * 4:blk * 4 + 4], knsq.ap[:, h, blk * 128:(blk + 1) * 128], onehot.ap[:, h, :], h == 0, False,
                     [knsq.b, onehot.b], [pq.b])
            C.mm(pq.ap[:, blk * 4:blk * 4 + 4], krsq.ap[:, blk * 128:(blk + 1) * 128], ones_bf.ap[0:64, 0:4], False, True,
                 [krsq.b, ones_bf.b], [pq.b])
        rkf = rk.ap.rearrange("p a b -> p (a b)")
        C.act(rkf, pq.ap[:, 0:16], AF.Sqrt, [pq.b, eps_t.b], [rk.b], bias=eps_t.ap, scale=1.0 / 192.0)
        yield
        C.recip(rkf, rkf, [rk.b], [rk.b])
        yield
        C.ts(rkf, rkf, 192.0 ** -0.5, None, ALU.mult, None, [rk.b], [rk.b])
        yield
        C.dma(rstdkD.ap[:, c * 4:(c + 1) * 4, :], rk.ap, [rk.b], [rstdkD.b], q="pool")
        yield
        for blk in range(4):
            p = C.ps()
            for k in range(2):
                C.mm(p.ap, ckvn.ap[:, k, blk * 128:(blk + 1) * 128], w_ukv.ap[:, k, 512:1024], k == 0, k == 1, [w_ukv.b, ckvn.b], [p.b])
            C.acopy(vb.ap[:, blk, :], p.ap, [p.b], [vb.b])
            yield
        C.dma(VD.ap[:, c * 4:(c + 1) * 4, :], vb.ap, [vb.b], [VD.b], q="pool")
        yield
        yield

    xs_bufs = [xs, xs_b]

    def pre_gen(c):
        xb = xs_bufs[c % 2]
        C.dma(xb.ap, xTv[:, :, c * 512:(c + 1) * 512], [], [xb.b])
        yield
        for _ in ffn_norm_gen(xb, hbf, "g_ffn1", sq_t, rstd_t):
            yield

    POST_STEP = 2
    for _ in pre_gen(0):
        pass
    prev = None
    for c in range(8):
        xs = xs_bufs[c % 2]
        ffn(xs, hbf, W["ffn1_wg"], W["ffn1_wu"], W["ffn1_wd"], "g_ffn1", sq_t, rstd_t, actT, wgs, wus, wds, silt, scr1, c == 0,
            bg=prev, bgB=(pre_gen(c + 1) if c + 1 < 8 else None), do_norm=False, nstep=POST_STEP, bgB_first=True)
        prev = post_gen(c, xs)
    for _ in prev:
        pass


    if stop == "A":
        S.emit()
        return nc
    C.barrier()
    C.sp = base_sp
    WgB2, WuB2, WdB2, WdB2t = scr2
    for (src, dst) in ((W["ffn2_wg"], WgB2), (W["ffn2_wu"], WuB2)):
        sv = src.rearrange("(k p) f -> p k f", p=128)
        for cc in range(4):
            C.dma(dst.ap[:, :, cc * 688:(cc + 1) * 688], sv[:, :, cc * 688:(cc + 1) * 688], [], list(dst.b), q="pool")
    wd2 = W["ffn2_wd"]
    for d in range(8):
        C.dma(WdB2.ap[:, d, 0:21, :], wd2[0:21 * 128, d * 128:(d + 1) * 128].rearrange("(f p) c -> p f c", p=128),
              [], [WdB2.b[d]], q="pool")
    C.dma(WdB2.ap[0:64, :, 21, :], wd2[21 * 128:DFF, :].rearrange("p (d c) -> p d c", c=128), [], list(WdB2t.b), q="pool")
    ssm_stage(C, G, W, gt, sel, uD, ysD, dbg, eps_t, ones_bf, iota_d, maskE_d, rstd_from, kvec_d)

    if stop == "B":
        S.emit()
        return nc
    ymD = dscr("ymD", [128, 4, NOWN], BF16)
    x3D = dscr("x3D", [128, 8, NOWN], F32)
    C.barrier()
    C.sp = base_sp
    Kn = C.alloc([128, 4, T], BF16)
    krA = C.alloc([64, T], BF16)
    Vv = C.alloc([128, 32, 512], BF16)
    rkA = C.alloc([128, 32, 4], F32)
    amask = C.alloc([128, 8, 512], BF16)
    C.dma(Kn.ap, KnD.ap, [KnD.b], [Kn.b])
    C.dma(krA.ap, krrD.ap, [krrD.b], [krA.b])
    C.dma(Vv.ap, VD.ap, [VD.b], [Vv.b])
    C.dma(rkA.ap, rstdkD.ap, [rstdkD.b], [rkA.b])
    C.dma(amask.ap, amask_d, [], [amask.b], q="pool")
    sq_t = C.alloc([128, 8, 512], BF16)
    r2 = C.alloc([128, 512], F32)
    ymn = C.alloc([128, 4, 512], BF16)
    ymla = C.alloc([128, 4, 512], F32)
    cosT = C.alloc([64, 512], F32)
    sinS = C.alloc([64, 512], F32)
    tmpi = C.alloc([64, 512], I32)
    tmpf = C.alloc([64, 512], F32)
    tmpk = C.alloc([64, 512], F32)
    t1 = C.alloc([64, 512], F32)
    t2 = C.alloc([64, 512], F32)
    Qn2 = [C.alloc([128, 4, 512], BF16) for _ in range(2)]
    Qr2 = [C.alloc([64, 4, 512], BF16) for _ in range(2)]
    PT = C.alloc([128, 6, 512], BF16, nb=6)
    rden = C.alloc([128, 512], F32)
    pti = 0
    for m in range(4):
        o0 = m * 512
        Qn = Qn2[m % 2]
        Qr = Qr2[m % 2]
        if m == 0:
            C.dma(Qn.ap, QnD.ap[:, :, 0:512], [QnD.b], [Qn.b])
            C.dma(Qr.ap, QrD.ap[:, :, 0:512], [QrD.b], [Qr.b])
        if m + 1 < 4:
            C.dma(Qn2[(m + 1) % 2].ap, QnD.ap[:, :, o0 + 512:o0 + 1024], [QnD.b], [Qn2[(m + 1) % 2].b])
            C.dma(Qr2[(m + 1) % 2].ap, QrD.ap[:, :, o0 + 512:o0 + 1024], [QrD.b], [Qr2[(m + 1) % 2].b])
        nkb = 8 * m + 8
        C.reserved = {4, 5, 6, 7}
        for h in range(4):
            po = C.psum[4 + 2 * (h % 2)]
            pd = C.psum[5 + 2 * (h % 2)]
            def score(kb):
                pst = C.ps()
                C.mm(pst.ap, Kn.ap[:, h, kb * 128:(kb + 1) * 128], Qn.ap[:, h, :], True, False, [Kn.b, Qn.b], [pst.b])
                C.mm(pst.ap, krA.ap[:, kb * 128:(kb + 1) * 128], Qr.ap[:, h, :], False, True, [krA.b, Qr.b], [pst.b])
                return pst
            LOOK = 3
            pend = [score(kb) for kb in range(min(LOOK, nkb))]
            for kb in range(nkb):
                pst = pend.pop(0)
                if kb + LOOK < nkb:
                    pend.append(score(kb + LOOK))
                sl = pti % 6
                pti += 1
                C.act(PT.ap[:, sl, :], pst.ap, AF.Exp, [pst.b, rkA.b], [PT.b[sl]], scale=rkA.ap[:, kb, h:h + 1])
                if kb >= 8 * m:
                    C.tt(PT.ap[:, sl, :], PT.ap[:, sl, :], amask.ap[:, kb - 8 * m, :], ALU.mult, [PT.b[sl], amask.b], [PT.b[sl]])
                C.mm(po.ap, Vv.ap[:, kb, h * 128:(h + 1) * 128], PT.ap[:, sl, :], kb == 0, kb == nkb - 1, [Vv.b, PT.b[sl]], [po.b])
                C.mm(pd.ap, ones_bf.ap, PT.ap[:, sl, :], kb == 0, kb == nkb - 1, [ones_bf.b, PT.b[sl]], [pd.b])
            C.recip(rden.ap, pd.ap, [pd.b], [rden.b])
            C.tt(ymla.ap[:, h, :], po.ap, rden.ap, ALU.mult, [po.b, rden.b], [ymla.b])
        C.reserved = set()
        if debug:
            C.dma(dbg["d_ymla"].ap.rearrange("(k p) t -> p k t", p=128)[:, :, o0:o0 + 512], ymla.ap, [ymla.b], [dbg["d_ymla"].b])
        rstd_from([(ymla.ap[:, k, :], 128, ymla.b) for k in range(4)], 512, 512, sq_t, r2)
        for k in range(4):
            C.stt(ymn.ap[:, k, :], ymla.ap[:, k, :], gt["g_om"].ap[:, k:k + 1], r2.ap, ALU.mult, ALU.mult,
                  [ymla.b, gt["g_om"].b, r2.b], [ymn.b])
        C.dma(ymD.ap[:, :, o0:o0 + 512], ymn.ap, [ymn.b], [ymD.b])

    if stop == "C1":
        S.emit()
        return nc
    C.barrier()
    C.sp = base_sp
    w_o = C.alloc([128, 8, 1024], BF16)
    w_xq = C.alloc([128, 8, 512], BF16)
    w_xo = C.alloc([128, 4, 1024], BF16)
    C.dma(w_o.ap, W["w_o"].rearrange("(k p) f -> p k f", p=128), [], [w_o.b], q="pool")
    C.dma(w_xq.ap, W["w_xq"].rearrange("(k p) f -> p k f", p=128), [], [w_xq.b], q="pool")
    C.dma(w_xo.ap, W["w_xo"].rearrange("(k p) f -> p k f", p=128), [], [w_xo.b], q="pool")
    xs = C.alloc([128, 8, 512], F32)
    hbf = C.alloc([128, 8, 512], BF16)
    sq_t = C.alloc([128, 8, 512], BF16)
    rstd_t = C.alloc([128, 512], F32)
    r2 = C.alloc([128, 512], F32)
    ycat = C.alloc([128, 8, 512], BF16, nb=2)
    PT = C.alloc([128, 6, 512], BF16, nb=6)
    rden = C.alloc([128, 512], F32)
    qx = C.alloc([128, 4, 512], BF16)
    ox = C.alloc([128, 4, 512], BF16)
    actT = C.alloc([128, NF, 512], BF16, nb=NF)
    wgs = C.alloc([128, 3, 8, 256], BF16, nb=3)
    wus = C.alloc([128, 3, 8, 256], BF16, nb=3)
    wds = C.alloc([128, 3, NF, 128], BF16, nb=6)
    silt = C.alloc([128, 2, 512], F32, nb=2)
    xs_b = C.alloc([128, 8, 512], F32)
    hbfx = C.alloc([128, 8, 512], BF16)
    xs_bufs = [xs, xs_b]
    pti_box = [0]

    def pre2_gen(m):
        o0 = m * 512
        xs = xs_bufs[m % 2]
        C.dma(xs.ap, x1D.ap[:, :, o0:o0 + 512], [x1D.b], [xs.b])
        C.dma(ycat.ap[:, 0:4, :], ymD.ap[:, :, o0:o0 + 512], [ymD.b], [ycat.b[0]])
        C.dma(ycat.ap[:, 4:8, :], ysD.ap[:, :, o0:o0 + 512], [ysD.b], [ycat.b[1]])
        if debug:
            C.dma(dbg["d_x1own"].ap.rearrange("(k p) t -> p k t", p=128)[:, :, o0:o0 + 512], xs.ap, [xs.b], [dbg["d_x1own"].b])
        yield
        for d in range(8):
            p = C.ps()
            for k in range(8):
                C.mm(p.ap, w_o.ap[:, k, d * 128:(d + 1) * 128], ycat.ap[:, k, :], k == 0, k == 7, [w_o.b, ycat.b[k // 4]], [p.b])
            C.tt(xs.ap[:, d, :], p.ap, xs.ap[:, d, :], ALU.add, [p.b, xs.b], [xs.b])
            yield
        if debug:
            C.dma(dbg["d_x2"].ap.rearrange("(k p) t -> p k t", p=128)[:, :, o0:o0 + 512], xs.ap, [xs.b], [dbg["d_x2"].b])
        rstd_from([(xs.ap[:, k, :], 128, xs.b) for k in range(8)], 512, D, sq_t, rstd_t)
        yield
        for k in range(8):
            C.stt(hbfx.ap[:, k, :], xs.ap[:, k, :], gt["g_xa"].ap[:, k:k + 1], rstd_t.ap, ALU.mult, ALU.mult,
                  [xs.b, gt["g_xa"].b, rstd_t.b], [hbfx.b])
            if k % 2 == 1:
                yield
        for h in range(4):
            p = C.ps()
            for k in range(8):
                C.mm(p.ap, w_xq.ap[:, k, h * 128:(h + 1) * 128], hbfx.ap[:, k, :], k == 0, k == 7, [w_xq.b, hbfx.b], [p.b])
            rstd_from([(p.ap, 128, p.b)], 512, 128, sq_t, r2)
            C.stt(qx.ap[:, h, :], p.ap, gt["g_xq"].ap[:, 0:1], r2.ap, ALU.mult, ALU.mult, [p.b, gt["g_xq"].b, r2.b], [qx.b])
            yield
        for h in range(4):
            po = C.ps()
            pd = C.ps()
            for j in range(2):
                pst = C.ps()
                C.mm(pst.ap, mK.ap[:, h, j * 128:(j + 1) * 128], qx.ap[:, h, :], True, True, [mK.b, qx.b], [pst.b])
                sl = pti_box[0] % 6
                pti_box[0] += 1
                C.act(PT.ap[:, sl, :], pst.ap, AF.Exp, [pst.b], [PT.b[sl]], scale=128.0 ** -0.5)
                C.mm(po.ap, mV.ap[:, j, h * 128:(h + 1) * 128], PT.ap[:, sl, :], j == 0, j == 1, [mV.b, PT.b[sl]], [po.b])
                C.mm(pd.ap, ones_bf.ap, PT.ap[:, sl, :], j == 0, j == 1, [ones_bf.b, PT.b[sl]], [pd.b])
            C.recip(rden.ap, pd.ap, [pd.b], [rden.b])
            C.tt(ox.ap[:, h, :], po.ap, rden.ap, ALU.mult, [po.b, rden.b], [ox.b])
            yield
        for d in range(8):
            p = C.ps()
            for k in range(4):
                C.mm(p.ap, w_xo.ap[:, k, d * 128:(d + 1) * 128], ox.ap[:, k, :], k == 0, k == 3, [w_xo.b, ox.b], [p.b])
            C.tt(xs.ap[:, d, :], p.ap, xs.ap[:, d, :], ALU.add, [p.b, xs.b], [xs.b])
            yield
        if debug:
            C.dma(dbg["d_x3"].ap.rearrange("(k p) t -> p k t", p=128)[:, :, o0:o0 + 512], xs.ap, [xs.b], [dbg["d_x3"].b])
        yield

    for _ in pre2_gen(0):
        pass
    for m in range(4):
        o0 = m * 512
        xs = xs_bufs[m % 2]
        ffn(xs, hbf, W["ffn2_wg"], W["ffn2_wu"], W["ffn2_wd"], "g_ffn2", sq_t, rstd_t, actT, wgs, wus, wds, silt, scr2, False,
            bg=(pre2_gen(m + 1) if m < 3 else None),
            bgB=(ffn_norm_gen(xs_bufs[(m + 1) % 2], hbf, "g_ffn2", sq_t, rstd_t) if m < 3 else None),
            do_norm=(m == 0), nstep=2)
        C.dma(outT.rearrange("(k p) t -> p k t", p=128)[:, :, o0:o0 + 512], xs.ap, [xs.b], [Buf()], q="pool")
    S.emit()
    return nc


def ssm_stage(C, G, W, gt, sel, uD, ysD, dbg, eps_t, ones_bf, iota_d, maskE_d, rstd_from, kvec_d):
    def ld(nm, shp):
        t = C.alloc(shp, F32)
        C.dma(t.ap, G[nm], [], [t.b])
        return t

    iota = C.alloc([128, NJ], F32)
    C.dma(iota.ap, iota_d, [], [iota.b])
    maskE = C.alloc([128, 2], F32)
    C.dma(maskE.ap, maskE_d, [], [maskE.b])
    w_glu = C.alloc([128, 4, 512], BF16)
    C.dma(w_glu.ap, W["w_glu"].rearrange("(k p) f -> p k f", p=128), [], [w_glu.b], q="pool")
    L1r = C.alloc([128, 4, L, 128], BF16)
    L1i = C.alloc([128, 4, L, 128], BF16)
    L3r = C.alloc([128, 16, L, 32], BF16)
    L3i = C.alloc([128, 16, L, 32], BF16)
    FIR = C.alloc([128, 4, L, 128], BF16)
    thr = C.alloc([128, 16], F32)
    RL = C.alloc([128, 16], F32)
    maskO = C.alloc([128, 2], F32)
    kvec = C.alloc([128, L + 1], F32)
    C.dma(kvec.ap, kvec_d, [], [kvec.b])
    hpi = C.alloc([128, 1], F32)
    C.memset(hpi.ap, math.pi / 2, [hpi.b])
    sp_T = C.sp

    K1 = L + 1

    def bc_mid(ap2, nb, n):
        return ap2.rearrange("p (o n) -> p o n", o=1).to_broadcast([128, nb, n])

    def bc_last(ap2, nb, n):
        return ap2.rearrange("p (k o) -> p k o", o=1).to_broadcast([128, nb, n])

    def powers_b(lr, li, ldt, n, Pr, Pi, lrdt, lidt, KB):
        dt = C.alloc([128, n], F32)
        C.act(dt.ap, ldt.ap, AF.Exp, [ldt.b], [dt.b])
        C.tt(lrdt.ap, lr.ap, dt.ap, ALU.mult, [lr.b, dt.b], [lrdt.b])
        C.tt(lidt.ap, li.ap, dt.ap, ALU.mult, [li.b, dt.b], [lidt.b])
        a3 = C.alloc([128, KB, n], F32)
        e3 = C.alloc([128, KB, n], F32)
        kf = C.alloc([128, KB, n], F32)
        ki = C.alloc([128, KB, n], I32)
        sn = C.alloc([128, KB, n], F32)
        cs = C.alloc([128, KB, n], F32)
        for k0 in range(0, K1, KB):
            nb = min(KB, K1 - k0)
            kv = bc_last(kvec.ap[:, k0:k0 + nb], nb, n)
            A3, E3, KF, KI, SN, CS = (t.ap[:, 0:nb, :] for t in (a3, e3, kf, ki, sn, cs))
            C.tt(A3, bc_mid(lidt.ap, nb, n), kv, ALU.mult, [lidt.b, kvec.b], [a3.b])
            C.tt(E3, bc_mid(lrdt.ap, nb, n), kv, ALU.mult, [lrdt.b, kvec.b], [e3.b])
            C.act(E3, E3, AF.Exp, [e3.b], [e3.b])
            for (dst, dt_, shift) in ((SN, sn, 0.0), (CS, cs, math.pi / 2)):
                C.ts(KF, A3, shift, 1.0 / TWO_PI, ALU.add, ALU.mult, [a3.b], [kf.b])
                C.cp(KI, KF, [kf.b], [ki.b])
                C.cp(KF, KI, [ki.b], [kf.b])
                C.stt(KF, KF, -TWO_PI, A3, ALU.mult, ALU.add, [kf.b, a3.b], [kf.b])
                C.ts(KF, KF, math.pi - shift, -math.pi - shift, ALU.min, ALU.max, [kf.b], [kf.b])
                if shift:
                    C.act(dst, KF, AF.Sin, [kf.b, hpi.b], [dt_.b], bias=hpi.ap)
                else:
                    C.act(dst, KF, AF.Sin, [kf.b], [dt_.b])
            C.tt(Pr.ap[:, k0:k0 + nb, :], E3, CS, ALU.mult, [e3.b, cs.b], [Pr.b])
            C.tt(Pi.ap[:, k0:k0 + nb, :], E3, SN, ALU.mult, [e3.b, sn.b], [Pi.b])

    def bbar(lr, li, Pr, Pi, n):
        nr = C.alloc([128, n], F32)
        den = C.alloc([128, n], F32)
        tA = C.alloc([128, n], F32)
        cr = C.alloc([128, n], F32)
        ci = C.alloc([128, n], F32)
        C.ts(nr.ap, Pr.ap[:, 1, :], -1.0, None, ALU.add, None, [Pr.b], [nr.b])
        C.tt(den.ap, lr.ap, lr.ap, ALU.mult, [lr.b], [den.b])
        C.tt(tA.ap, li.ap, li.ap, ALU.mult, [li.b], [tA.b])
        C.tt(den.ap, den.ap, tA.ap, ALU.add, [den.b, tA.b], [den.b])
        C.recip(den.ap, den.ap, [den.b], [den.b])
        C.tt(cr.ap, nr.ap, lr.ap, ALU.mult, [nr.b, lr.b], [cr.b])
        C.tt(tA.ap, Pi.ap[:, 1, :], li.ap, ALU.mult, [Pi.b, li.b], [tA.b])
        C.tt(cr.ap, cr.ap, tA.ap, ALU.add, [cr.b, tA.b], [cr.b])
        C.tt(cr.ap, cr.ap, den.ap, ALU.mult, [cr.b, den.b], [cr.b])
        C.tt(ci.ap, Pi.ap[:, 1, :], lr.ap, ALU.mult, [Pi.b, lr.b], [ci.b])
        C.tt(tA.ap, nr.ap, li.ap, ALU.mult, [nr.b, li.b], [tA.b])
        C.tt(ci.ap, ci.ap, tA.ap, ALU.subtract, [ci.b, tA.b], [ci.b])
        C.tt(ci.ap, ci.ap, den.ap, ALU.mult, [ci.b, den.b], [ci.b])
        return cr, ci

    def cmul(out_r, out_i, ar, ai, br, bi, tmp, reads, wr, wi):
        C.tt(out_r, ar, br, ALU.mult, reads, [wr])
        C.tt(tmp.ap, ai, bi, ALU.mult, reads, [tmp.b])
        C.tt(out_r, out_r, tmp.ap, ALU.subtract, [wr, tmp.b], [wr])
        C.tt(out_i, ar, bi, ALU.mult, reads, [wi])
        C.tt(tmp.ap, ai, br, ALU.mult, reads, [tmp.b])
        C.tt(out_i, out_i, tmp.ap, ALU.add, [wi, tmp.b], [wi])

    n_in = 256
    lr_i = ld("are_in", [128, 4, 64]); li_i = ld("aim_in", [128, 4, 64]); ldt_i = ld("ldt_in", [128, 4, 64])
    br_i = ld("bre_in", [128, 4, 64]); bi_i = ld("bim_in", [128, 4, 64])
    f2 = lambda t: Tl(t.ap.rearrange("p a b -> p (a b)"), t.b)
    lr_i2, li_i2, ldt_i2, br_i2, bi_i2 = f2(lr_i), f2(li_i), f2(ldt_i), f2(br_i), f2(bi_i)
    Pr_i = C.alloc([128, K1, n_in], F32)
    Pi_i = C.alloc([128, K1, n_in], F32)
    lrdt_i = C.alloc([128, n_in], F32)
    lidt_i = C.alloc([128, n_in], F32)
    powers_b(lr_i2, li_i2, ldt_i2, n_in, Pr_i, Pi_i, lrdt_i, lidt_i, 4)
    cr, ci = bbar(lr_i2, li_i2, Pr_i, Pi_i, n_in)
    bbr = C.alloc([128, n_in], F32)
    bbi = C.alloc([128, n_in], F32)
    tA = C.alloc([128, n_in], F32)
    cmul(bbr.ap, bbi.ap, cr.ap, ci.ap, br_i2.ap, bi_i2.ap, tA, [cr.b, ci.b, br_i2.b, bi_i2.b], bbr.b, bbi.b)
    HB = 4
    w1r = C.alloc([128, HB, n_in], F32)
    w1i = C.alloc([128, HB, n_in], F32)
    w1t = C.alloc([128, HB, n_in], F32)
    for p0 in range(0, L, HB):
        cmul(w1r.ap, w1i.ap, Pr_i.ap[:, p0:p0 + HB, :], Pi_i.ap[:, p0:p0 + HB, :], bc_mid(bbr.ap, HB, n_in), bc_mid(bbi.ap, HB, n_in),
             w1t, [Pr_i.b, Pi_i.b, bbr.b, bbi.b], w1r.b, w1i.b)
        w1r4 = w1r.ap.rearrange("p k (c q) -> p c k q", c=4)
        w1i4 = w1i.ap.rearrange("p k (c q) -> p c k q", c=4)
        for e in range(2):
            C.ts(L1r.ap[:, :, p0:p0 + HB, e * 64:(e + 1) * 64], w1r4, maskE.ap[:, e:e + 1], None, ALU.mult, None, [w1r.b, maskE.b], [L1r.b])
            C.ts(L1i.ap[:, :, p0:p0 + HB, e * 64:(e + 1) * 64], w1i4, maskE.ap[:, e:e + 1], None, ALU.mult, None, [w1i.b, maskE.b], [L1i.b])

    C.barrier()
    C.sp = sp_T
    lr_o = ld("are_out", [128, 16]); li_o = ld("aim_out", [128, 16]); ldt_o = ld("ldt_out", [128, 16])
    cr_o = ld("cre_out", [128, 16, 16]); ci_o = ld("cim_out", [128, 16, 16])
    br_o = ld("bre_out", [128, 16, 16]); bi_o = ld("bim_out", [128, 16, 16])
    Pr_o = C.alloc([128, K1, 16], F32)
    Pi_o = C.alloc([128, K1, 16], F32)
    lrdt_o = C.alloc([128, 16], F32)
    lidt_o = C.alloc([128, 16], F32)
    powers_b(lr_o, li_o, ldt_o, 16, Pr_o, Pi_o, lrdt_o, lidt_o, K1)
    cro, cio = bbar(lr_o, li_o, Pr_o, Pi_o, 16)
    C.memset(maskO.ap, 0.0, [maskO.b])
    C.memset(maskO.ap[0:64, 0:1], 1.0, [maskO.b])
    C.memset(maskO.ap[64:128, 1:2], 1.0, [maskO.b])
    bc = lambda ap2: ap2.rearrange("p (q o) -> p q o", o=1).to_broadcast([128, 16, 16])
    bbro = C.alloc([128, 16, 16], F32)
    bbio = C.alloc([128, 16, 16], F32)
    tB0 = C.alloc([128, 16, 16], F32)
    cmul(bbro.ap, bbio.ap, bc(cro.ap), bc(cio.ap), br_o.ap, bi_o.ap, tB0, [cro.b, cio.b, br_o.b, bi_o.b], bbro.b, bbio.b)
    Wsr = C.alloc([128, 16, L, 32], BF16)
    Wsi = C.alloc([128, 16, L, 32], BF16)
    Cer = C.alloc([128, 16, 32], BF16)
    Cei = C.alloc([128, 16, 32], BF16)
    for e in range(2):
        C.ts(Cer.ap[:, :, e * 16:(e + 1) * 16], cr_o.ap, maskO.ap[:, e:e + 1], None, ALU.mult, None, [cr_o.b, maskO.b], [Cer.b])
        C.ts(Cei.ap[:, :, e * 16:(e + 1) * 16], ci_o.ap, maskO.ap[:, e:e + 1], -1.0, ALU.mult, ALU.mult, [ci_o.b, maskO.b], [Cei.b])
    KH = 8
    tr = C.alloc([128, KH, 16, 16], F32)
    ti = C.alloc([128, KH, 16, 16], F32)
    tt_ = C.alloc([128, KH, 16, 16], F32)

    def bq(ap3):
        return ap3.rearrange("p k (q o) -> p k q o", o=1).to_broadcast([128, KH, 16, 16])

    def bk(ap3):
        return ap3.rearrange("p (o q) h -> p o q h", o=1).to_broadcast([128, KH, 16, 16])

    for k0 in range(0, L, KH):
        cmul(tr.ap, ti.ap, bk(cr_o.ap), bk(ci_o.ap), bq(Pr_o.ap[:, k0 + 1:k0 + 1 + KH, :]), bq(Pi_o.ap[:, k0 + 1:k0 + 1 + KH, :]),
             tt_, [cr_o.b, ci_o.b, Pr_o.b, Pi_o.b], tr.b, ti.b)
        for e in range(2):
            o3r = L3r.ap[:, :, k0:k0 + KH, e * 16:(e + 1) * 16].rearrange("p q k h -> p k q h")
            o3i = L3i.ap[:, :, k0:k0 + KH, e * 16:(e + 1) * 16].rearrange("p q k h -> p k q h")
            C.ts(o3r, tr.ap, maskO.ap[:, e:e + 1], None, ALU.mult, None, [tr.b, maskO.b], [L3r.b])
            C.ts(o3i, ti.ap, maskO.ap[:, e:e + 1], -1.0, ALU.mult, ALU.mult, [ti.b, maskO.b], [L3i.b])
        cmul(tr.ap, ti.ap, bk(bbro.ap), bk(bbio.ap), bq(Pr_o.ap[:, k0:k0 + KH, :]), bq(Pi_o.ap[:, k0:k0 + KH, :]),
             tt_, [bbro.b, bbio.b, Pr_o.b, Pi_o.b], tr.b, ti.b)
        for e in range(2):
            o3r = Wsr.ap[:, :, k0:k0 + KH, e * 16:(e + 1) * 16].rearrange("p q k h -> p k q h")
            o3i = Wsi.ap[:, :, k0:k0 + KH, e * 16:(e + 1) * 16].rearrange("p q k h -> p k q h")
            C.ts(o3r, tr.ap, maskO.ap[:, e:e + 1], None, ALU.mult, None, [tr.b, maskO.b], [Wsr.b])
            C.ts(o3i, ti.ap, maskO.ap[:, e:e + 1], None, ALU.mult, None, [ti.b, maskO.b], [Wsi.b])

    C.memset(FIR.ap, 0.0, [FIR.b])
    for q in range(16):
        c, q4 = q // 4, q % 4
        p = C.ps()
        for tau in range(L):
            o = p.ap[32 * q4:32 * q4 + 32, tau * 32:(tau + 1) * 32]
            C.mm(o, Wsr.ap[:, q, tau, :], Cer.ap[:, q, :], True, False, [Wsr.b, Cer.b], [p.b], tile_position=(0, 32 * q4))
            C.mm(o, Wsi.ap[:, q, tau, :], Cei.ap[:, q, :], False, True, [Wsi.b, Cei.b], [p.b], tile_position=(0, 32 * q4))
        C.acopy(FIR.ap[32 * q4:32 * q4 + 32, c, :, 32 * q4:32 * q4 + 32],
                p.ap[32 * q4:32 * q4 + 32, :].rearrange("p (t h) -> p t h", h=32), [p.b], [FIR.b])
    tki = C.alloc([128, 16], I32)
    tkf = C.alloc([128, 16], F32)
    C.ts(thr.ap, lidt_o.ap, float(L), None, ALU.mult, None, [lidt_o.b], [thr.b])
    C.ts(tkf.ap, thr.ap, 1.0 / TWO_PI, None, ALU.mult, None, [thr.b], [tkf.b])
    C.cp(tki.ap, tkf.ap, [tkf.b], [tki.b])
    C.cp(tkf.ap, tki.ap, [tki.b], [tkf.b])
    C.stt(thr.ap, tkf.ap, -TWO_PI, thr.ap, ALU.mult, ALU.add, [tkf.b, thr.b], [thr.b])
    C.act(RL.ap, lrdt_o.ap, AF.Exp, [lrdt_o.b], [RL.b], scale=float(L))

    C.barrier()
    C.sp = sp_T
    yown = C.alloc([128, 4, NOWN], F32)
    sp_U = C.sp
    NJ1 = NJ + 2
    un = C.alloc([128, T], BF16)
    ud2 = [C.alloc([128, L, NJ], BF16) for _ in range(2)]
    tab2 = [(C.alloc([128, 4, NJ], F32), C.alloc([128, 4, NJ], F32)) for _ in range(2)]
    TB = 2
    ang = C.alloc([128, TB, NJ], F32)
    akf = C.alloc([128, TB, NJ], F32)
    aki = C.alloc([128, TB, NJ], I32)
    iota3 = iota.ap.rearrange("p (o j) -> p o j", o=1).to_broadcast([128, TB, NJ])
    zr = C.alloc([128, NJ], F32)
    zi = C.alloc([128, NJ], F32)
    za = C.alloc([128, NJ], F32)
    zb = C.alloc([128, NJ], F32)
    Zr = C.alloc([128, NJ], F32)
    Zi = C.alloc([128, NJ], F32)
    X2 = [(C.alloc([128, 4, NJ1], BF16, nb=4), C.alloc([128, 4, NJ1], BF16, nb=4)) for _ in range(2)]
    yall = C.alloc([128, T], F32)
    yall_kj = yall.ap.rearrange("p (j k) -> p k j", k=L)

    def load_c(c):
        ud = ud2[c % 2]
        tabC, tabS = tab2[c % 2]
        C.dma(un.ap, uD.ap[:, c, :], [uD.b], [un.b])
        C.acopy(ud.ap, un.ap.rearrange("p (j k) -> p k j", k=L), [un.b], [ud.b])
        for cb in range(4 // TB):
            q0 = 4 * c + TB * cb
            thr3 = thr.ap[:, q0:q0 + TB].rearrange("p (q o) -> p q o", o=1).to_broadcast([128, TB, NJ])
            for (dst, shift) in ((tabS, 0.0), (tabC, math.pi / 2)):
                C.tt(ang.ap, iota3, thr3, ALU.mult, [iota.b, thr.b], [ang.b])
                if shift:
                    C.ts(ang.ap, ang.ap, shift, None, ALU.add, None, [ang.b], [ang.b])
                C.ts(akf.ap, ang.ap, 1.0 / TWO_PI, None, ALU.mult, None, [ang.b], [akf.b])
                C.cp(aki.ap, akf.ap, [akf.b], [aki.b])
                C.cp(akf.ap, aki.ap, [aki.b], [akf.b])
                C.stt(akf.ap, akf.ap, -TWO_PI, ang.ap, ALU.mult, ALU.add, [akf.b, ang.b], [akf.b])
                C.clamp_pi(akf.ap, akf.b)
                C.act(dst.ap[:, TB * cb:TB * cb + TB, :], akf.ap, AF.Sin, [akf.b], [dst.b])

    def l12(c, q4):
        ud = ud2[c % 2]
        tabC, tabS = tab2[c % 2]
        Xr, Xi = X2[c % 2]
        q = 4 * c + q4
        pr = C.ps()
        pi = C.ps()
        for k in range(L):
            C.mm(pr.ap[:, 0:NJ], L1r.ap[32 * q4:32 * q4 + 32, c, L - 1 - k, :], ud.ap[32 * q4:32 * q4 + 32, k, :], k == 0, k == L - 1,
                 [L1r.b, ud.b], [pr.b], tile_position=(32 * q4, 0))
        for k in range(L):
            C.mm(pi.ap[:, 0:NJ], L1i.ap[32 * q4:32 * q4 + 32, c, L - 1 - k, :], ud.ap[32 * q4:32 * q4 + 32, k, :], k == 0, k == L - 1,
                 [L1i.b, ud.b], [pi.b], tile_position=(32 * q4, 0))
        cosJ = Tl(tabC.ap[:, q4, :], tabC.b)
        sinJ = Tl(tabS.ap[:, q4, :], tabS.b)
        C.tt(zr.ap, pr.ap[:, 0:NJ], cosJ.ap, ALU.mult, [pr.b, cosJ.b], [zr.b])
        C.tt(za.ap, pi.ap[:, 0:NJ], sinJ.ap, ALU.mult, [pi.b, sinJ.b], [za.b])
        C.tt(zr.ap, zr.ap, za.ap, ALU.add, [zr.b, za.b], [zr.b])
        C.tt(zi.ap, pi.ap[:, 0:NJ], cosJ.ap, ALU.mult, [pi.b, cosJ.b], [zi.b])
        C.tt(zb.ap, pr.ap[:, 0:NJ], sinJ.ap, ALU.mult, [pr.b, sinJ.b], [zb.b])
        C.tt(zi.ap, zi.ap, zb.ap, ALU.subtract, [zi.b, zb.b], [zi.b])
        Rb = RL.ap[:, q:q + 1].to_broadcast([128, NJ])
        C.S.add("dve", lambda e, Rb=Rb: e.tensor_tensor_scan(out=Zr.ap, data0=Rb, data1=zr.ap, initial=0.0, op0=ALU.mult, op1=ALU.add),
                reads=[RL.b, zr.b], writes=[Zr.b])
        C.S.add("dve", lambda e, Rb=Rb: e.tensor_tensor_scan(out=Zi.ap, data0=Rb, data1=zi.ap, initial=0.0, op0=ALU.mult, op1=ALU.add),
                reads=[RL.b, zi.b], writes=[Zi.b])
        C.memset(Xr.ap[:, q4, 0:2], 0.0, [Xr.b[q4]])
        C.memset(Xi.ap[:, q4, 0:2], 0.0, [Xi.b[q4]])
        C.tt(za.ap, Zr.ap, cosJ.ap, ALU.mult, [Zr.b, cosJ.b], [za.b])
        C.tt(zb.ap, Zi.ap, sinJ.ap, ALU.mult, [Zi.b, sinJ.b], [zb.b])
        C.tt(Xr.ap[:, q4, 1:NJ + 1], za.ap, zb.ap, ALU.subtract, [za.b, zb.b], [Xr.b[q4]])
        C.tt(za.ap, Zi.ap, cosJ.ap, ALU.mult, [Zi.b, cosJ.b], [za.b])
        C.tt(zb.ap, Zr.ap, sinJ.ap, ALU.mult, [Zr.b, sinJ.b], [zb.b])
        C.tt(Xi.ap[:, q4, 1:NJ + 1], za.ap, zb.ap, ALU.add, [za.b, zb.b], [Xi.b[q4]])

    def l3(c, k):
        ud = ud2[c % 2]
        Xr, Xi = X2[c % 2]
        p = C.ps()
        for q4 in range(4):
            q = 4 * c + q4
            o = p.ap[32 * q4:32 * q4 + 32, 0:NJ]
            tp = (0, 32 * q4)
            C.mm(o, L3r.ap[:, q, k, :], Xr.ap[:, q4, 0:NJ], True, False, [L3r.b, Xr.b[q4]], [p.b], tile_position=tp)
            C.mm(o, L3i.ap[:, q, k, :], Xi.ap[:, q4, 0:NJ], False, False, [L3i.b, Xi.b[q4]], [p.b], tile_position=tp)
            for kp in range(k + 1):
                C.mm(o, FIR.ap[:, c, k - kp, 32 * q4:32 * q4 + 32], ud.ap[:, kp, :], False, kp == k, [FIR.b, ud.b], [p.b], tile_position=tp)
        C.stt(yall_kj[:, k, :], ud.ap[:, k, :], gt["d_fm"].ap[:, c:c + 1], p.ap[:, 0:NJ], ALU.mult, ALU.add,
              [ud.b, gt["d_fm"].b, p.b], [yall.b])

    load_c(0)
    for q4 in range(4):
        l12(0, q4)
    for c in range(4):
        if c + 1 < 4:
            load_c(c + 1)
        for i in range(4):
            if c + 1 < 4:
                l12(c + 1, i)
            for k in range(4 * i, 4 * i + 4):
                l3(c, k)
        ya4 = yall.ap.rearrange("p (ch b t) -> p ch b t", b=4, t=128)
        yo4 = yown.ap[:, c, :].rearrange("p (ch b t) -> p ch b t", b=2, t=128)
        for (ob, ba, bb) in ((0, 0, 1), (1, 3, 2)):
            C.ts(yo4[:, :, ob, :], ya4[:, :, ba, :], sel.ap[:, 0:1], None, ALU.mult, None, [yall.b, sel.b], [yown.b])
            C.stt(yo4[:, :, ob, :], ya4[:, :, bb, :], sel.ap[:, 1:2], yo4[:, :, ob, :], ALU.mult, ALU.add, [yall.b, sel.b, yown.b], [yown.b])
    C.barrier()
    C.sp = sp_U
    gg = C.alloc([128, 4, 512], F32)
    gb = C.alloc([128, 4, 512], BF16)
    ta = C.alloc([128, 512], F32)
    tb = C.alloc([128, 512], F32)
    sq_t = C.alloc([128, 8, 512], BF16)
    r2 = C.alloc([128, 512], F32)
    yn = C.alloc([128, 4, 512], BF16)
    CG = 2.0 * math.sqrt(2.0 / math.pi)
    for m in range(4):
        o0 = m * 512
        for k in range(4):
            y = yown.ap[:, k, o0:o0 + 512]
            C.tt(ta.ap, y, y, ALU.mult, [yown.b], [ta.b])
            C.ts(ta.ap, ta.ap, 0.044715, 1.0, ALU.mult, ALU.add, [ta.b], [ta.b])
            C.tt(ta.ap, ta.ap, y, ALU.mult, [ta.b, yown.b], [ta.b])
            C.act(tb.ap, ta.ap, AF.Sigmoid, [ta.b], [tb.b], scale=CG)
            C.tt(gg.ap[:, k, :], tb.ap, y, ALU.mult, [tb.b, yown.b], [gg.b])
            C.cp(gb.ap[:, k, :], gg.ap[:, k, :], [gg.b], [gb.b])
        for d in range(4):
            p = C.ps()
            for k in range(4):
                C.mm(p.ap, w_glu.ap[:, k, d * 128:(d + 1) * 128], gb.ap[:, k, :], k == 0, k == 3, [w_glu.b, gb.b], [p.b])
            C.act(tb.ap, p.ap, AF.Sigmoid, [p.b, gt["b_glu"].b], [tb.b], bias=gt["b_glu"].ap[:, d:d + 1], scale=1.0)
            C.tt(gg.ap[:, d, :], gg.ap[:, d, :], tb.ap, ALU.mult, [gg.b, tb.b], [gg.b])
        if dbg:
            C.dma(dbg["d_yssm"].ap.rearrange("(k p) t -> p k t", p=128)[:, :, o0:o0 + 512], gg.ap, [gg.b], [dbg["d_yssm"].b])
        rstd_from([(gg.ap[:, k, :], 128, gg.b) for k in range(4)], 512, 512, sq_t, r2)
        for k in range(4):
            C.stt(yn.ap[:, k, :], gg.ap[:, k, :], gt["g_os"].ap[:, k:k + 1], r2.ap, ALU.mult, ALU.mult,
                  [gg.b, gt["g_os"].b, r2.b], [yn.b])
        C.dma(ysD.ap[:, :, o0:o0 + 512], yn.ap, [yn.b], [ysD.b])


def own_blocks(j):
    out = []
    for c in range(8):
        out += [4 * c + (0 if j == 0 else 1), 4 * c + (3 if j == 0 else 2)]
    return out


def make_in_maps(inp):
    f32 = np.float32
    A = lambda a: np.ascontiguousarray(a)
    fm = lambda g: A(np.asarray(g, f32).reshape(-1, 128).T)
    col = lambda g: A(np.asarray(g, f32).reshape(-1, 1))
    sw = np.concatenate([np.arange(32, 64), np.arange(0, 32)])
    w_in = np.asarray(inp["w_in"][0], f32)
    w_in2 = A(np.concatenate([w_in[:, 0:640], w_in[:, 640:704], w_in[:, 640:704][:, sw], w_in[:, 704:1216]], axis=1))
    w_uq = np.asarray(inp["mla_w_uq"][0], f32).reshape(384, 4, 192)
    w_uq2 = A(np.concatenate([w_uq[:, :, 0:128], w_uq[:, :, 128:192], w_uq[:, :, 128:192][:, :, sw]], axis=2).reshape(384, 1024))
    w_ukv = np.asarray(inp["mla_w_ukv"][0], f32).reshape(256, 4, 256)
    w_ukv2 = A(np.concatenate([w_ukv[:, :, 0:128].reshape(256, 512), w_ukv[:, :, 128:256].reshape(256, 512)], axis=1))
    gq = np.asarray(inp["mla_qk_norm_q"][0], f32)
    gk = np.asarray(inp["mla_qk_norm_k"][0], f32)
    a_re = np.asarray(inp["ssm_a_re"][0], f32); a_im = np.asarray(inp["ssm_a_im"][0], f32)
    ldt = np.asarray(inp["ssm_log_dt"][0], f32)
    b_re = np.asarray(inp["ssm_b_re"][0], f32); b_im = np.asarray(inp["ssm_b_im"][0], f32)
    c_re = np.asarray(inp["ssm_c_re"][0], f32); c_im = np.asarray(inp["ssm_c_im"][0], f32)

    def in_side_gp(a):
        v = a.reshape(4, 4, 2, 64)
        v = np.transpose(v, (1, 2, 0, 3))
        return A(np.broadcast_to(v[:, :, None], (4, 2, 16, 4, 64)).reshape(128, 4, 64))

    def in_side_b(b):
        v = b.reshape(4, 4, 2, 64, 16)
        return A(np.transpose(v, (1, 2, 4, 0, 3)).reshape(128, 4, 64))

    def out_side_gp(a):
        v = a.reshape(16, 2, 64)
        return A(np.transpose(v, (1, 2, 0)).reshape(128, 16))

    def out_side_c(cc):
        v = cc.reshape(16, 2, 16, 64)
        return A(np.transpose(v, (1, 3, 0, 2)).reshape(128, 16, 16))

    def out_side_b(b):
        v = b.reshape(16, 2, 64, 16)
        return A(np.transpose(v, (1, 2, 0, 3)).reshape(128, 16, 16))

    ldt_gp = np.broadcast_to(ldt[:, None], (32, 64))
    common = {
        "ffn1_wg": A(inp["ffn1_w_gate"][0]), "ffn1_wu": A(inp["ffn1_w_up"][0]), "ffn1_wd": A(inp["ffn1_w_down"][0]),
        "ffn2_wg": A(inp["ffn2_w_gate"][0]), "ffn2_wu": A(inp["ffn2_w_up"][0]), "ffn2_wd": A(inp["ffn2_w_down"][0]),
        "w_in": w_in2, "w_uq": w_uq2, "w_ukv": w_ukv2,
        "w_glu": A(inp["ssm_w_glu"][0]), "w_o": A(inp["w_o"][0]), "w_xq": A(inp["xattn_w_q"][0]),
        "w_xkv": A(inp["xattn_w_kv"][0]), "w_xo": A(inp["xattn_w_o"][0]),
        "g_ffn1": fm(inp["ffn1_norm"][0]), "g_mix": fm(inp["mix_norm"][0]), "g_q": fm(inp["mla_q_norm"][0]),
        "g_kv": fm(inp["mla_kv_norm"][0]),
        "g_qn": col(gq[0:128]), "g_qr": col(gq[128:192]), "g_qrs": col(gq[128:192][sw]),
        "g_kn": col(gk[0:128]), "g_kr": col(gk[128:192]), "g_krs": col(gk[128:192][sw]),
        "g_om": fm(inp["out_norm_mla"][0]), "g_os": fm(inp["out_norm_ssm"][0]), "g_xa": fm(inp["xattn_norm"][0]),
        "g_mem": fm(inp["mem_norm"][0]), "g_xq": col(inp["xattn_q_norm"][0]), "g_xk": col(inp["xattn_k_norm"][0]),
        "g_ffn2": fm(inp["ffn2_norm"][0]), "b_glu": fm(inp["ssm_b_glu"][0]), "d_fm": fm(np.asarray(inp["ssm_d"][0], f32).reshape(-1)),
        "are_in": in_side_gp(a_re), "aim_in": in_side_gp(a_im), "ldt_in": in_side_gp(ldt_gp),
        "bre_in": in_side_b(b_re), "bim_in": in_side_b(b_im),
        "are_out": out_side_gp(a_re), "aim_out": out_side_gp(a_im), "ldt_out": out_side_gp(ldt_gp),
        "cre_out": out_side_c(c_re), "cim_out": out_side_c(c_im),
        "bre_out": out_side_b(b_re), "bim_out": out_side_b(b_im),
    }
    d = np.arange(64)
    invf = (10000.0 ** (-(d % 32).astype(np.float64) / 32.0)).astype(f32).reshape(64, 1)
    invf = (np.float32(10000.0) ** (-(np.arange(32, dtype=f32)) / np.float32(32))).astype(f32)
    invf = A(np.concatenate([invf, invf]).reshape(64, 1))
    sgn = A(np.where(d < 32, -1.0, 1.0).astype(f32).reshape(64, 1))
    iota = A(np.broadcast_to(np.arange(1, NJ + 1, dtype=f32)[None, :], (128, NJ)))
    r = np.arange(128)
    ee = (r // 16) % 2
    maskE = A(np.stack([(ee == 0), (ee == 1)], axis=1).astype(f32))
    kvec = A(np.broadcast_to(np.arange(0, L + 1, dtype=f32)[None, :], (128, L + 1)))
    common.update({"invf": invf, "sgn": sgn, "iota": iota, "maskE": maskE, "kvec": kvec})
    x = np.asarray(inp["x"], f32)
    mem = np.asarray(inp["mem"], f32)
    pos = np.asarray(inp["positions"]).astype(np.int32)
    maps = []
    for core in range(8):
        b, j = core // 2, core % 2
        ob = own_blocks(j)
        own_tok = np.concatenate([np.arange(g * 128, (g + 1) * 128) for g in ob])
        qg = np.array(ob[0:4])
        qpos = (qg[:, None] * 128 + np.arange(128)[None, :]).reshape(-1)
        am = np.zeros((128, 8, 512), f32)
        for a in range(8):
            kpos = a * 128 + np.arange(128)
            am[:, a, :] = (kpos[:, None] <= qpos[None, :]).astype(f32)
        m = dict(common)
        m.update({
            "xT": A(x[b].T), "memT": A(mem[b].T),
            "pos_all": A(np.broadcast_to(pos[b][None, :], (64, T))),
            "pos_own": A(np.broadcast_to(pos[b][own_tok][None, :], (64, NOWN))),
            "sel": A(np.broadcast_to(np.array([1.0, 0.0] if j == 0 else [0.0, 1.0], f32)[None, :], (128, 2))),
            "amask": am,
        })
        maps.append(m)
    return maps


_NC_CACHE = {}


def kernel(**inputs):
    if "nc" not in _NC_CACHE:
        _NC_CACHE["nc"] = build(False)
    nc = _NC_CACHE["nc"]
    maps = make_in_maps(inputs)
    res = run_bass_kernel_spmd(nc, maps, core_ids=list(range(8)))
    out = np.zeros((4, T, D), np.float32)
    for core in range(8):
        b, j = core // 2, core % 2
        ob = own_blocks(j)
        o = np.asarray(res.results[core]["outT"], np.float32)
        for i, g in enumerate(ob):
            out[b, g * 128:(g + 1) * 128, :] = o[:, i * 128:(i + 1) * 128].T
    return out
```

```python
import math
import contextlib
import numpy as np
import ml_dtypes
import concourse.bass as bass
import concourse.mybir as mybir
from concourse.bass_utils import run_bass_kernel_spmd

F32 = mybir.dt.float32
BF16 = mybir.dt.bfloat16
I32 = mybir.dt.int32
U8 = mybir.dt.uint8
ALU = mybir.AluOpType
AF = mybir.ActivationFunctionType

D = 1024
T = 4096
NOWN = 2048
DFF = 2752
NF = 22
EPS = 1e-6
L = 16
NJ = T // L
TWO_PI = 2.0 * math.pi


class Buf:
    __slots__ = ("w", "r")

    def __init__(self, w=None):
        self.w = w
        self.r = []


class Op:
    __slots__ = ("idx", "eng", "fn", "deps", "dma", "slot", "ticket", "inc", "waits")

    def __init__(self, idx, eng, fn, dma):
        self.idx = idx
        self.eng = eng
        self.fn = fn
        self.deps = {}
        self.dma = dma
        self.slot = None
        self.ticket = None
        self.inc = False
        self.waits = []


class Sched:
    NSLOT = 12

    def __init__(self, nc):
        self.nc = nc
        self.ops = []
        self.slot_ctr = {}
        self.bar = None
        self.touched = set()

    def add(self, eng, fn, reads=(), writes=(), dma=False):
        op = Op(len(self.ops), eng, fn, dma)
        for b in reads:
            if b.w is not None:
                op.deps[b.w] = "raw"
        for b in writes:
            if b.w is not None and b.w not in op.deps:
                op.deps[b.w] = "waw"
            for r in b.r:
                if r not in op.deps:
                    op.deps[r] = "war"
        for b in reads:
            b.r.append(op.idx)
            self.touched.add(b)
        for b in writes:
            b.w = op.idx
            b.r = []
            self.touched.add(b)
        if dma:
            c = self.slot_ctr.get(eng, 0)
            op.slot = (eng, c % self.NSLOT)
            self.slot_ctr[eng] = c + 1
        self.ops.append(op)
        return op

    def emit(self):
        nc = self.nc
        ops = self.ops
        for op in ops:
            best = {}
            for p, kind in op.deps.items():
                P = ops[p]
                if P.dma:
                    key = ("dma",) + P.slot
                else:
                    if P.eng == op.eng and not op.dma:
                        if P.eng == "pe":
                            continue
                    key = ("eng", P.eng)
                if key not in best or best[key] < p:
                    best[key] = p
            op.waits = sorted(best.items(), key=lambda kv: kv[1])
            for _, p in op.waits:
                ops[p].inc = True
        cnt = {}
        for op in ops:
            if op.dma:
                key = ("dma",) + op.slot
                cnt[key] = cnt.get(key, 0) + 1
                op.ticket = 16 * cnt[key]
            elif op.inc:
                key = ("eng", op.eng)
                cnt[key] = cnt.get(key, 0) + 1
                op.ticket = cnt[key]
        keys = sorted(set(cnt.keys()), key=str)
        sems = {}
        with contextlib.ExitStack() as es:
            for k in keys:
                sems[k] = es.enter_context(nc.semaphore("s_" + "_".join(str(x) for x in k)))
            block = es.enter_context(nc.Block())
            by_eng = {}
            for op in ops:
                by_eng.setdefault(op.eng, []).append(op)
            engmap = {"pe": "tensor", "act": "scalar", "dve": "vector", "pool": "gpsimd", "sp": "sync"}

            def make(elist):
                def body(e):
                    seen = {}
                    for op in elist:
                        for key, p in op.waits:
                            v = ops[p].ticket
                            if seen.get(key, 0) < v:
                                e.wait_ge(sems[key], v)
                                seen[key] = v
                        if op.dma:
                            key = ("dma",) + op.slot
                            prev = op.ticket - 16
                            if prev > 0 and seen.get(key, 0) < prev:
                                e.wait_ge(sems[key], prev)
                                seen[key] = prev
                            op.fn(e).then_inc(sems[key], 16)
                        else:
                            ins = op.fn(e)
                            if op.inc:
                                ins.then_inc(sems[("eng", op.eng)], 1)
                    for op in elist:
                        if op.dma:
                            key = ("dma",) + op.slot
                            if seen.get(key, 0) < op.ticket:
                                e.wait_ge(sems[key], op.ticket)
                                seen[key] = op.ticket
                return body

            for engname, elist in by_eng.items():
                getattr(block, engmap[engname])(make(elist))


class Tl:
    def __init__(self, ap, b):
        self.ap = ap
        self.b = b

    def __getitem__(self, k):
        return self.ap[k]


def _dtsize(dt):
    return {F32: 4, BF16: 2, I32: 4, U8: 1}[dt]


class Ctx:
    ARENA = 212480

    def __init__(self, nc):
        self.nc = nc
        self.S = Sched(nc)
        self.arena = nc.alloc_sbuf_tensor("arena", [128, self.ARENA], U8)
        self.sp = 0
        self.barrier_idx = None
        self.psum = [Tl(nc.alloc_psum_tensor("pb%d" % i, [128, 512], F32)[:, :], Buf()) for i in range(8)]
        self.pi = 0
        self.reserved = set()

    def alloc(self, shape, dt, nb=1):
        n = 1
        for s in shape[1:]:
            n *= s
        nbytes = (n * _dtsize(dt) + 63) // 64 * 64
        assert self.sp + nbytes <= self.ARENA, ("SBUF overflow", self.sp, nbytes)
        ap = self.arena[:, self.sp:self.sp + n * _dtsize(dt)].bitcast(dt)
        self.sp += nbytes
        if len(shape) == 3:
            ap = ap.rearrange("p (a b) -> p a b", b=shape[2])
        elif len(shape) == 4:
            ap = ap.rearrange("p (a b c) -> p a b c", b=shape[2], c=shape[3])
        if shape[0] < 128:
            ap = ap[0:shape[0]]
        if nb == 1:
            return Tl(ap, Buf(self.barrier_idx))
        return Tl(ap, [Buf(self.barrier_idx) for _ in range(nb)])

    def ps(self):
        while (self.pi % 8) in self.reserved:
            self.pi += 1
        t = self.psum[self.pi % 8]
        self.pi += 1
        return t

    def barrier(self):
        S = self.S
        bufs = list(S.touched)
        dummy = self._dummy
        op = S.add("dve", lambda e: e.memset(dummy.ap, 0.0), reads=bufs, writes=bufs + [dummy.b])
        self.barrier_idx = op.idx
        S.touched = set()
        for t in self.psum:
            t.b.w = op.idx
            t.b.r = []
        return op.idx

    def mm(self, out, lhsT, rhs, start, stop, reads, writes, **kw):
        self.S.add("pe", lambda e: e.matmul(out, lhsT=lhsT, rhs=rhs, start=start, stop=stop, **kw),
                   reads=reads, writes=writes)

    def act(self, out, in_, func, reads, writes, bias=None, scale=None):
        kw = {}
        if bias is not None:
            kw["bias"] = bias
        if scale is not None:
            kw["scale"] = scale
        self.S.add("act", lambda e: e.activation(out=out, in_=in_, func=func, **kw), reads=reads, writes=writes)

    def ts(self, out, in0, s1, s2, op0, op1, reads, writes, eng="dve"):
        if op1 is None:
            self.S.add(eng, lambda e: e.tensor_scalar(out=out, in0=in0, scalar1=s1, scalar2=None, op0=op0),
                       reads=reads, writes=writes)
        else:
            self.S.add(eng, lambda e: e.tensor_scalar(out=out, in0=in0, scalar1=s1, scalar2=s2, op0=op0, op1=op1),
                       reads=reads, writes=writes)

    def stt(self, out, in0, scalar, in1, op0, op1, reads, writes, eng="dve"):
        self.S.add(eng, lambda e: e.scalar_tensor_tensor(out=out, in0=in0, scalar=scalar, in1=in1, op0=op0, op1=op1),
                   reads=reads, writes=writes)

    def tt(self, out, in0, in1, op, reads, writes, eng="dve"):
        self.S.add(eng, lambda e: e.tensor_tensor(out=out, in0=in0, in1=in1, op=op), reads=reads, writes=writes)

    def cp(self, out, in_, reads, writes, eng="dve"):
        self.S.add(eng, lambda e: e.tensor_copy(out=out, in_=in_), reads=reads, writes=writes)

    def memset(self, out, val, writes, eng="dve"):
        self.S.add(eng, lambda e: e.memset(out, val), writes=writes)

    def acopy(self, out, in_, reads, writes):
        self.S.add("act", lambda e: e.copy(out=out, in_=in_), reads=reads, writes=writes)

    def clamp_pi(self, ap, b):
        self.ts(ap, ap, math.pi, -math.pi, ALU.min, ALU.max, [b], [b])

    def recip(self, out, in_, reads, writes):
        self.S.add("dve", lambda e: e.reciprocal(out=out, in_=in_), reads=reads, writes=writes)

    def dma(self, out, in_, reads, writes, q="sp"):
        self.S.add(q, lambda e: e.dma_start(out=out, in_=in_), reads=reads, writes=writes, dma=True)


def _bl(x):
    return x if isinstance(x, (list, tuple)) else [x]


def build(debug=False, stop=None):
    nc = bass.Bass("TRN2", target_bir_lowering=False)
    C = Ctx(nc)
    S = C.S

    def din(name, shape, dt=F32):
        return nc.dram_tensor(name, list(shape), dt, kind="ExternalInput").ap()

    def dscr(name, shape, dt):
        return Tl(nc.dram_tensor(name, list(shape), dt, kind="Internal").ap(), Buf())

    xT = din("xT", [D, T])
    memT = din("memT", [D, 256])
    pos_all = din("pos_all", [64, T], I32)
    pos_own = din("pos_own", [64, NOWN], I32)
    sel_d = din("sel", [128, 2])
    amask_d = din("amask", [128, 8, 512])
    invf_d = din("invf", [64, 1])
    sgn_d = din("sgn", [64, 1])
    iota_d = din("iota", [128, NJ])
    maskE_d = din("maskE", [128, 2])
    kvec_d = din("kvec", [128, L + 1])
    W = {}
    for nm, shp in [("ffn1_wg", [D, DFF]), ("ffn1_wu", [D, DFF]), ("ffn1_wd", [DFF, D]),
                    ("ffn2_wg", [D, DFF]), ("ffn2_wu", [D, DFF]), ("ffn2_wd", [DFF, D]),
                    ("w_in", [D, 1280]), ("w_uq", [384, 1024]), ("w_ukv", [256, 1024]),
                    ("w_glu", [512, 512]), ("w_o", [D, D]), ("w_xq", [D, 512]), ("w_xkv", [D, 1024]),
                    ("w_xo", [512, D])]:
        W[nm] = din(nm, shp)
    G = {}
    for nm, shp in [("g_ffn1", [128, 8]), ("g_mix", [128, 8]), ("g_q", [128, 3]), ("g_kv", [128, 2]),
                    ("g_qn", [128, 1]), ("g_qr", [64, 1]), ("g_qrs", [64, 1]),
                    ("g_kn", [128, 1]), ("g_kr", [64, 1]), ("g_krs", [64, 1]),
                    ("g_om", [128, 4]), ("g_os", [128, 4]), ("g_xa", [128, 8]), ("g_mem", [128, 8]),
                    ("g_xq", [128, 1]), ("g_xk", [128, 1]), ("g_ffn2", [128, 8]),
                    ("b_glu", [128, 4]), ("d_fm", [128, 4]),
                    ("are_in", [128, 4, 64]), ("aim_in", [128, 4, 64]), ("ldt_in", [128, 4, 64]),
                    ("bre_in", [128, 4, 64]), ("bim_in", [128, 4, 64]),
                    ("are_out", [128, 16]), ("aim_out", [128, 16]), ("ldt_out", [128, 16]),
                    ("cre_out", [128, 16, 16]), ("cim_out", [128, 16, 16]),
                    ("bre_out", [128, 16, 16]), ("bim_out", [128, 16, 16])]:
        G[nm] = din(nm, shp)
    outT = nc.dram_tensor("outT", [D, NOWN], F32, kind="ExternalOutput").ap()
    dbg = {}
    if debug:
        for nm, shp in [("d_x1own", [D, NOWN]), ("d_yssm", [512, NOWN]), ("d_ymla", [512, NOWN]), ("d_x2", [D, NOWN]),
                        ("d_x3", [D, NOWN])]:
            dbg[nm] = Tl(nc.dram_tensor(nm, shp, F32, kind="ExternalOutput").ap(), Buf())

    def wscr(tag):
        a = Tl(nc.dram_tensor("WgB" + tag, [128, 8, DFF], BF16, kind="Internal").ap(), [Buf() for _ in range(11)])
        b = Tl(nc.dram_tensor("WuB" + tag, [128, 8, DFF], BF16, kind="Internal").ap(), [Buf() for _ in range(11)])
        c = Tl(nc.dram_tensor("WdB" + tag, [128, 8, NF, 128], BF16, kind="Internal").ap(), [Buf() for _ in range(8)])
        ct = Tl(c.ap, [Buf() for _ in range(8)])
        return (a, b, c, ct)
    scr1 = wscr("1")
    scr2 = wscr("2")
    KnD = dscr("KnD", [128, 4, T], BF16)
    krrD = dscr("krrD", [64, T], BF16)
    VD = dscr("VD", [128, 32, 512], BF16)
    rstdkD = dscr("rstdkD", [128, 32, 4], F32)
    uD = dscr("uD", [128, 4, T], BF16)
    QnD = dscr("QnD", [128, 4, NOWN], BF16)
    QrD = dscr("QrD", [64, 4, NOWN], BF16)
    x1D = dscr("x1D", [128, 8, NOWN], F32)
    ysD = dscr("ysD", [128, 4, NOWN], BF16)

    C._dummy = C.alloc([128, 8], F32)
    ones_bf = C.alloc([128, 128], BF16)
    eps_t = C.alloc([128, 1], F32)
    sel = C.alloc([128, 2], F32)
    gt = {}
    for nm in ["g_ffn1", "g_mix", "g_q", "g_kv", "g_qn", "g_qr", "g_qrs", "g_kn", "g_kr", "g_krs", "g_om", "g_os",
               "g_xa", "g_mem", "g_xq", "g_xk", "g_ffn2", "b_glu", "d_fm"]:
        shp = [int(s) for s in G[nm].shape]
        gt[nm] = C.alloc(shp, F32)
        C.dma(gt[nm].ap, G[nm], [], [gt[nm].b])
    invf = C.alloc([64, 1], F32)
    sgn = C.alloc([64, 1], F32)
    C.dma(invf.ap, invf_d, [], [invf.b])
    C.dma(sgn.ap, sgn_d, [], [sgn.b])
    C.dma(sel.ap, sel_d, [], [sel.b])
    C.memset(ones_bf.ap, 1.0, [ones_bf.b])
    C.memset(eps_t.ap, EPS, [eps_t.b])
    base_sp = C.sp

    def rstd_from(srcs, N, Dn, sq_t, rstd_t):
        ps = C.ps()
        n = len(srcs)
        for i, (ap, K, bufs) in enumerate(srcs):
            C.act(sq_t.ap[0:K, i, 0:N], ap, AF.Square, _bl(bufs), [sq_t.b])
        for i, (ap, K, bufs) in enumerate(srcs):
            C.mm(ps.ap[:, 0:N], ones_bf.ap[0:K, :], sq_t.ap[0:K, i, 0:N], i == 0, i == n - 1, [ones_bf.b, sq_t.b], [ps.b])
        C.act(rstd_t.ap[:, 0:N], ps.ap[:, 0:N], AF.Sqrt, [ps.b, eps_t.b], [rstd_t.b], bias=eps_t.ap, scale=1.0 / Dn)
        C.recip(rstd_t.ap[:, 0:N], rstd_t.ap[:, 0:N], [rstd_t.b], [rstd_t.b])

    def rope_tables(pos_ap_dram, col0, N, cosT, sinS, tmpi, tmpf, tmpk):
        C.dma(tmpi.ap[:, 0:N], pos_ap_dram[:, col0:col0 + N], [], [tmpi.b])
        C.cp(tmpf.ap[:, 0:N], tmpi.ap[:, 0:N], [tmpi.b], [tmpf.b])
        C.ts(tmpf.ap[:, 0:N], tmpf.ap[:, 0:N], invf.ap, None, ALU.mult, None, [tmpf.b, invf.b], [tmpf.b])
        C.ts(tmpk.ap[:, 0:N], tmpf.ap[:, 0:N], 1.0 / TWO_PI, None, ALU.mult, None, [tmpf.b], [tmpk.b])
        C.cp(tmpi.ap[:, 0:N], tmpk.ap[:, 0:N], [tmpk.b], [tmpi.b])
        C.cp(tmpk.ap[:, 0:N], tmpi.ap[:, 0:N], [tmpi.b], [tmpk.b])
        C.stt(tmpk.ap[:, 0:N], tmpk.ap[:, 0:N], -TWO_PI, tmpf.ap[:, 0:N], ALU.mult, ALU.add, [tmpk.b, tmpf.b], [tmpk.b])
        C.clamp_pi(tmpk.ap[:, 0:N], tmpk.b)
        C.act(sinS.ap[:, 0:N], tmpk.ap[:, 0:N], AF.Sin, [tmpk.b], [sinS.b])
        C.ts(sinS.ap[:, 0:N], sinS.ap[:, 0:N], sgn.ap, None, ALU.mult, None, [sinS.b, sgn.b], [sinS.b])
        C.ts(tmpf.ap[:, 0:N], tmpf.ap[:, 0:N], math.pi / 2, None, ALU.add, None, [tmpf.b], [tmpf.b])
        C.ts(tmpk.ap[:, 0:N], tmpf.ap[:, 0:N], 1.0 / TWO_PI, None, ALU.mult, None, [tmpf.b], [tmpk.b])
        C.cp(tmpi.ap[:, 0:N], tmpk.ap[:, 0:N], [tmpk.b], [tmpi.b])
        C.cp(tmpk.ap[:, 0:N], tmpi.ap[:, 0:N], [tmpi.b], [tmpk.b])
        C.stt(tmpk.ap[:, 0:N], tmpk.ap[:, 0:N], -TWO_PI, tmpf.ap[:, 0:N], ALU.mult, ALU.add, [tmpk.b, tmpf.b], [tmpk.b])
        C.clamp_pi(tmpk.ap[:, 0:N], tmpk.b)
        C.act(cosT.ap[:, 0:N], tmpk.ap[:, 0:N], AF.Sin, [tmpk.b], [cosT.b])

    def ffn_norm_gen(xs, hbf, gname, sq_t, rstd_t):
        N = 512
        rstd_from([(xs.ap[:, k, :], 128, xs.b) for k in range(8)], N, D, sq_t, rstd_t)
        yield
        for k in range(8):
            C.stt(hbf.ap[:, k, :], xs.ap[:, k, :], gt[gname].ap[:, k:k + 1], rstd_t.ap[:, 0:N], ALU.mult, ALU.mult,
                  [xs.b, gt[gname].b, rstd_t.b], [hbf.b])
            if k % 2 == 1:
                yield

    def ffn(xs, hbf, wg_d, wu_d, wd_d, gname, sq_t, rstd_t, actT, wgs, wus, wds, silt, scr, first, bg=None, bgB=None, do_norm=True, nstep=2, bgB_first=False):
        N = 512
        if do_norm:
            for _ in ffn_norm_gen(xs, hbf, gname, sq_t, rstd_t):
                pass
        wgv = wg_d.rearrange("(k p) f -> p k f", p=128)
        wuv = wu_d.rearrange("(k p) f -> p k f", p=128)
        NG = 11
        WgB, WuB, WdB, WdBt = scr

        def load_wd(d):
            slot = d % 3
            if first:
                C.dma(wds.ap[:, slot, 0:21, :], wd_d[0:21 * 128, d * 128:(d + 1) * 128].rearrange("(f p) d -> p f d", p=128),
                      [], [wds.b[slot]], q="pool")
                C.dma(wds.ap[0:64, slot, 21, :], wd_d[21 * 128:DFF, d * 128:(d + 1) * 128], [], [wds.b[slot + 3]], q="pool")
                C.dma(WdB.ap[:, d, 0:21, :], wds.ap[:, slot, 0:21, :], [wds.b[slot]], [WdB.b[d]])
                C.dma(WdB.ap[0:64, d, 21, :], wds.ap[0:64, slot, 21, :], [wds.b[slot + 3]], [WdBt.b[d]])
            else:
                C.dma(wds.ap[:, slot, 0:21, :], WdB.ap[:, d, 0:21, :], [WdB.b[d]], [wds.b[slot]], q="sp")
                C.dma(wds.ap[0:64, slot, 21, :], WdB.ap[0:64, d, 21, :], [WdBt.b[d]], [wds.b[slot + 3]], q="sp")

        for g in range(NG):
            if g in (5, 7, 9):
                load_wd((g - 5) // 2)
            c0 = g * 256
            cw = min(256, DFF - c0)
            slot = g % 3
            if first:
                C.dma(wgs.ap[:, slot, :, 0:cw], wgv[:, :, c0:c0 + cw], [], [wgs.b[slot]], q="pool")
                C.dma(wus.ap[:, slot, :, 0:cw], wuv[:, :, c0:c0 + cw], [], [wus.b[slot]], q="pool")
                C.dma(WgB.ap[:, :, c0:c0 + cw], wgs.ap[:, slot, :, 0:cw], [wgs.b[slot]], [WgB.b[g]])
                C.dma(WuB.ap[:, :, c0:c0 + cw], wus.ap[:, slot, :, 0:cw], [wus.b[slot]], [WuB.b[g]])
            else:
                C.dma(wgs.ap[:, slot, :, 0:cw], WgB.ap[:, :, c0:c0 + cw], [WgB.b[g]], [wgs.b[slot]], q="sp")
                C.dma(wus.ap[:, slot, :, 0:cw], WuB.ap[:, :, c0:c0 + cw], [WuB.b[g]], [wus.b[slot]], q="sp")
            for ff in range(2):
                f = 2 * g + ff
                if f >= NF:
                    break
                fs = min(128, DFF - f * 128)
                pg = C.ps()
                pu = C.ps()
                for k in range(8):
                    C.mm(pg.ap[0:fs, :], wgs.ap[:, slot, k, ff * 128:ff * 128 + fs], hbf.ap[:, k, :], k == 0, k == 7,
                         [wgs.b[slot], hbf.b], [pg.b])
                for k in range(8):
                    C.mm(pu.ap[0:fs, :], wus.ap[:, slot, k, ff * 128:ff * 128 + fs], hbf.ap[:, k, :], k == 0, k == 7,
                         [wus.b[slot], hbf.b], [pu.b])
                C.act(silt.ap[0:fs, f % 2, :], pg.ap[0:fs, :], AF.Silu, [pg.b], [silt.b[f % 2]])
                C.tt(actT.ap[0:fs, f, :], pu.ap[0:fs, :], silt.ap[0:fs, f % 2, :], ALU.mult, [pu.b, silt.b[f % 2]], [actT.b[f]])
                if bg is not None:
                    for _ in range(nstep):
                        next(bg, None)
        for d in range(8):
            slot = d % 3
            if d + 3 < 8:
                pass
            po = C.ps()
            for f in range(NF):
                fs = min(128, DFF - f * 128)
                C.mm(po.ap, wds.ap[0:fs, slot, f, :], actT.ap[0:fs, f, :], f == 0, f == NF - 1,
                     [wds.b[slot + (3 if f == NF - 1 else 0)], actT.b[f]], [po.b])
            C.stt(xs.ap[:, d, :], po.ap, 0.5, xs.ap[:, d, :], ALU.mult, ALU.add, [po.b, xs.b], [xs.b])
            if d + 3 < 8:
                load_wd(d + 3)
            if bgB is not None and bgB_first:
                next(bgB, None)
            bg_done = True
            if bg is not None:
                for _ in range(2 * nstep):
                    if next(bg, "END") == "END":
                        bg = None
                        break
                bg_done = bg is None
            if bgB is not None and bg_done and not bgB_first:
                next(bgB, None)
        if bg is not None:
            for _ in bg:
                pass
        if bgB is not None:
            for _ in bgB:
                pass

    def select_own(out3, src_fn, reads, writes):
        C.ts(out3[:, :, 0:128], src_fn(0), sel.ap[:, 0:1], None, ALU.mult, None, reads + [sel.b], writes)
        C.stt(out3[:, :, 0:128], src_fn(1), sel.ap[:, 1:2], out3[:, :, 0:128], ALU.mult, ALU.add, reads + [sel.b] + writes, writes)
        C.ts(out3[:, :, 128:256], src_fn(3), sel.ap[:, 0:1], None, ALU.mult, None, reads + [sel.b], writes)
        C.stt(out3[:, :, 128:256], src_fn(2), sel.ap[:, 1:2], out3[:, :, 128:256], ALU.mult, ALU.add, reads + [sel.b] + writes, writes)

    mK = C.alloc([128, 4, 256], BF16)
    mV = C.alloc([128, 2, 512], BF16)
    base_sp = C.sp
    C.barrier()
    if True:
        mx = C.alloc([128, 8, 256], F32)
        mh = C.alloc([128, 8, 256], BF16)
        sq_t = C.alloc([128, 8, 512], BF16)
        rstd_t = C.alloc([128, 512], F32)
        wkv = C.alloc([128, 8, 1024], BF16)
        C.dma(mx.ap, memT.rearrange("(k p) t -> p k t", p=128), [], [mx.b])
        C.dma(wkv.ap, W["w_xkv"].rearrange("(k p) f -> p k f", p=128), [], [wkv.b], q="pool")
        rstd_from([(mx.ap[:, k, :], 128, mx.b) for k in range(8)], 256, D, sq_t, rstd_t)
        for k in range(8):
            C.stt(mh.ap[:, k, :], mx.ap[:, k, :], gt["g_mem"].ap[:, k:k + 1], rstd_t.ap[:, 0:256], ALU.mult, ALU.mult,
                  [mx.b, gt["g_mem"].b, rstd_t.b], [mh.b])
        rk = C.alloc([128, 512], F32)
        for h in range(4):
            pk = C.ps()
            for k in range(8):
                C.mm(pk.ap[:, 0:256], wkv.ap[:, k, h * 128:(h + 1) * 128], mh.ap[:, k, :], k == 0, k == 7, [wkv.b, mh.b], [pk.b])
            rstd_from([(pk.ap[:, 0:256], 128, pk.b)], 256, 128, sq_t, rk)
            C.stt(mK.ap[:, h, :], pk.ap[:, 0:256], gt["g_xk"].ap[:, 0:1], rk.ap[:, 0:256], ALU.mult, ALU.mult,
                  [pk.b, gt["g_xk"].b, rk.b], [mK.b])
        for j in range(2):
            pv = C.ps()
            for k in range(8):
                C.mm(pv.ap, mh.ap[:, k, j * 128:(j + 1) * 128], wkv.ap[:, k, 512:1024], k == 0, k == 7, [wkv.b, mh.b], [pv.b])
            C.acopy(mV.ap[:, j, :], pv.ap, [pv.b], [mV.b])

    if stop == "M":
        S.emit()
        return nc
    C.barrier()
    C.sp = base_sp
    xs = C.alloc([128, 8, 512], F32)
    xs_b = C.alloc([128, 8, 512], F32)
    hbf = C.alloc([128, 8, 512], BF16)
    hbf2 = C.alloc([128, 8, 512], BF16)
    sq_t = C.alloc([128, 8, 512], BF16)
    rstd_t = C.alloc([128, 512], F32)
    actT = C.alloc([128, NF, 512], BF16, nb=NF)
    wgs = C.alloc([128, 3, 8, 256], BF16, nb=3)
    wus = C.alloc([128, 3, 8, 256], BF16, nb=3)
    wds = C.alloc([128, 3, NF, 128], BF16, nb=6)
    silt = C.alloc([128, 2, 512], F32, nb=2)
    w_in = C.alloc([128, 8, 1280], BF16)
    w_ukv = C.alloc([128, 2, 1024], BF16)
    x1own = C.alloc([128, 8, 128], F32)
    cqo = C.alloc([128, 3, 256], F32)
    cqn = C.alloc([128, 3, 256], BF16)
    w_uq = C.alloc([128, 3, 1024], BF16)
    C.dma(w_uq.ap, W["w_uq"].rearrange("(k p) f -> p k f", p=128), [], [w_uq.b], q="pool")
    Qn_c = C.alloc([128, 4, 256], BF16)
    Qr_c = C.alloc([64, 4, 256], BF16)
    cosO = C.alloc([64, 256], F32)
    sinO = C.alloc([64, 256], F32)
    ckvn = C.alloc([128, 2, 512], BF16)
    r2 = C.alloc([128, 512], F32)
    cosT = C.alloc([64, 512], F32)
    sinS = C.alloc([64, 512], F32)
    tmpi = C.alloc([64, 512], I32)
    tmpf = C.alloc([64, 512], F32)
    tmpk = C.alloc([64, 512], F32)
    t1 = C.alloc([64, 512], F32)
    t2 = C.alloc([64, 512], F32)
    krr = C.alloc([64, 512], BF16)
    krsq = C.alloc([64, 512], BF16)
    t3 = C.alloc([64, 512], F32)
    t4 = C.alloc([64, 512], F32)
    knb = None
    knsq = C.alloc([128, 4, 512], BF16)
    onehot = C.alloc([128, 4, 4], BF16)
    ub = C.alloc([128, 4, 512], BF16)
    vb = ub
    knb = ub
    rk = C.alloc([128, 4, 4], F32)
    w_in_v = W["w_in"].rearrange("(k p) f -> p k f", p=128)
    for cc in range(2):
        C.dma(w_in.ap[:, :, cc * 640:(cc + 1) * 640], w_in_v[:, :, cc * 640:(cc + 1) * 640], [], [w_in.b], q="pool")
    C.dma(w_ukv.ap, W["w_ukv"].rearrange("(k p) f -> p k f", p=128), [], [w_ukv.b], q="pool")
    onehot_f = C.alloc([128, 4, 4], F32)
    C.memset(onehot_f.ap, 0.0, [onehot_f.b])
    for h in range(4):
        C.memset(onehot_f.ap[:, h, h:h + 1], 1.0, [onehot_f.b])
    C.cp(onehot.ap, onehot_f.ap, [onehot_f.b], [onehot.b])
    xTv = xT.rearrange("(k p) t -> p k t", p=128)
    if stop == "A0":
        S.emit()
        return nc
    def post_gen(c, xs):
        t0 = c * 512
        for (ob, ba, bb) in ((0, 0, 1), (1, 3, 2)):
            C.ts(x1own.ap, xs.ap[:, :, ba * 128:(ba + 1) * 128], sel.ap[:, 0:1], None, ALU.mult, None, [xs.b, sel.b], [x1own.b])
            C.stt(x1own.ap, xs.ap[:, :, bb * 128:(bb + 1) * 128], sel.ap[:, 1:2], x1own.ap, ALU.mult, ALU.add, [xs.b, sel.b, x1own.b], [x1own.b])
            C.dma(x1D.ap[:, :, c * 256 + ob * 128:c * 256 + (ob + 1) * 128], x1own.ap, [x1own.b], [x1D.b], q="pool")
            yield
        rstd_from([(xs.ap[:, k, :], 128, xs.b) for k in range(8)], 512, D, sq_t, rstd_t)
        yield
        for k in range(8):
            C.stt(hbf2.ap[:, k, :], xs.ap[:, k, :], gt["g_mix"].ap[:, k:k + 1], rstd_t.ap, ALU.mult, ALU.mult,
                  [xs.b, gt["g_mix"].b, rstd_t.b], [hbf2.b])
            yield
        for i in range(3):
            p = C.ps()
            for k in range(8):
                C.mm(p.ap, w_in.ap[:, k, i * 128:(i + 1) * 128], hbf2.ap[:, k, :], k == 0, k == 7, [w_in.b, hbf2.b], [p.b])
            pv3 = p.ap.rearrange("p (a t) -> p a t", a=1)
            select_own(cqo.ap[:, i:i + 1, :], lambda blk: pv3[:, :, blk * 128:(blk + 1) * 128], [p.b], [cqo.b])
            yield
        rstd_from([(cqo.ap[:, i, :], 128, cqo.b) for i in range(3)], 256, 384, sq_t, r2)
        yield
        for i in range(3):
            C.stt(cqn.ap[:, i, :], cqo.ap[:, i, :], gt["g_q"].ap[:, i:i + 1], r2.ap[:, 0:256], ALU.mult, ALU.mult,
                  [cqo.b, gt["g_q"].b, r2.b], [cqn.b])
            yield
        rope_tables(pos_own, c * 256, 256, cosO, sinO, tmpi, tmpf, tmpk)
        yield
        for h in range(4):
            pn = C.ps()
            pr = C.ps()
            prs = C.ps()
            for k in range(3):
                C.mm(pn.ap[:, 0:256], w_uq.ap[:, k, h * 256:h * 256 + 128], cqn.ap[:, k, :], k == 0, k == 2, [w_uq.b, cqn.b], [pn.b])
            for k in range(3):
                C.mm(pr.ap[0:64, 0:256], w_uq.ap[:, k, h * 256 + 128:h * 256 + 192], cqn.ap[:, k, :], k == 0, k == 2, [w_uq.b, cqn.b], [pr.b])
            for k in range(3):
                C.mm(prs.ap[0:64, 0:256], w_uq.ap[:, k, h * 256 + 192:h * 256 + 256], cqn.ap[:, k, :], k == 0, k == 2, [w_uq.b, cqn.b], [prs.b])
            rstd_from([(pn.ap[:, 0:256], 128, pn.b), (pr.ap[0:64, 0:256], 64, pr.b)], 256, 192, sq_t, r2)
            C.stt(Qn_c.ap[:, h, :], pn.ap[:, 0:256], gt["g_qn"].ap[:, 0:1], r2.ap[:, 0:256], ALU.mult, ALU.mult, [pn.b, gt["g_qn"].b, r2.b], [Qn_c.b])
            C.stt(t1.ap[:, 0:256], pr.ap[0:64, 0:256], gt["g_qr"].ap, cosO.ap, ALU.mult, ALU.mult, [pr.b, gt["g_qr"].b, cosO.b, r2.b], [t1.b])
            C.stt(t2.ap[:, 0:256], prs.ap[0:64, 0:256], gt["g_qrs"].ap, sinO.ap, ALU.mult, ALU.mult, [prs.b, gt["g_qrs"].b, sinO.b, r2.b], [t2.b])
            C.tt(t1.ap[:, 0:256], t1.ap[:, 0:256], t2.ap[:, 0:256], ALU.add, [t1.b, t2.b], [t1.b])
            C.tt(Qr_c.ap[:, h, :], t1.ap[:, 0:256], r2.ap[0:64, 0:256], ALU.mult, [t1.b, r2.b], [Qr_c.b])
            yield
        C.dma(QnD.ap[:, :, c * 256:(c + 1) * 256], Qn_c.ap, [Qn_c.b], [QnD.b], q="pool")
        C.dma(QrD.ap[:, :, c * 256:(c + 1) * 256], Qr_c.ap, [Qr_c.b], [QrD.b], q="pool")
        yield
        pkv = [C.ps(), C.ps()]
        for i in range(2):
            for k in range(8):
                C.mm(pkv[i].ap, w_in.ap[:, k, 384 + i * 128:384 + (i + 1) * 128], hbf2.ap[:, k, :], k == 0, k == 7, [w_in.b, hbf2.b], [pkv[i].b])
        rstd_from([(pkv[i].ap, 128, pkv[i].b) for i in range(2)], 512, 256, sq_t, r2)
        for i in range(2):
            C.stt(ckvn.ap[:, i, :], pkv[i].ap, gt["g_kv"].ap[:, i:i + 1], r2.ap, ALU.mult, ALU.mult,
                  [pkv[i].b, gt["g_kv"].b, r2.b], [ckvn.b])
        yield
        rope_tables(pos_all, t0, 512, cosT, sinS, tmpi, tmpf, tmpk)
        yield
        pk1 = C.ps()
        pk2 = C.ps()
        for k in range(8):
            C.mm(pk1.ap[0:64, :], w_in.ap[:, k, 640:704], hbf2.ap[:, k, :], k == 0, k == 7, [w_in.b, hbf2.b], [pk1.b])
        for k in range(8):
            C.mm(pk2.ap[0:64, :], w_in.ap[:, k, 704:768], hbf2.ap[:, k, :], k == 0, k == 7, [w_in.b, hbf2.b], [pk2.b])
        C.acopy(t3.ap, pk1.ap[0:64, :], [pk1.b], [t3.b])
        C.acopy(t4.ap, pk2.ap[0:64, :], [pk2.b], [t4.b])
        yield
        C.act(krsq.ap, t3.ap, AF.Square, [t3.b], [krsq.b])
        yield
        C.stt(t1.ap, t3.ap, gt["g_kr"].ap, cosT.ap, ALU.mult, ALU.mult, [t3.b, gt["g_kr"].b, cosT.b], [t1.b])
        yield
        C.stt(t2.ap, t4.ap, gt["g_krs"].ap, sinS.ap, ALU.mult, ALU.mult, [t4.b, gt["g_krs"].b, sinS.b], [t2.b])
        yield
        C.tt(krr.ap, t1.ap, t2.ap, ALU.add, [t1.b, t2.b], [krr.b])
        yield
        C.dma(krrD.ap[:, t0:t0 + 512], krr.ap, [krr.b], [krrD.b], q="pool")
        yield
        for i in range(4):
            p = C.ps()
            for k in range(8):
                C.mm(p.ap, w_in.ap[:, k, 768 + i * 128:768 + (i + 1) * 128], hbf2.ap[:, k, :], k == 0, k == 7, [w_in.b, hbf2.b], [p.b])
            C.acopy(ub.ap[:, i, :], p.ap, [p.b], [ub.b])
            yield
        C.dma(uD.ap[:, :, t0:t0 + 512], ub.ap, [ub.b], [uD.b], q="pool")
        yield
        for h in range(4):
            p = C.ps()
            for k in range(2):
                C.mm(p.ap, w_ukv.ap[:, k, h * 128:(h + 1) * 128], ckvn.ap[:, k, :], k == 0, k == 1, [w_ukv.b, ckvn.b], [p.b])
            C.act(knsq.ap[:, h, :], p.ap, AF.Square, [p.b], [knsq.b])
            C.ts(knb.ap[:, h, :], p.ap, gt["g_kn"].ap[:, 0:1], None, ALU.mult, None, [p.b, gt["g_kn"].b, knsq.b], [knb.b])
            yield
        C.dma(KnD.ap[:, :, t0:t0 + 512], knb.ap, [knb.b], [KnD.b], q="pool")
        yield
        pq = C.ps()
        for blk in range(4):
            for h in range(4):
                C.mm(pq.ap[:, blk * 4:blk * 4 + 4], knsq.ap[:, h, blk * 128:(blk + 1) * 128], onehot.ap[:, h, :], h == 0, False,
                     [knsq.b, onehot.b], [pq.b])
            C.mm(pq.ap[:, blk * 4:blk * 4 + 4], krsq.ap[:, blk * 128:(blk + 1) * 128], ones_bf.ap[0:64, 0:4], False, True,
                 [krsq.b, ones_bf.b], [pq.b])
        rkf = rk.ap.rearrange("p a b -> p (a b)")
        C.act(rkf, pq.ap[:, 0:16], AF.Sqrt, [pq.b, eps_t.b], [rk.b], bias=eps_t.ap, scale=1.0 / 192.0)
        yield
        C.recip(rkf, rkf, [rk.b], [rk.b])
        yield
        C.ts(rkf, rkf, 192.0 ** -0.5, None, ALU.mult, None, [rk.b], [rk.b])
        yield
        C.dma(rstdkD.ap[:, c * 4:(c + 1) * 4, :], rk.ap, [rk.b], [rstdkD.b], q="pool")
        yield
        for blk in range(4):
            p = C.ps()
            for k in range(2):
                C.mm(p.ap, ckvn.ap[:, k, blk * 128:(blk + 1) * 128], w_ukv.ap[:, k, 512:1024], k == 0, k == 1, [w_ukv.b, ckvn.b], [p.b])
            C.acopy(vb.ap[:, blk, :], p.ap, [p.b], [vb.b])
            yield
        C.dma(VD.ap[:, c * 4:(c + 1) * 4, :], vb.ap, [vb.b], [VD.b], q="pool")
        yield
        yield

    xs_bufs = [xs, xs_b]

    def pre_gen(c):
        xb = xs_bufs[c % 2]
        C.dma(xb.ap, xTv[:, :, c * 512:(c + 1) * 512], [], [xb.b])
        yield
        for _ in ffn_norm_gen(xb, hbf, "g_ffn1", sq_t, rstd_t):
            yield

    POST_STEP = 2
    for _ in pre_gen(0):
        pass
    prev = None
    for c in range(8):
        xs = xs_bufs[c % 2]
        ffn(xs, hbf, W["ffn1_wg"], W["ffn1_wu"], W["ffn1_wd"], "g_ffn1", sq_t, rstd_t, actT, wgs, wus, wds, silt, scr1, c == 0,
            bg=prev, bgB=(pre_gen(c + 1) if c + 1 < 8 else None), do_norm=False, nstep=POST_STEP, bgB_first=True)
        prev = post_gen(c, xs)
    for _ in prev:
        pass


    if stop == "A":
        S.emit()
        return nc
    C.barrier()
    C.sp = base_sp
    WgB2, WuB2, WdB2, WdB2t = scr2
    for (src, dst) in ((W["ffn2_wg"], WgB2), (W["ffn2_wu"], WuB2)):
        sv = src.rearrange("(k p) f -> p k f", p=128)
        for cc in range(4):
            C.dma(dst.ap[:, :, cc * 688:(cc + 1) * 688], sv[:, :, cc * 688:(cc + 1) * 688], [], list(dst.b), q="pool")
    wd2 = W["ffn2_wd"]
    for d in range(8):
        C.dma(WdB2.ap[:, d, 0:21, :], wd2[0:21 * 128, d * 128:(d + 1) * 128].rearrange("(f p) c -> p f c", p=128),
              [], [WdB2.b[d]], q="pool")
    C.dma(WdB2.ap[0:64, :, 21, :], wd2[21 * 128:DFF, :].rearrange("p (d c) -> p d c", c=128), [], list(WdB2t.b), q="pool")
    ssm_stage(C, G, W, gt, sel, uD, ysD, dbg, eps_t, ones_bf, iota_d, maskE_d, rstd_from, kvec_d)

    if stop == "B":
        S.emit()
        return nc
    ymD = dscr("ymD", [128, 4, NOWN], BF16)
    x3D = dscr("x3D", [128, 8, NOWN], F32)
    C.barrier()
    C.sp = base_sp
    Kn = C.alloc([128, 4, T], BF16)
    krA = C.alloc([64, T], BF16)
    Vv = C.alloc([128, 32, 512], BF16)
    rkA = C.alloc([128, 32, 4], F32)
    amask = C.alloc([128, 8, 512], BF16)
    C.dma(Kn.ap, KnD.ap, [KnD.b], [Kn.b])
    C.dma(krA.ap, krrD.ap, [krrD.b], [krA.b])
    C.dma(Vv.ap, VD.ap, [VD.b], [Vv.b])
    C.dma(rkA.ap, rstdkD.ap, [rstdkD.b], [rkA.b])
    C.dma(amask.ap, amask_d, [], [amask.b], q="pool")
    sq_t = C.alloc([128, 8, 512], BF16)
    r2 = C.alloc([128, 512], F32)
    ymn = C.alloc([128, 4, 512], BF16)
    ymla = C.alloc([128, 4, 512], F32)
    cosT = C.alloc([64, 512], F32)
    sinS = C.alloc([64, 512], F32)
    tmpi = C.alloc([64, 512], I32)
    tmpf = C.alloc([64, 512], F32)
    tmpk = C.alloc([64, 512], F32)
    t1 = C.alloc([64, 512], F32)
    t2 = C.alloc([64, 512], F32)
    Qn2 = [C.alloc([128, 4, 512], BF16) for _ in range(2)]
    Qr2 = [C.alloc([64, 4, 512], BF16) for _ in range(2)]
    PT = C.alloc([128, 6, 512], BF16, nb=6)
    rden = C.alloc([128, 512], F32)
    pti = 0
    for m in range(4):
        o0 = m * 512
        Qn = Qn2[m % 2]
        Qr = Qr2[m % 2]
        if m == 0:
            C.dma(Qn.ap, QnD.ap[:, :, 0:512], [QnD.b], [Qn.b])
            C.dma(Qr.ap, QrD.ap[:, :, 0:512], [QrD.b], [Qr.b])
        if m + 1 < 4:
            C.dma(Qn2[(m + 1) % 2].ap, QnD.ap[:, :, o0 + 512:o0 + 1024], [QnD.b], [Qn2[(m + 1) % 2].b])
            C.dma(Qr2[(m + 1) % 2].ap, QrD.ap[:, :, o0 + 512:o0 + 1024], [QrD.b], [Qr2[(m + 1) % 2].b])
        nkb = 8 * m + 8
        C.reserved = {4, 5, 6, 7}
        for h in range(4):
            po = C.psum[4 + 2 * (h % 2)]
            pd = C.psum[5 + 2 * (h % 2)]
            def score(kb):
                pst = C.ps()
                C.mm(pst.ap, Kn.ap[:, h, kb * 128:(kb + 1) * 128], Qn.ap[:, h, :], True, False, [Kn.b, Qn.b], [pst.b])
                C.mm(pst.ap, krA.ap[:, kb * 128:(kb + 1) * 128], Qr.ap[:, h, :], False, True, [krA.b, Qr.b], [pst.b])
                return pst
            LOOK = 2
            pend = [score(kb) for kb in range(min(LOOK, nkb))]
            for kb in range(nkb):
                pst = pend.pop(0)
                if kb + LOOK < nkb:
                    pend.append(score(kb + LOOK))
                sl = pti % 6
                pti += 1
                C.act(PT.ap[:, sl, :], pst.ap, AF.Exp, [pst.b, rkA.b], [PT.b[sl]], scale=rkA.ap[:, kb, h:h + 1])
                if kb >= 8 * m:
                    C.tt(PT.ap[:, sl, :], PT.ap[:, sl, :], amask.ap[:, kb - 8 * m, :], ALU.mult, [PT.b[sl], amask.b], [PT.b[sl]])
                C.mm(po.ap, Vv.ap[:, kb, h * 128:(h + 1) * 128], PT.ap[:, sl, :], kb == 0, kb == nkb - 1, [Vv.b, PT.b[sl]], [po.b])
                C.mm(pd.ap, ones_bf.ap, PT.ap[:, sl, :], kb == 0, kb == nkb - 1, [ones_bf.b, PT.b[sl]], [pd.b])
            C.recip(rden.ap, pd.ap, [pd.b], [rden.b])
            C.tt(ymla.ap[:, h, :], po.ap, rden.ap, ALU.mult, [po.b, rden.b], [ymla.b])
        C.reserved = set()
        if debug:
            C.dma(dbg["d_ymla"].ap.rearrange("(k p) t -> p k t", p=128)[:, :, o0:o0 + 512], ymla.ap, [ymla.b], [dbg["d_ymla"].b])
        rstd_from([(ymla.ap[:, k, :], 128, ymla.b) for k in range(4)], 512, 512, sq_t, r2)
        for k in range(4):
            C.stt(ymn.ap[:, k, :], ymla.ap[:, k, :], gt["g_om"].ap[:, k:k + 1], r2.ap, ALU.mult, ALU.mult,
                  [ymla.b, gt["g_om"].b, r2.b], [ymn.b])
        C.dma(ymD.ap[:, :, o0:o0 + 512], ymn.ap, [ymn.b], [ymD.b])

    if stop == "C1":
        S.emit()
        return nc
    C.barrier()
    C.sp = base_sp
    w_o = C.alloc([128, 8, 1024], BF16)
    w_xq = C.alloc([128, 8, 512], BF16)
    w_xo = C.alloc([128, 4, 1024], BF16)
    C.dma(w_o.ap, W["w_o"].rearrange("(k p) f -> p k f", p=128), [], [w_o.b], q="pool")
    C.dma(w_xq.ap, W["w_xq"].rearrange("(k p) f -> p k f", p=128), [], [w_xq.b], q="pool")
    C.dma(w_xo.ap, W["w_xo"].rearrange("(k p) f -> p k f", p=128), [], [w_xo.b], q="pool")
    xs = C.alloc([128, 8, 512], F32)
    hbf = C.alloc([128, 8, 512], BF16)
    sq_t = C.alloc([128, 8, 512], BF16)
    rstd_t = C.alloc([128, 512], F32)
    r2 = C.alloc([128, 512], F32)
    ycat = C.alloc([128, 8, 512], BF16, nb=2)
    PT = C.alloc([128, 6, 512], BF16, nb=6)
    rden = C.alloc([128, 512], F32)
    qx = C.alloc([128, 4, 512], BF16)
    ox = C.alloc([128, 4, 512], BF16)
    actT = C.alloc([128, NF, 512], BF16, nb=NF)
    wgs = C.alloc([128, 3, 8, 256], BF16, nb=3)
    wus = C.alloc([128, 3, 8, 256], BF16, nb=3)
    wds = C.alloc([128, 3, NF, 128], BF16, nb=6)
    silt = C.alloc([128, 2, 512], F32, nb=2)
    xs_b = C.alloc([128, 8, 512], F32)
    hbfx = C.alloc([128, 8, 512], BF16)
    xs_bufs = [xs, xs_b]
    pti_box = [0]

    def pre2_gen(m):
        o0 = m * 512
        xs = xs_bufs[m % 2]
        C.dma(xs.ap, x1D.ap[:, :, o0:o0 + 512], [x1D.b], [xs.b])
        C.dma(ycat.ap[:, 0:4, :], ymD.ap[:, :, o0:o0 + 512], [ymD.b], [ycat.b[0]])
        C.dma(ycat.ap[:, 4:8, :], ysD.ap[:, :, o0:o0 + 512], [ysD.b], [ycat.b[1]])
        if debug:
            C.dma(dbg["d_x1own"].ap.rearrange("(k p) t -> p k t", p=128)[:, :, o0:o0 + 512], xs.ap, [xs.b], [dbg["d_x1own"].b])
        yield
        for d in range(8):
            p = C.ps()
            for k in range(8):
                C.mm(p.ap, w_o.ap[:, k, d * 128:(d + 1) * 128], ycat.ap[:, k, :], k == 0, k == 7, [w_o.b, ycat.b[k // 4]], [p.b])
            C.tt(xs.ap[:, d, :], p.ap, xs.ap[:, d, :], ALU.add, [p.b, xs.b], [xs.b])
            yield
        if debug:
            C.dma(dbg["d_x2"].ap.rearrange("(k p) t -> p k t", p=128)[:, :, o0:o0 + 512], xs.ap, [xs.b], [dbg["d_x2"].b])
        rstd_from([(xs.ap[:, k, :], 128, xs.b) for k in range(8)], 512, D, sq_t, rstd_t)
        yield
        for k in range(8):
            C.stt(hbfx.ap[:, k, :], xs.ap[:, k, :], gt["g_xa"].ap[:, k:k + 1], rstd_t.ap, ALU.mult, ALU.mult,
                  [xs.b, gt["g_xa"].b, rstd_t.b], [hbfx.b])
            if k % 2 == 1:
                yield
        for h in range(4):
            p = C.ps()
            for k in range(8):
                C.mm(p.ap, w_xq.ap[:, k, h * 128:(h + 1) * 128], hbfx.ap[:, k, :], k == 0, k == 7, [w_xq.b, hbfx.b], [p.b])
            rstd_from([(p.ap, 128, p.b)], 512, 128, sq_t, r2)
            C.stt(qx.ap[:, h, :], p.ap, gt["g_xq"].ap[:, 0:1], r2.ap, ALU.mult, ALU.mult, [p.b, gt["g_xq"].b, r2.b], [qx.b])
            yield
        for h in range(4):
            po = C.ps()
            pd = C.ps()
            for j in range(2):
                pst = C.ps()
                C.mm(pst.ap, mK.ap[:, h, j * 128:(j + 1) * 128], qx.ap[:, h, :], True, True, [mK.b, qx.b], [pst.b])
                sl = pti_box[0] % 6
                pti_box[0] += 1
                C.act(PT.ap[:, sl, :], pst.ap, AF.Exp, [pst.b], [PT.b[sl]], scale=128.0 ** -0.5)
                C.mm(po.ap, mV.ap[:, j, h * 128:(h + 1) * 128], PT.ap[:, sl, :], j == 0, j == 1, [mV.b, PT.b[sl]], [po.b])
                C.mm(pd.ap, ones_bf.ap, PT.ap[:, sl, :], j == 0, j == 1, [ones_bf.b, PT.b[sl]], [pd.b])
            C.recip(rden.ap, pd.ap, [pd.b], [rden.b])
            C.tt(ox.ap[:, h, :], po.ap, rden.ap, ALU.mult, [po.b, rden.b], [ox.b])
            yield
        for d in range(8):
            p = C.ps()
            for k in range(4):
                C.mm(p.ap, w_xo.ap[:, k, d * 128:(d + 1) * 128], ox.ap[:, k, :], k == 0, k == 3, [w_xo.b, ox.b], [p.b])
            C.tt(xs.ap[:, d, :], p.ap, xs.ap[:, d, :], ALU.add, [p.b, xs.b], [xs.b])
            yield
        if debug:
            C.dma(dbg["d_x3"].ap.rearrange("(k p) t -> p k t", p=128)[:, :, o0:o0 + 512], xs.ap, [xs.b], [dbg["d_x3"].b])
        yield

    for _ in pre2_gen(0):
        pass
    for m in range(4):
        o0 = m * 512
        xs = xs_bufs[m % 2]
        ffn(xs, hbf, W["ffn2_wg"], W["ffn2_wu"], W["ffn2_wd"], "g_ffn2", sq_t, rstd_t, actT, wgs, wus, wds, silt, scr2, False,
            bg=(pre2_gen(m + 1) if m < 3 else None),
            bgB=(ffn_norm_gen(xs_bufs[(m + 1) % 2], hbf, "g_ffn2", sq_t, rstd_t) if m < 3 else None),
            do_norm=(m == 0), nstep=2)
        C.dma(outT.rearrange("(k p) t -> p k t", p=128)[:, :, o0:o0 + 512], xs.ap, [xs.b], [Buf()], q="pool")
    S.emit()
    return nc


def ssm_stage(C, G, W, gt, sel, uD, ysD, dbg, eps_t, ones_bf, iota_d, maskE_d, rstd_from, kvec_d):
    def ld(nm, shp):
        t = C.alloc(shp, F32)
        C.dma(t.ap, G[nm], [], [t.b])
        return t

    iota = C.alloc([128, NJ], F32)
    C.dma(iota.ap, iota_d, [], [iota.b])
    maskE = C.alloc([128, 2], F32)
    C.dma(maskE.ap, maskE_d, [], [maskE.b])
    w_glu = C.alloc([128, 4, 512], BF16)
    C.dma(w_glu.ap, W["w_glu"].rearrange("(k p) f -> p k f", p=128), [], [w_glu.b], q="pool")
    L1r = C.alloc([128, 4, L, 128], BF16)
    L1i = C.alloc([128, 4, L, 128], BF16)
    L3r = C.alloc([128, 16, L, 32], BF16)
    L3i = C.alloc([128, 16, L, 32], BF16)
    FIR = C.alloc([128, 4, L, 128], BF16)
    thr = C.alloc([128, 16], F32)
    RL = C.alloc([128, 16], F32)
    maskO = C.alloc([128, 2], F32)
    kvec = C.alloc([128, L + 1], F32)
    C.dma(kvec.ap, kvec_d, [], [kvec.b])
    hpi = C.alloc([128, 1], F32)
    C.memset(hpi.ap, math.pi / 2, [hpi.b])
    sp_T = C.sp

    K1 = L + 1

    def bc_mid(ap2, nb, n):
        return ap2.rearrange("p (o n) -> p o n", o=1).to_broadcast([128, nb, n])

    def bc_last(ap2, nb, n):
        return ap2.rearrange("p (k o) -> p k o", o=1).to_broadcast([128, nb, n])

    def powers_b(lr, li, ldt, n, Pr, Pi, lrdt, lidt, KB):
        dt = C.alloc([128, n], F32)
        C.act(dt.ap, ldt.ap, AF.Exp, [ldt.b], [dt.b])
        C.tt(lrdt.ap, lr.ap, dt.ap, ALU.mult, [lr.b, dt.b], [lrdt.b])
        C.tt(lidt.ap, li.ap, dt.ap, ALU.mult, [li.b, dt.b], [lidt.b])
        a3 = C.alloc([128, KB, n], F32)
        e3 = C.alloc([128, KB, n], F32)
        kf = C.alloc([128, KB, n], F32)
        ki = C.alloc([128, KB, n], I32)
        sn = C.alloc([128, KB, n], F32)
        cs = C.alloc([128, KB, n], F32)
        for k0 in range(0, K1, KB):
            nb = min(KB, K1 - k0)
            kv = bc_last(kvec.ap[:, k0:k0 + nb], nb, n)
            A3, E3, KF, KI, SN, CS = (t.ap[:, 0:nb, :] for t in (a3, e3, kf, ki, sn, cs))
            C.tt(A3, bc_mid(lidt.ap, nb, n), kv, ALU.mult, [lidt.b, kvec.b], [a3.b])
            C.tt(E3, bc_mid(lrdt.ap, nb, n), kv, ALU.mult, [lrdt.b, kvec.b], [e3.b])
            C.act(E3, E3, AF.Exp, [e3.b], [e3.b])
            for (dst, dt_, shift) in ((SN, sn, 0.0), (CS, cs, math.pi / 2)):
                C.ts(KF, A3, shift, 1.0 / TWO_PI, ALU.add, ALU.mult, [a3.b], [kf.b])
                C.cp(KI, KF, [kf.b], [ki.b])
                C.cp(KF, KI, [ki.b], [kf.b])
                C.stt(KF, KF, -TWO_PI, A3, ALU.mult, ALU.add, [kf.b, a3.b], [kf.b])
                C.ts(KF, KF, math.pi - shift, -math.pi - shift, ALU.min, ALU.max, [kf.b], [kf.b])
                if shift:
                    C.act(dst, KF, AF.Sin, [kf.b, hpi.b], [dt_.b], bias=hpi.ap)
                else:
                    C.act(dst, KF, AF.Sin, [kf.b], [dt_.b])
            C.tt(Pr.ap[:, k0:k0 + nb, :], E3, CS, ALU.mult, [e3.b, cs.b], [Pr.b])
            C.tt(Pi.ap[:, k0:k0 + nb, :], E3, SN, ALU.mult, [e3.b, sn.b], [Pi.b])

    def bbar(lr, li, Pr, Pi, n):
        nr = C.alloc([128, n], F32)
        den = C.alloc([128, n], F32)
        tA = C.alloc([128, n], F32)
        cr = C.alloc([128, n], F32)
        ci = C.alloc([128, n], F32)
        C.ts(nr.ap, Pr.ap[:, 1, :], -1.0, None, ALU.add, None, [Pr.b], [nr.b])
        C.tt(den.ap, lr.ap, lr.ap, ALU.mult, [lr.b], [den.b])
        C.tt(tA.ap, li.ap, li.ap, ALU.mult, [li.b], [tA.b])
        C.tt(den.ap, den.ap, tA.ap, ALU.add, [den.b, tA.b], [den.b])
        C.recip(den.ap, den.ap, [den.b], [den.b])
        C.tt(cr.ap, nr.ap, lr.ap, ALU.mult, [nr.b, lr.b], [cr.b])
        C.tt(tA.ap, Pi.ap[:, 1, :], li.ap, ALU.mult, [Pi.b, li.b], [tA.b])
        C.tt(cr.ap, cr.ap, tA.ap, ALU.add, [cr.b, tA.b], [cr.b])
        C.tt(cr.ap, cr.ap, den.ap, ALU.mult, [cr.b, den.b], [cr.b])
        C.tt(ci.ap, Pi.ap[:, 1, :], lr.ap, ALU.mult, [Pi.b, lr.b], [ci.b])
        C.tt(tA.ap, nr.ap, li.ap, ALU.mult, [nr.b, li.b], [tA.b])
        C.tt(ci.ap, ci.ap, tA.ap, ALU.subtract, [ci.b, tA.b], [ci.b])
        C.tt(ci.ap, ci.ap, den.ap, ALU.mult, [ci.b, den.b], [ci.b])
        return cr, ci

    def cmul(out_r, out_i, ar, ai, br, bi, tmp, reads, wr, wi):
        C.tt(out_r, ar, br, ALU.mult, reads, [wr])
        C.tt(tmp.ap, ai, bi, ALU.mult, reads, [tmp.b])
        C.tt(out_r, out_r, tmp.ap, ALU.subtract, [wr, tmp.b], [wr])
        C.tt(out_i, ar, bi, ALU.mult, reads, [wi])
        C.tt(tmp.ap, ai, br, ALU.mult, reads, [tmp.b])
        C.tt(out_i, out_i, tmp.ap, ALU.add, [wi, tmp.b], [wi])

    n_in = 256
    lr_i = ld("are_in", [128, 4, 64]); li_i = ld("aim_in", [128, 4, 64]); ldt_i = ld("ldt_in", [128, 4, 64])
    br_i = ld("bre_in", [128, 4, 64]); bi_i = ld("bim_in", [128, 4, 64])
    f2 = lambda t: Tl(t.ap.rearrange("p a b -> p (a b)"), t.b)
    lr_i2, li_i2, ldt_i2, br_i2, bi_i2 = f2(lr_i), f2(li_i), f2(ldt_i), f2(br_i), f2(bi_i)
    Pr_i = C.alloc([128, K1, n_in], F32)
    Pi_i = C.alloc([128, K1, n_in], F32)
    lrdt_i = C.alloc([128, n_in], F32)
    lidt_i = C.alloc([128, n_in], F32)
    powers_b(lr_i2, li_i2, ldt_i2, n_in, Pr_i, Pi_i, lrdt_i, lidt_i, 4)
    cr, ci = bbar(lr_i2, li_i2, Pr_i, Pi_i, n_in)
    bbr = C.alloc([128, n_in], F32)
    bbi = C.alloc([128, n_in], F32)
    tA = C.alloc([128, n_in], F32)
    cmul(bbr.ap, bbi.ap, cr.ap, ci.ap, br_i2.ap, bi_i2.ap, tA, [cr.b, ci.b, br_i2.b, bi_i2.b], bbr.b, bbi.b)
    HB = 4
    w1r = C.alloc([128, HB, n_in], F32)
    w1i = C.alloc([128, HB, n_in], F32)
    w1t = C.alloc([128, HB, n_in], F32)
    for p0 in range(0, L, HB):
        cmul(w1r.ap, w1i.ap, Pr_i.ap[:, p0:p0 + HB, :], Pi_i.ap[:, p0:p0 + HB, :], bc_mid(bbr.ap, HB, n_in), bc_mid(bbi.ap, HB, n_in),
             w1t, [Pr_i.b, Pi_i.b, bbr.b, bbi.b], w1r.b, w1i.b)
        w1r4 = w1r.ap.rearrange("p k (c q) -> p c k q", c=4)
        w1i4 = w1i.ap.rearrange("p k (c q) -> p c k q", c=4)
        for e in range(2):
            C.ts(L1r.ap[:, :, p0:p0 + HB, e * 64:(e + 1) * 64], w1r4, maskE.ap[:, e:e + 1], None, ALU.mult, None, [w1r.b, maskE.b], [L1r.b])
            C.ts(L1i.ap[:, :, p0:p0 + HB, e * 64:(e + 1) * 64], w1i4, maskE.ap[:, e:e + 1], None, ALU.mult, None, [w1i.b, maskE.b], [L1i.b])

    C.barrier()
    C.sp = sp_T
    lr_o = ld("are_out", [128, 16]); li_o = ld("aim_out", [128, 16]); ldt_o = ld("ldt_out", [128, 16])
    cr_o = ld("cre_out", [128, 16, 16]); ci_o = ld("cim_out", [128, 16, 16])
    br_o = ld("bre_out", [128, 16, 16]); bi_o = ld("bim_out", [128, 16, 16])
    Pr_o = C.alloc([128, K1, 16], F32)
    Pi_o = C.alloc([128, K1, 16], F32)
    lrdt_o = C.alloc([128, 16], F32)
    lidt_o = C.alloc([128, 16], F32)
    powers_b(lr_o, li_o, ldt_o, 16, Pr_o, Pi_o, lrdt_o, lidt_o, K1)
    cro, cio = bbar(lr_o, li_o, Pr_o, Pi_o, 16)
    C.memset(maskO.ap, 0.0, [maskO.b])
    C.memset(maskO.ap[0:64, 0:1], 1.0, [maskO.b])
    C.memset(maskO.ap[64:128, 1:2], 1.0, [maskO.b])
    bc = lambda ap2: ap2.rearrange("p (q o) -> p q o", o=1).to_broadcast([128, 16, 16])
    bbro = C.alloc([128, 16, 16], F32)
    bbio = C.alloc([128, 16, 16], F32)
    tB0 = C.alloc([128, 16, 16], F32)
    cmul(bbro.ap, bbio.ap, bc(cro.ap), bc(cio.ap), br_o.ap, bi_o.ap, tB0, [cro.b, cio.b, br_o.b, bi_o.b], bbro.b, bbio.b)
    Wsr = C.alloc([128, 16, L, 32], BF16)
    Wsi = C.alloc([128, 16, L, 32], BF16)
    Cer = C.alloc([128, 16, 32], BF16)
    Cei = C.alloc([128, 16, 32], BF16)
    for e in range(2):
        C.ts(Cer.ap[:, :, e * 16:(e + 1) * 16], cr_o.ap, maskO.ap[:, e:e + 1], None, ALU.mult, None, [cr_o.b, maskO.b], [Cer.b])
        C.ts(Cei.ap[:, :, e * 16:(e + 1) * 16], ci_o.ap, maskO.ap[:, e:e + 1], -1.0, ALU.mult, ALU.mult, [ci_o.b, maskO.b], [Cei.b])
    KH = 8
    tr = C.alloc([128, KH, 16, 16], F32)
    ti = C.alloc([128, KH, 16, 16], F32)
    tt_ = C.alloc([128, KH, 16, 16], F32)

    def bq(ap3):
        return ap3.rearrange("p k (q o) -> p k q o", o=1).to_broadcast([128, KH, 16, 16])

    def bk(ap3):
        return ap3.rearrange("p (o q) h -> p o q h", o=1).to_broadcast([128, KH, 16, 16])

    for k0 in range(0, L, KH):
        cmul(tr.ap, ti.ap, bk(cr_o.ap), bk(ci_o.ap), bq(Pr_o.ap[:, k0 + 1:k0 + 1 + KH, :]), bq(Pi_o.ap[:, k0 + 1:k0 + 1 + KH, :]),
             tt_, [cr_o.b, ci_o.b, Pr_o.b, Pi_o.b], tr.b, ti.b)
        for e in range(2):
            o3r = L3r.ap[:, :, k0:k0 + KH, e * 16:(e + 1) * 16].rearrange("p q k h -> p k q h")
            o3i = L3i.ap[:, :, k0:k0 + KH, e * 16:(e + 1) * 16].rearrange("p q k h -> p k q h")
            C.ts(o3r, tr.ap, maskO.ap[:, e:e + 1], None, ALU.mult, None, [tr.b, maskO.b], [L3r.b])
            C.ts(o3i, ti.ap, maskO.ap[:, e:e + 1], -1.0, ALU.mult, ALU.mult, [ti.b, maskO.b], [L3i.b])
        cmul(tr.ap, ti.ap, bk(bbro.ap), bk(bbio.ap), bq(Pr_o.ap[:, k0:k0 + KH, :]), bq(Pi_o.ap[:, k0:k0 + KH, :]),
             tt_, [bbro.b, bbio.b, Pr_o.b, Pi_o.b], tr.b, ti.b)
        for e in range(2):
            o3r = Wsr.ap[:, :, k0:k0 + KH, e * 16:(e + 1) * 16].rearrange("p q k h -> p k q h")
            o3i = Wsi.ap[:, :, k0:k0 + KH, e * 16:(e + 1) * 16].rearrange("p q k h -> p k q h")
            C.ts(o3r, tr.ap, maskO.ap[:, e:e + 1], None, ALU.mult, None, [tr.b, maskO.b], [Wsr.b])
            C.ts(o3i, ti.ap, maskO.ap[:, e:e + 1], None, ALU.mult, None, [ti.b, maskO.b], [Wsi.b])

    C.memset(FIR.ap, 0.0, [FIR.b])
    for q in range(16):
        c, q4 = q // 4, q % 4
        p = C.ps()
        for tau in range(L):
            o = p.ap[32 * q4:32 * q4 + 32, tau * 32:(tau + 1) * 32]
            C.mm(o, Wsr.ap[:, q, tau, :], Cer.ap[:, q, :], True, False, [Wsr.b, Cer.b], [p.b], tile_position=(0, 32 * q4))
            C.mm(o, Wsi.ap[:, q, tau, :], Cei.ap[:, q, :], False, True, [Wsi.b, Cei.b], [p.b], tile_position=(0, 32 * q4))
        C.acopy(FIR.ap[32 * q4:32 * q4 + 32, c, :, 32 * q4:32 * q4 + 32],
                p.ap[32 * q4:32 * q4 + 32, :].rearrange("p (t h) -> p t h", h=32), [p.b], [FIR.b])
    tki = C.alloc([128, 16], I32)
    tkf = C.alloc([128, 16], F32)
    C.ts(thr.ap, lidt_o.ap, float(L), None, ALU.mult, None, [lidt_o.b], [thr.b])
    C.ts(tkf.ap, thr.ap, 1.0 / TWO_PI, None, ALU.mult, None, [thr.b], [tkf.b])
    C.cp(tki.ap, tkf.ap, [tkf.b], [tki.b])
    C.cp(tkf.ap, tki.ap, [tki.b], [tkf.b])
    C.stt(thr.ap, tkf.ap, -TWO_PI, thr.ap, ALU.mult, ALU.add, [tkf.b, thr.b], [thr.b])
    C.act(RL.ap, lrdt_o.ap, AF.Exp, [lrdt_o.b], [RL.b], scale=float(L))

    C.barrier()
    C.sp = sp_T
    yown = C.alloc([128, 4, NOWN], F32)
    sp_U = C.sp
    NJ1 = NJ + 2
    un = C.alloc([128, T], BF16)
    ud2 = [C.alloc([128, L, NJ], BF16) for _ in range(2)]
    tab2 = [(C.alloc([128, 4, NJ], F32), C.alloc([128, 4, NJ], F32)) for _ in range(2)]
    TB = 2
    ang = C.alloc([128, TB, NJ], F32)
    akf = C.alloc([128, TB, NJ], F32)
    aki = C.alloc([128, TB, NJ], I32)
    iota3 = iota.ap.rearrange("p (o j) -> p o j", o=1).to_broadcast([128, TB, NJ])
    zr = C.alloc([128, NJ], F32)
    zi = C.alloc([128, NJ], F32)
    za = C.alloc([128, NJ], F32)
    zb = C.alloc([128, NJ], F32)
    Zr = C.alloc([128, NJ], F32)
    Zi = C.alloc([128, NJ], F32)
    X2 = [(C.alloc([128, 4, NJ1], BF16, nb=4), C.alloc([128, 4, NJ1], BF16, nb=4)) for _ in range(2)]
    yall = C.alloc([128, T], F32)
    yall_kj = yall.ap.rearrange("p (j k) -> p k j", k=L)

    def load_c(c):
        ud = ud2[c % 2]
        tabC, tabS = tab2[c % 2]
        C.dma(un.ap, uD.ap[:, c, :], [uD.b], [un.b])
        C.acopy(ud.ap, un.ap.rearrange("p (j k) -> p k j", k=L), [un.b], [ud.b])
        for cb in range(4 // TB):
            q0 = 4 * c + TB * cb
            thr3 = thr.ap[:, q0:q0 + TB].rearrange("p (q o) -> p q o", o=1).to_broadcast([128, TB, NJ])
            for (dst, shift) in ((tabS, 0.0), (tabC, math.pi / 2)):
                C.tt(ang.ap, iota3, thr3, ALU.mult, [iota.b, thr.b], [ang.b])
                if shift:
                    C.ts(ang.ap, ang.ap, shift, None, ALU.add, None, [ang.b], [ang.b])
                C.ts(akf.ap, ang.ap, 1.0 / TWO_PI, None, ALU.mult, None, [ang.b], [akf.b])
                C.cp(aki.ap, akf.ap, [akf.b], [aki.b])
                C.cp(akf.ap, aki.ap, [aki.b], [akf.b])
                C.stt(akf.ap, akf.ap, -TWO_PI, ang.ap, ALU.mult, ALU.add, [akf.b, ang.b], [akf.b])
                C.clamp_pi(akf.ap, akf.b)
                C.act(dst.ap[:, TB * cb:TB * cb + TB, :], akf.ap, AF.Sin, [akf.b], [dst.b])

    def l12(c, q4):
        ud = ud2[c % 2]
        tabC, tabS = tab2[c % 2]
        Xr, Xi = X2[c % 2]
        q = 4 * c + q4
        pr = C.ps()
        pi = C.ps()
        for k in range(L):
            C.mm(pr.ap[:, 0:NJ], L1r.ap[32 * q4:32 * q4 + 32, c, L - 1 - k, :], ud.ap[32 * q4:32 * q4 + 32, k, :], k == 0, k == L - 1,
                 [L1r.b, ud.b], [pr.b], tile_position=(32 * q4, 0))
        for k in range(L):
            C.mm(pi.ap[:, 0:NJ], L1i.ap[32 * q4:32 * q4 + 32, c, L - 1 - k, :], ud.ap[32 * q4:32 * q4 + 32, k, :], k == 0, k == L - 1,
                 [L1i.b, ud.b], [pi.b], tile_position=(32 * q4, 0))
        cosJ = Tl(tabC.ap[:, q4, :], tabC.b)
        sinJ = Tl(tabS.ap[:, q4, :], tabS.b)
        C.tt(zr.ap, pr.ap[:, 0:NJ], cosJ.ap, ALU.mult, [pr.b, cosJ.b], [zr.b])
        C.tt(za.ap, pi.ap[:, 0:NJ], sinJ.ap, ALU.mult, [pi.b, sinJ.b], [za.b])
        C.tt(zr.ap, zr.ap, za.ap, ALU.add, [zr.b, za.b], [zr.b])
        C.tt(zi.ap, pi.ap[:, 0:NJ], cosJ.ap, ALU.mult, [pi.b, cosJ.b], [zi.b])
        C.tt(zb.ap, pr.ap[:, 0:NJ], sinJ.ap, ALU.mult, [pr.b, sinJ.b], [zb.b])
        C.tt(zi.ap, zi.ap, zb.ap, ALU.subtract, [zi.b, zb.b], [zi.b])
        Rb = RL.ap[:, q:q + 1].to_broadcast([128, NJ])
        C.S.add("dve", lambda e, Rb=Rb: e.tensor_tensor_scan(out=Zr.ap, data0=Rb, data1=zr.ap, initial=0.0, op0=ALU.mult, op1=ALU.add),
                reads=[RL.b, zr.b], writes=[Zr.b])
        C.S.add("dve", lambda e, Rb=Rb: e.tensor_tensor_scan(out=Zi.ap, data0=Rb, data1=zi.ap, initial=0.0, op0=ALU.mult, op1=ALU.add),
                reads=[RL.b, zi.b], writes=[Zi.b])
        C.memset(Xr.ap[:, q4, 0:2], 0.0, [Xr.b[q4]])
        C.memset(Xi.ap[:, q4, 0:2], 0.0, [Xi.b[q4]])
        C.tt(za.ap, Zr.ap, cosJ.ap, ALU.mult, [Zr.b, cosJ.b], [za.b])
        C.tt(zb.ap, Zi.ap, sinJ.ap, ALU.mult, [Zi.b, sinJ.b], [zb.b])
        C.tt(Xr.ap[:, q4, 1:NJ + 1], za.ap, zb.ap, ALU.subtract, [za.b, zb.b], [Xr.b[q4]])
        C.tt(za.ap, Zi.ap, cosJ.ap, ALU.mult, [Zi.b, cosJ.b], [za.b])
        C.tt(zb.ap, Zr.ap, sinJ.ap, ALU.mult, [Zr.b, sinJ.b], [zb.b])
        C.tt(Xi.ap[:, q4, 1:NJ + 1], za.ap, zb.ap, ALU.add, [za.b, zb.b], [Xi.b[q4]])

    def l3(c, k):
        ud = ud2[c % 2]
        Xr, Xi = X2[c % 2]
        p = C.ps()
        for q4 in range(4):
            q = 4 * c + q4
            o = p.ap[32 * q4:32 * q4 + 32, 0:NJ]
            tp = (0, 32 * q4)
            C.mm(o, L3r.ap[:, q, k, :], Xr.ap[:, q4, 0:NJ], True, False, [L3r.b, Xr.b[q4]], [p.b], tile_position=tp)
            C.mm(o, L3i.ap[:, q, k, :], Xi.ap[:, q4, 0:NJ], False, False, [L3i.b, Xi.b[q4]], [p.b], tile_position=tp)
            for kp in range(k + 1):
                C.mm(o, FIR.ap[:, c, k - kp, 32 * q4:32 * q4 + 32], ud.ap[:, kp, :], False, kp == k, [FIR.b, ud.b], [p.b], tile_position=tp)
        C.stt(yall_kj[:, k, :], ud.ap[:, k, :], gt["d_fm"].ap[:, c:c + 1], p.ap[:, 0:NJ], ALU.mult, ALU.add,
              [ud.b, gt["d_fm"].b, p.b], [yall.b])

    load_c(0)
    for q4 in range(4):
        l12(0, q4)
    for c in range(4):
        if c + 1 < 4:
            load_c(c + 1)
        for i in range(4):
            if c + 1 < 4:
                l12(c + 1, i)
            for k in range(4 * i, 4 * i + 4):
                l3(c, k)
        ya4 = yall.ap.rearrange("p (ch b t) -> p ch b t", b=4, t=128)
        yo4 = yown.ap[:, c, :].rearrange("p (ch b t) -> p ch b t", b=2, t=128)
        for (ob, ba, bb) in ((0, 0, 1), (1, 3, 2)):
            C.ts(yo4[:, :, ob, :], ya4[:, :, ba, :], sel.ap[:, 0:1], None, ALU.mult, None, [yall.b, sel.b], [yown.b])
            C.stt(yo4[:, :, ob, :], ya4[:, :, bb, :], sel.ap[:, 1:2], yo4[:, :, ob, :], ALU.mult, ALU.add, [yall.b, sel.b, yown.b], [yown.b])
    C.barrier()
    C.sp = sp_U
    gg = C.alloc([128, 4, 512], F32)
    gb = C.alloc([128, 4, 512], BF16)
    ta = C.alloc([128, 512], F32)
    tb = C.alloc([128, 512], F32)
    sq_t = C.alloc([128, 8, 512], BF16)
    r2 = C.alloc([128, 512], F32)
    yn = C.alloc([128, 4, 512], BF16)
    CG = 2.0 * math.sqrt(2.0 / math.pi)
    for m in range(4):
        o0 = m * 512
        for k in range(4):
            y = yown.ap[:, k, o0:o0 + 512]
            C.tt(ta.ap, y, y, ALU.mult, [yown.b], [ta.b])
            C.ts(ta.ap, ta.ap, 0.044715, 1.0, ALU.mult, ALU.add, [ta.b], [ta.b])
            C.tt(ta.ap, ta.ap, y, ALU.mult, [ta.b, yown.b], [ta.b])
            C.act(tb.ap, ta.ap, AF.Sigmoid, [ta.b], [tb.b], scale=CG)
            C.tt(gg.ap[:, k, :], tb.ap, y, ALU.mult, [tb.b, yown.b], [gg.b])
            C.cp(gb.ap[:, k, :], gg.ap[:, k, :], [gg.b], [gb.b])
        for d in range(4):
            p = C.ps()
            for k in range(4):
                C.mm(p.ap, w_glu.ap[:, k, d * 128:(d + 1) * 128], gb.ap[:, k, :], k == 0, k == 3, [w_glu.b, gb.b], [p.b])
            C.act(tb.ap, p.ap, AF.Sigmoid, [p.b, gt["b_glu"].b], [tb.b], bias=gt["b_glu"].ap[:, d:d + 1], scale=1.0)
            C.tt(gg.ap[:, d, :], gg.ap[:, d, :], tb.ap, ALU.mult, [gg.b, tb.b], [gg.b])
        if dbg:
            C.dma(dbg["d_yssm"].ap.rearrange("(k p) t -> p k t", p=128)[:, :, o0:o0 + 512], gg.ap, [gg.b], [dbg["d_yssm"].b])
        rstd_from([(gg.ap[:, k, :], 128, gg.b) for k in range(4)], 512, 512, sq_t, r2)
        for k in range(4):
            C.stt(yn.ap[:, k, :], gg.ap[:, k, :], gt["g_os"].ap[:, k:k + 1], r2.ap, ALU.mult, ALU.mult,
                  [gg.b, gt["g_os"].b, r2.b], [yn.b])
        C.dma(ysD.ap[:, :, o0:o0 + 512], yn.ap, [yn.b], [ysD.b])


def own_blocks(j):
    out = []
    for c in range(8):
        out += [4 * c + (0 if j == 0 else 1), 4 * c + (3 if j == 0 else 2)]
    return out


def make_in_maps(inp):
    f32 = np.float32
    A = lambda a: np.ascontiguousarray(a)
    fm = lambda g: A(np.asarray(g, f32).reshape(-1, 128).T)
    col = lambda g: A(np.asarray(g, f32).reshape(-1, 1))
    sw = np.concatenate([np.arange(32, 64), np.arange(0, 32)])
    w_in = np.asarray(inp["w_in"][0], f32)
    w_in2 = A(np.concatenate([w_in[:, 0:640], w_in[:, 640:704], w_in[:, 640:704][:, sw], w_in[:, 704:1216]], axis=1))
    w_uq = np.asarray(inp["mla_w_uq"][0], f32).reshape(384, 4, 192)
    w_uq2 = A(np.concatenate([w_uq[:, :, 0:128], w_uq[:, :, 128:192], w_uq[:, :, 128:192][:, :, sw]], axis=2).reshape(384, 1024))
    w_ukv = np.asarray(inp["mla_w_ukv"][0], f32).reshape(256, 4, 256)
    w_ukv2 = A(np.concatenate([w_ukv[:, :, 0:128].reshape(256, 512), w_ukv[:, :, 128:256].reshape(256, 512)], axis=1))
    gq = np.asarray(inp["mla_qk_norm_q"][0], f32)
    gk = np.asarray(inp["mla_qk_norm_k"][0], f32)
    a_re = np.asarray(inp["ssm_a_re"][0], f32); a_im = np.asarray(inp["ssm_a_im"][0], f32)
    ldt = np.asarray(inp["ssm_log_dt"][0], f32)
    b_re = np.asarray(inp["ssm_b_re"][0], f32); b_im = np.asarray(inp["ssm_b_im"][0], f32)
    c_re = np.asarray(inp["ssm_c_re"][0], f32); c_im = np.asarray(inp["ssm_c_im"][0], f32)

    def in_side_gp(a):
        v = a.reshape(4, 4, 2, 64)
        v = np.transpose(v, (1, 2, 0, 3))
        return A(np.broadcast_to(v[:, :, None], (4, 2, 16, 4, 64)).reshape(128, 4, 64))

    def in_side_b(b):
        v = b.reshape(4, 4, 2, 64, 16)
        return A(np.transpose(v, (1, 2, 4, 0, 3)).reshape(128, 4, 64))

    def out_side_gp(a):
        v = a.reshape(16, 2, 64)
        return A(np.transpose(v, (1, 2, 0)).reshape(128, 16))

    def out_side_c(cc):
        v = cc.reshape(16, 2, 16, 64)
        return A(np.transpose(v, (1, 3, 0, 2)).reshape(128, 16, 16))

    def out_side_b(b):
        v = b.reshape(16, 2, 64, 16)
        return A(np.transpose(v, (1, 2, 0, 3)).reshape(128, 16, 16))

    ldt_gp = np.broadcast_to(ldt[:, None], (32, 64))
    common = {
        "ffn1_wg": A(inp["ffn1_w_gate"][0]), "ffn1_wu": A(inp["ffn1_w_up"][0]), "ffn1_wd": A(inp["ffn1_w_down"][0]),
        "ffn2_wg": A(inp["ffn2_w_gate"][0]), "ffn2_wu": A(inp["ffn2_w_up"][0]), "ffn2_wd": A(inp["ffn2_w_down"][0]),
        "w_in": w_in2, "w_uq": w_uq2, "w_ukv": w_ukv2,
        "w_glu": A(inp["ssm_w_glu"][0]), "w_o": A(inp["w_o"][0]), "w_xq": A(inp["xattn_w_q"][0]),
        "w_xkv": A(inp["xattn_w_kv"][0]), "w_xo": A(inp["xattn_w_o"][0]),
        "g_ffn1": fm(inp["ffn1_norm"][0]), "g_mix": fm(inp["mix_norm"][0]), "g_q": fm(inp["mla_q_norm"][0]),
        "g_kv": fm(inp["mla_kv_norm"][0]),
        "g_qn": col(gq[0:128]), "g_qr": col(gq[128:192]), "g_qrs": col(gq[128:192][sw]),
        "g_kn": col(gk[0:128]), "g_kr": col(gk[128:192]), "g_krs": col(gk[128:192][sw]),
        "g_om": fm(inp["out_norm_mla"][0]), "g_os": fm(inp["out_norm_ssm"][0]), "g_xa": fm(inp["xattn_norm"][0]),
        "g_mem": fm(inp["mem_norm"][0]), "g_xq": col(inp["xattn_q_norm"][0]), "g_xk": col(inp["xattn_k_norm"][0]),
        "g_ffn2": fm(inp["ffn2_norm"][0]), "b_glu": fm(inp["ssm_b_glu"][0]), "d_fm": fm(np.asarray(inp["ssm_d"][0], f32).reshape(-1)),
        "are_in": in_side_gp(a_re), "aim_in": in_side_gp(a_im), "ldt_in": in_side_gp(ldt_gp),
        "bre_in": in_side_b(b_re), "bim_in": in_side_b(b_im),
        "are_out": out_side_gp(a_re), "aim_out": out_side_gp(a_im), "ldt_out": out_side_gp(ldt_gp),
        "cre_out": out_side_c(c_re), "cim_out": out_side_c(c_im),
        "bre_out": out_side_b(b_re), "bim_out": out_side_b(b_im),
    }
    d = np.arange(64)
    invf = (10000.0 ** (-(d % 32).astype(np.float64) / 32.0)).astype(f32).reshape(64, 1)
    invf = (np.float32(10000.0) ** (-(np.arange(32, dtype=f32)) / np.float32(32))).astype(f32)
    invf = A(np.concatenate([invf, invf]).reshape(64, 1))
    sgn = A(np.where(d < 32, -1.0, 1.0).astype(f32).reshape(64, 1))
    iota = A(np.broadcast_to(np.arange(1, NJ + 1, dtype=f32)[None, :], (128, NJ)))
    r = np.arange(128)
    ee = (r // 16) % 2
    maskE = A(np.stack([(ee == 0), (ee == 1)], axis=1).astype(f32))
    kvec = A(np.broadcast_to(np.arange(0, L + 1, dtype=f32)[None, :], (128, L + 1)))
    common.update({"invf": invf, "sgn": sgn, "iota": iota, "maskE": maskE, "kvec": kvec})
    x = np.asarray(inp["x"], f32)
    mem = np.asarray(inp["mem"], f32)
    pos = np.asarray(inp["positions"]).astype(np.int32)
    maps = []
    for core in range(8):
        b, j = core // 2, core % 2
        ob = own_blocks(j)
        own_tok = np.concatenate([np.arange(g * 128, (g + 1) * 128) for g in ob])
        qg = np.array(ob[0:4])
        qpos = (qg[:, None] * 128 + np.arange(128)[None, :]).reshape(-1)
        am = np.zeros((128, 8, 512), f32)
        for a in range(8):
            kpos = a * 128 + np.arange(128)
            am[:, a, :] = (kpos[:, None] <= qpos[None, :]).astype(f32)
        m = dict(common)
        m.update({
            "xT": A(x[b].T), "memT": A(mem[b].T),
            "pos_all": A(np.broadcast_to(pos[b][None, :], (64, T))),
            "pos_own": A(np.broadcast_to(pos[b][own_tok][None, :], (64, NOWN))),
            "sel": A(np.broadcast_to(np.array([1.0, 0.0] if j == 0 else [0.0, 1.0], f32)[None, :], (128, 2))),
            "amask": am,
        })
        maps.append(m)
    return maps


_NC_CACHE = {}


def kernel(**inputs):
    if "nc" not in _NC_CACHE:
        _NC_CACHE["nc"] = build(False)
    nc = _NC_CACHE["nc"]
    maps = make_in_maps(inputs)
    res = run_bass_kernel_spmd(nc, maps, core_ids=list(range(8)))
    out = np.zeros((4, T, D), np.float32)
    for core in range(8):
        b, j = core // 2, core % 2
        ob = own_blocks(j)
        o = np.asarray(res.results[core]["outT"], np.float32)
        for i, g in enumerate(ob):
            out[b, g * 128:(g + 1) * 128, :] = o[:, i * 128:(i + 1) * 128].T
    return out
```

```python
import math
import contextlib
import numpy as np
import ml_dtypes
import concourse.bass as bass
import concourse.mybir as mybir
from concourse.bass_utils import run_bass_kernel_spmd

F32 = mybir.dt.float32
BF16 = mybir.dt.bfloat16
I32 = mybir.dt.int32
U8 = mybir.dt.uint8
ALU = mybir.AluOpType
AF = mybir.ActivationFunctionType

D = 1024
T = 4096
NOWN = 2048
DFF = 2752
NF = 22
EPS = 1e-6
L = 16
NJ = T // L
TWO_PI = 2.0 * math.pi


class Buf:
    __slots__ = ("w", "r")

    def __init__(self, w=None):
        self.w = w
        self.r = []


class Op:
    __slots__ = ("idx", "eng", "fn", "deps", "dma", "slot", "ticket", "inc", "waits")

    def __init__(self, idx, eng, fn, dma):
        self.idx = idx
        self.eng = eng
        self.fn = fn
        self.deps = {}
        self.dma = dma
        self.slot = None
        self.ticket = None
        self.inc = False
        self.waits = []


class Sched:
    NSLOT = 12

    def __init__(self, nc):
        self.nc = nc
        self.ops = []
        self.slot_ctr = {}
        self.bar = None
        self.touched = set()

    def add(self, eng, fn, reads=(), writes=(), dma=False):
        op = Op(len(self.ops), eng, fn, dma)
        for b in reads:
            if b.w is not None:
                op.deps[b.w] = "raw"
        for b in writes:
            if b.w is not None and b.w not in op.deps:
                op.deps[b.w] = "waw"
            for r in b.r:
                if r not in op.deps:
                    op.deps[r] = "war"
        for b in reads:
            b.r.append(op.idx)
            self.touched.add(b)
        for b in writes:
            b.w = op.idx
            b.r = []
            self.touched.add(b)
        if dma:
            c = self.slot_ctr.get(eng, 0)
            op.slot = (eng, c % self.NSLOT)
            self.slot_ctr[eng] = c + 1
        self.ops.append(op)
        return op

    def emit(self):
        nc = self.nc
        ops = self.ops
        for op in ops:
            best = {}
            for p, kind in op.deps.items():
                P = ops[p]
                if P.dma:
                    key = ("dma",) + P.slot
                else:
                    if P.eng == op.eng and not op.dma:
                        if P.eng == "pe":
                            continue
                    key = ("eng", P.eng)
                if key not in best or best[key] < p:
                    best[key] = p
            op.waits = sorted(best.items(), key=lambda kv: kv[1])
            for _, p in op.waits:
                ops[p].inc = True
        cnt = {}
        for op in ops:
            if op.dma:
                key = ("dma",) + op.slot
                cnt[key] = cnt.get(key, 0) + 1
                op.ticket = 16 * cnt[key]
            elif op.inc:
                key = ("eng", op.eng)
                cnt[key] = cnt.get(key, 0) + 1
                op.ticket = cnt[key]
        keys = sorted(set(cnt.keys()), key=str)
        sems = {}
        with contextlib.ExitStack() as es:
            for k in keys:
                sems[k] = es.enter_context(nc.semaphore("s_" + "_".join(str(x) for x in k)))
            block = es.enter_context(nc.Block())
            by_eng = {}
            for op in ops:
                by_eng.setdefault(op.eng, []).append(op)
            engmap = {"pe": "tensor", "act": "scalar", "dve": "vector", "pool": "gpsimd", "sp": "sync"}

            def make(elist):
                def body(e):
                    seen = {}
                    for op in elist:
                        for key, p in op.waits:
                            v = ops[p].ticket
                            if seen.get(key, 0) < v:
                                e.wait_ge(sems[key], v)
                                seen[key] = v
                        if op.dma:
                            key = ("dma",) + op.slot
                            prev = op.ticket - 16
                            if prev > 0 and seen.get(key, 0) < prev:
                                e.wait_ge(sems[key], prev)
                                seen[key] = prev
                            op.fn(e).then_inc(sems[key], 16)
                        else:
                            ins = op.fn(e)
                            if op.inc:
                                ins.then_inc(sems[("eng", op.eng)], 1)
                    for op in elist:
                        if op.dma:
                            key = ("dma",) + op.slot
                            if seen.get(key, 0) < op.ticket:
                                e.wait_ge(sems[key], op.ticket)
                                seen[key] = op.ticket
                return body

            for engname, elist in by_eng.items():
                getattr(block, engmap[engname])(make(elist))


class Tl:
    def __init__(self, ap, b):
        self.ap = ap
        self.b = b

    def __getitem__(self, k):
        return self.ap[k]


def _dtsize(dt):
    return {F32: 4, BF16: 2, I32: 4, U8: 1}[dt]


class Ctx:
    ARENA = 212480

    def __init__(self, nc):
        self.nc = nc
        self.S = Sched(nc)
        self.arena = nc.alloc_sbuf_tensor("arena", [128, self.ARENA], U8)
        self.sp = 0
        self.barrier_idx = None
        self.psum = [Tl(nc.alloc_psum_tensor("pb%d" % i, [128, 512], F32)[:, :], Buf()) for i in range(8)]
        self.pi = 0
        self.reserved = set()

    def alloc(self, shape, dt, nb=1):
        n = 1
        for s in shape[1:]:
            n *= s
        nbytes = (n * _dtsize(dt) + 63) // 64 * 64
        assert self.sp + nbytes <= self.ARENA, ("SBUF overflow", self.sp, nbytes)
        ap = self.arena[:, self.sp:self.sp + n * _dtsize(dt)].bitcast(dt)
        self.sp += nbytes
        if len(shape) == 3:
            ap = ap.rearrange("p (a b) -> p a b", b=shape[2])
        elif len(shape) == 4:
            ap = ap.rearrange("p (a b c) -> p a b c", b=shape[2], c=shape[3])
        if shape[0] < 128:
            ap = ap[0:shape[0]]
        if nb == 1:
            return Tl(ap, Buf(self.barrier_idx))
        return Tl(ap, [Buf(self.barrier_idx) for _ in range(nb)])

    def ps(self):
        while (self.pi % 8) in self.reserved:
            self.pi += 1
        t = self.psum[self.pi % 8]
        self.pi += 1
        return t

    def barrier(self):
        S = self.S
        bufs = list(S.touched)
        dummy = self._dummy
        op = S.add("dve", lambda e: e.memset(dummy.ap, 0.0), reads=bufs, writes=bufs + [dummy.b])
        self.barrier_idx = op.idx
        S.touched = set()
        for t in self.psum:
            t.b.w = op.idx
            t.b.r = []
        return op.idx

    def mm(self, out, lhsT, rhs, start, stop, reads, writes, **kw):
        self.S.add("pe", lambda e: e.matmul(out, lhsT=lhsT, rhs=rhs, start=start, stop=stop, **kw),
                   reads=reads, writes=writes)

    def act(self, out, in_, func, reads, writes, bias=None, scale=None):
        kw = {}
        if bias is not None:
            kw["bias"] = bias
        if scale is not None:
            kw["scale"] = scale
        self.S.add("act", lambda e: e.activation(out=out, in_=in_, func=func, **kw), reads=reads, writes=writes)

    def ts(self, out, in0, s1, s2, op0, op1, reads, writes, eng="dve"):
        if op1 is None:
            self.S.add(eng, lambda e: e.tensor_scalar(out=out, in0=in0, scalar1=s1, scalar2=None, op0=op0),
                       reads=reads, writes=writes)
        else:
            self.S.add(eng, lambda e: e.tensor_scalar(out=out, in0=in0, scalar1=s1, scalar2=s2, op0=op0, op1=op1),
                       reads=reads, writes=writes)

    def stt(self, out, in0, scalar, in1, op0, op1, reads, writes, eng="dve"):
        self.S.add(eng, lambda e: e.scalar_tensor_tensor(out=out, in0=in0, scalar=scalar, in1=in1, op0=op0, op1=op1),
                   reads=reads, writes=writes)

    def tt(self, out, in0, in1, op, reads, writes, eng="dve"):
        self.S.add(eng, lambda e: e.tensor_tensor(out=out, in0=in0, in1=in1, op=op), reads=reads, writes=writes)

    def cp(self, out, in_, reads, writes, eng="dve"):
        self.S.add(eng, lambda e: e.tensor_copy(out=out, in_=in_), reads=reads, writes=writes)

    def memset(self, out, val, writes, eng="dve"):
        self.S.add(eng, lambda e: e.memset(out, val), writes=writes)

    def acopy(self, out, in_, reads, writes):
        self.S.add("act", lambda e: e.copy(out=out, in_=in_), reads=reads, writes=writes)

    def clamp_pi(self, ap, b):
        self.ts(ap, ap, math.pi, -math.pi, ALU.min, ALU.max, [b], [b])

    def recip(self, out, in_, reads, writes):
        self.S.add("dve", lambda e: e.reciprocal(out=out, in_=in_), reads=reads, writes=writes)

    def dma(self, out, in_, reads, writes, q="sp"):
        self.S.add(q, lambda e: e.dma_start(out=out, in_=in_), reads=reads, writes=writes, dma=True)


def _bl(x):
    return x if isinstance(x, (list, tuple)) else [x]


def build(debug=False, stop=None):
    nc = bass.Bass("TRN2", target_bir_lowering=False)
    C = Ctx(nc)
    S = C.S

    def din(name, shape, dt=F32):
        return nc.dram_tensor(name, list(shape), dt, kind="ExternalInput").ap()

    def dscr(name, shape, dt):
        return Tl(nc.dram_tensor(name, list(shape), dt, kind="Internal").ap(), Buf())

    xT = din("xT", [D, T])
    memT = din("memT", [D, 256])
    pos_all = din("pos_all", [64, T], I32)
    pos_own = din("pos_own", [64, NOWN], I32)
    sel_d = din("sel", [128, 2])
    amask_d = din("amask", [128, 8, 512])
    invf_d = din("invf", [64, 1])
    sgn_d = din("sgn", [64, 1])
    iota_d = din("iota", [128, NJ])
    maskE_d = din("maskE", [128, 2])
    kvec_d = din("kvec", [128, L + 1])
    W = {}
    for nm, shp in [("ffn1_wg", [D, DFF]), ("ffn1_wu", [D, DFF]), ("ffn1_wd", [DFF, D]),
                    ("ffn2_wg", [D, DFF]), ("ffn2_wu", [D, DFF]), ("ffn2_wd", [DFF, D]),
                    ("w_in", [D, 1280]), ("w_uq", [384, 1024]), ("w_ukv", [256, 1024]),
                    ("w_glu", [512, 512]), ("w_o", [D, D]), ("w_xq", [D, 512]), ("w_xkv", [D, 1024]),
                    ("w_xo", [512, D])]:
        W[nm] = din(nm, shp)
    G = {}
    for nm, shp in [("g_ffn1", [128, 8]), ("g_mix", [128, 8]), ("g_q", [128, 3]), ("g_kv", [128, 2]),
                    ("g_qn", [128, 1]), ("g_qr", [64, 1]), ("g_qrs", [64, 1]),
                    ("g_kn", [128, 1]), ("g_kr", [64, 1]), ("g_krs", [64, 1]),
                    ("g_om", [128, 4]), ("g_os", [128, 4]), ("g_xa", [128, 8]), ("g_mem", [128, 8]),
                    ("g_xq", [128, 1]), ("g_xk", [128, 1]), ("g_ffn2", [128, 8]),
                    ("b_glu", [128, 4]), ("d_fm", [128, 4]),
                    ("are_in", [128, 4, 64]), ("aim_in", [128, 4, 64]), ("ldt_in", [128, 4, 64]),
                    ("bre_in", [128, 4, 64]), ("bim_in", [128, 4, 64]),
                    ("are_out", [128, 16]), ("aim_out", [128, 16]), ("ldt_out", [128, 16]),
                    ("cre_out", [128, 16, 16]), ("cim_out", [128, 16, 16]),
                    ("bre_out", [128, 16, 16]), ("bim_out", [128, 16, 16])]:
        G[nm] = din(nm, shp)
    outT = nc.dram_tensor("outT", [D, NOWN], F32, kind="ExternalOutput").ap()
    dbg = {}
    if debug:
        for nm, shp in [("d_x1own", [D, NOWN]), ("d_yssm", [512, NOWN]), ("d_ymla", [512, NOWN]), ("d_x2", [D, NOWN]),
                        ("d_x3", [D, NOWN])]:
            dbg[nm] = Tl(nc.dram_tensor(nm, shp, F32, kind="ExternalOutput").ap(), Buf())

    def wscr(tag):
        a = Tl(nc.dram_tensor("WgB" + tag, [128, 8, DFF], BF16, kind="Internal").ap(), [Buf() for _ in range(11)])
        b = Tl(nc.dram_tensor("WuB" + tag, [128, 8, DFF], BF16, kind="Internal").ap(), [Buf() for _ in range(11)])
        c = Tl(nc.dram_tensor("WdB" + tag, [128, 8, NF, 128], BF16, kind="Internal").ap(), [Buf() for _ in range(8)])
        ct = Tl(c.ap, [Buf() for _ in range(8)])
        return (a, b, c, ct)
    scr1 = wscr("1")
    scr2 = wscr("2")
    KnD = dscr("KnD", [128, 4, T], BF16)
    krrD = dscr("krrD", [64, T], BF16)
    VD = dscr("VD", [128, 32, 512], BF16)
    rstdkD = dscr("rstdkD", [128, 32, 4], F32)
    uD = dscr("uD", [128, 4, T], BF16)
    QnD = dscr("QnD", [128, 4, NOWN], BF16)
    QrD = dscr("QrD", [64, 4, NOWN], BF16)
    x1D = dscr("x1D", [128, 8, NOWN], F32)
    ysD = dscr("ysD", [128, 4, NOWN], BF16)

    C._dummy = C.alloc([128, 8], F32)
    ones_bf = C.alloc([128, 128], BF16)
    eps_t = C.alloc([128, 1], F32)
    sel = C.alloc([128, 2], F32)
    gt = {}
    for nm in ["g_ffn1", "g_mix", "g_q", "g_kv", "g_qn", "g_qr", "g_qrs", "g_kn", "g_kr", "g_krs", "g_om", "g_os",
               "g_xa", "g_mem", "g_xq", "g_xk", "g_ffn2", "b_glu", "d_fm"]:
        shp = [int(s) for s in G[nm].shape]
        gt[nm] = C.alloc(shp, F32)
        C.dma(gt[nm].ap, G[nm], [], [gt[nm].b])
    invf = C.alloc([64, 1], F32)
    sgn = C.alloc([64, 1], F32)
    C.dma(invf.ap, invf_d, [], [invf.b])
    C.dma(sgn.ap, sgn_d, [], [sgn.b])
    C.dma(sel.ap, sel_d, [], [sel.b])
    C.memset(ones_bf.ap, 1.0, [ones_bf.b])
    C.memset(eps_t.ap, EPS, [eps_t.b])
    base_sp = C.sp

    def rstd_from(srcs, N, Dn, sq_t, rstd_t):
        ps = C.ps()
        n = len(srcs)
        for i, (ap, K, bufs) in enumerate(srcs):
            C.act(sq_t.ap[0:K, i, 0:N], ap, AF.Square, _bl(bufs), [sq_t.b])
        for i, (ap, K, bufs) in enumerate(srcs):
            C.mm(ps.ap[:, 0:N], ones_bf.ap[0:K, :], sq_t.ap[0:K, i, 0:N], i == 0, i == n - 1, [ones_bf.b, sq_t.b], [ps.b])
        C.act(rstd_t.ap[:, 0:N], ps.ap[:, 0:N], AF.Sqrt, [ps.b, eps_t.b], [rstd_t.b], bias=eps_t.ap, scale=1.0 / Dn)
        C.recip(rstd_t.ap[:, 0:N], rstd_t.ap[:, 0:N], [rstd_t.b], [rstd_t.b])

    def rope_tables(pos_ap_dram, col0, N, cosT, sinS, tmpi, tmpf, tmpk):
        C.dma(tmpi.ap[:, 0:N], pos_ap_dram[:, col0:col0 + N], [], [tmpi.b])
        C.cp(tmpf.ap[:, 0:N], tmpi.ap[:, 0:N], [tmpi.b], [tmpf.b])
        C.ts(tmpf.ap[:, 0:N], tmpf.ap[:, 0:N], invf.ap, None, ALU.mult, None, [tmpf.b, invf.b], [tmpf.b])
        C.ts(tmpk.ap[:, 0:N], tmpf.ap[:, 0:N], 1.0 / TWO_PI, None, ALU.mult, None, [tmpf.b], [tmpk.b])
        C.cp(tmpi.ap[:, 0:N], tmpk.ap[:, 0:N], [tmpk.b], [tmpi.b])
        C.cp(tmpk.ap[:, 0:N], tmpi.ap[:, 0:N], [tmpi.b], [tmpk.b])
        C.stt(tmpk.ap[:, 0:N], tmpk.ap[:, 0:N], -TWO_PI, tmpf.ap[:, 0:N], ALU.mult, ALU.add, [tmpk.b, tmpf.b], [tmpk.b])
        C.clamp_pi(tmpk.ap[:, 0:N], tmpk.b)
        C.act(sinS.ap[:, 0:N], tmpk.ap[:, 0:N], AF.Sin, [tmpk.b], [sinS.b])
        C.ts(sinS.ap[:, 0:N], sinS.ap[:, 0:N], sgn.ap, None, ALU.mult, None, [sinS.b, sgn.b], [sinS.b])
        C.ts(tmpf.ap[:, 0:N], tmpf.ap[:, 0:N], math.pi / 2, None, ALU.add, None, [tmpf.b], [tmpf.b])
        C.ts(tmpk.ap[:, 0:N], tmpf.ap[:, 0:N], 1.0 / TWO_PI, None, ALU.mult, None, [tmpf.b], [tmpk.b])
        C.cp(tmpi.ap[:, 0:N], tmpk.ap[:, 0:N], [tmpk.b], [tmpi.b])
        C.cp(tmpk.ap[:, 0:N], tmpi.ap[:, 0:N], [tmpi.b], [tmpk.b])
        C.stt(tmpk.ap[:, 0:N], tmpk.ap[:, 0:N], -TWO_PI, tmpf.ap[:, 0:N], ALU.mult, ALU.add, [tmpk.b, tmpf.b], [tmpk.b])
        C.clamp_pi(tmpk.ap[:, 0:N], tmpk.b)
        C.act(cosT.ap[:, 0:N], tmpk.ap[:, 0:N], AF.Sin, [tmpk.b], [cosT.b])

    def ffn_norm_gen(xs, hbf, gname, sq_t, rstd_t):
        N = 512
        rstd_from([(xs.ap[:, k, :], 128, xs.b) for k in range(8)], N, D, sq_t, rstd_t)
        yield
        for k in range(8):
            C.stt(hbf.ap[:, k, :], xs.ap[:, k, :], gt[gname].ap[:, k:k + 1], rstd_t.ap[:, 0:N], ALU.mult, ALU.mult,
                  [xs.b, gt[gname].b, rstd_t.b], [hbf.b])
            if k % 2 == 1:
                yield

    def ffn(xs, hbf, wg_d, wu_d, wd_d, gname, sq_t, rstd_t, actT, wgs, wus, wds, silt, scr, first, bg=None, bgB=None, do_norm=True, nstep=2, bgB_first=False):
        N = 512
        if do_norm:
            for _ in ffn_norm_gen(xs, hbf, gname, sq_t, rstd_t):
                pass
        wgv = wg_d.rearrange("(k p) f -> p k f", p=128)
        wuv = wu_d.rearrange("(k p) f -> p k f", p=128)
        NG = 11
        WgB, WuB, WdB, WdBt = scr

        def load_wd(d):
            slot = d % 3
            if first:
                C.dma(wds.ap[:, slot, 0:21, :], wd_d[0:21 * 128, d * 128:(d + 1) * 128].rearrange("(f p) d -> p f d", p=128),
                      [], [wds.b[slot]], q="pool")
                C.dma(wds.ap[0:64, slot, 21, :], wd_d[21 * 128:DFF, d * 128:(d + 1) * 128], [], [wds.b[slot + 3]], q="pool")
                C.dma(WdB.ap[:, d, 0:21, :], wds.ap[:, slot, 0:21, :], [wds.b[slot]], [WdB.b[d]])
                C.dma(WdB.ap[0:64, d, 21, :], wds.ap[0:64, slot, 21, :], [wds.b[slot + 3]], [WdBt.b[d]])
            else:
                C.dma(wds.ap[:, slot, 0:21, :], WdB.ap[:, d, 0:21, :], [WdB.b[d]], [wds.b[slot]], q="sp")
                C.dma(wds.ap[0:64, slot, 21, :], WdB.ap[0:64, d, 21, :], [WdBt.b[d]], [wds.b[slot + 3]], q="sp")

        for g in range(NG):
            if g in (5, 7, 9):
                load_wd((g - 5) // 2)
            c0 = g * 256
            cw = min(256, DFF - c0)
            slot = g % 3
            if first:
                C.dma(wgs.ap[:, slot, :, 0:cw], wgv[:, :, c0:c0 + cw], [], [wgs.b[slot]], q="pool")
                C.dma(wus.ap[:, slot, :, 0:cw], wuv[:, :, c0:c0 + cw], [], [wus.b[slot]], q="pool")
                C.dma(WgB.ap[:, :, c0:c0 + cw], wgs.ap[:, slot, :, 0:cw], [wgs.b[slot]], [WgB.b[g]])
                C.dma(WuB.ap[:, :, c0:c0 + cw], wus.ap[:, slot, :, 0:cw], [wus.b[slot]], [WuB.b[g]])
            else:
                C.dma(wgs.ap[:, slot, :, 0:cw], WgB.ap[:, :, c0:c0 + cw], [WgB.b[g]], [wgs.b[slot]], q="sp")
                C.dma(wus.ap[:, slot, :, 0:cw], WuB.ap[:, :, c0:c0 + cw], [WuB.b[g]], [wus.b[slot]], q="sp")
            for ff in range(2):
                f = 2 * g + ff
                if f >= NF:
                    break
                fs = min(128, DFF - f * 128)
                pg = C.ps()
                pu = C.ps()
                for k in range(8):
                    C.mm(pg.ap[0:fs, :], wgs.ap[:, slot, k, ff * 128:ff * 128 + fs], hbf.ap[:, k, :], k == 0, k == 7,
                         [wgs.b[slot], hbf.b], [pg.b])
                for k in range(8):
                    C.mm(pu.ap[0:fs, :], wus.ap[:, slot, k, ff * 128:ff * 128 + fs], hbf.ap[:, k, :], k == 0, k == 7,
                         [wus.b[slot], hbf.b], [pu.b])
                C.act(silt.ap[0:fs, f % 2, :], pg.ap[0:fs, :], AF.Silu, [pg.b], [silt.b[f % 2]])
                C.tt(actT.ap[0:fs, f, :], pu.ap[0:fs, :], silt.ap[0:fs, f % 2, :], ALU.mult, [pu.b, silt.b[f % 2]], [actT.b[f]])
                if bg is not None:
                    for _ in range(nstep):
                        next(bg, None)
        for d in range(8):
            slot = d % 3
            if d + 3 < 8:
                pass
            po = C.ps()
            for f in range(NF):
                fs = min(128, DFF - f * 128)
                C.mm(po.ap, wds.ap[0:fs, slot, f, :], actT.ap[0:fs, f, :], f == 0, f == NF - 1,
                     [wds.b[slot + (3 if f == NF - 1 else 0)], actT.b[f]], [po.b])
            C.stt(xs.ap[:, d, :], po.ap, 0.5, xs.ap[:, d, :], ALU.mult, ALU.add, [po.b, xs.b], [xs.b])
            if d + 3 < 8:
                load_wd(d + 3)
            if bgB is not None and bgB_first:
                next(bgB, None)
            bg_done = True
            if bg is not None:
                for _ in range(2 * nstep):
                    if next(bg, "END") == "END":
                        bg = None
                        break
                bg_done = bg is None
            if bgB is not None and bg_done and not bgB_first:
                next(bgB, None)
        if bg is not None:
            for _ in bg:
                pass
        if bgB is not None:
            for _ in bgB:
                pass

    def select_own(out3, src_fn, reads, writes):
        C.ts(out3[:, :, 0:128], src_fn(0), sel.ap[:, 0:1], None, ALU.mult, None, reads + [sel.b], writes)
        C.stt(out3[:, :, 0:128], src_fn(1), sel.ap[:, 1:2], out3[:, :, 0:128], ALU.mult, ALU.add, reads + [sel.b] + writes, writes)
        C.ts(out3[:, :, 128:256], src_fn(3), sel.ap[:, 0:1], None, ALU.mult, None, reads + [sel.b], writes)
        C.stt(out3[:, :, 128:256], src_fn(2), sel.ap[:, 1:2], out3[:, :, 128:256], ALU.mult, ALU.add, reads + [sel.b] + writes, writes)

    mK = C.alloc([128, 4, 256], BF16)
    mV = C.alloc([128, 2, 512], BF16)
    base_sp = C.sp
    C.barrier()
    if True:
        mx = C.alloc([128, 8, 256], F32)
        mh = C.alloc([128, 8, 256], BF16)
        sq_t = C.alloc([128, 8, 512], BF16)
        rstd_t = C.alloc([128, 512], F32)
        wkv = C.alloc([128, 8, 1024], BF16)
        C.dma(mx.ap, memT.rearrange("(k p) t -> p k t", p=128), [], [mx.b])
        C.dma(wkv.ap, W["w_xkv"].rearrange("(k p) f -> p k f", p=128), [], [wkv.b], q="pool")
        rstd_from([(mx.ap[:, k, :], 128, mx.b) for k in range(8)], 256, D, sq_t, rstd_t)
        for k in range(8):
            C.stt(mh.ap[:, k, :], mx.ap[:, k, :], gt["g_mem"].ap[:, k:k + 1], rstd_t.ap[:, 0:256], ALU.mult, ALU.mult,
                  [mx.b, gt["g_mem"].b, rstd_t.b], [mh.b])
        rk = C.alloc([128, 512], F32)
        for h in range(4):
            pk = C.ps()
            for k in range(8):
                C.mm(pk.ap[:, 0:256], wkv.ap[:, k, h * 128:(h + 1) * 128], mh.ap[:, k, :], k == 0, k == 7, [wkv.b, mh.b], [pk.b])
            rstd_from([(pk.ap[:, 0:256], 128, pk.b)], 256, 128, sq_t, rk)
            C.stt(mK.ap[:, h, :], pk.ap[:, 0:256], gt["g_xk"].ap[:, 0:1], rk.ap[:, 0:256], ALU.mult, ALU.mult,
                  [pk.b, gt["g_xk"].b, rk.b], [mK.b])
        for j in range(2):
            pv = C.ps()
            for k in range(8):
                C.mm(pv.ap, mh.ap[:, k, j * 128:(j + 1) * 128], wkv.ap[:, k, 512:1024], k == 0, k == 7, [wkv.b, mh.b], [pv.b])
            C.acopy(mV.ap[:, j, :], pv.ap, [pv.b], [mV.b])

    if stop == "M":
        S.emit()
        return nc
    C.barrier()
    C.sp = base_sp
    xs = C.alloc([128, 8, 512], F32)
    xs_b = C.alloc([128, 8, 512], F32)
    hbf = C.alloc([128, 8, 512], BF16)
    hbf2 = C.alloc([128, 8, 512], BF16)
    sq_t = C.alloc([128, 8, 512], BF16)
    rstd_t = C.alloc([128, 512], F32)
    actT = C.alloc([128, NF, 512], BF16, nb=NF)
    wgs = C.alloc([128, 3, 8, 256], BF16, nb=3)
    wus = C.alloc([128, 3, 8, 256], BF16, nb=3)
    wds = C.alloc([128, 3, NF, 128], BF16, nb=6)
    silt = C.alloc([128, 2, 512], F32, nb=2)
    w_in = C.alloc([128, 8, 1280], BF16)
    w_ukv = C.alloc([128, 2, 1024], BF16)
    x1own = C.alloc([128, 8, 128], F32)
    cqo = C.alloc([128, 3, 256], F32)
    cqn = C.alloc([128, 3, 256], BF16)
    w_uq = C.alloc([128, 3, 1024], BF16)
    w_uq.b.w = None
    C.dma(w_uq.ap, W["w_uq"].rearrange("(k p) f -> p k f", p=128), [], [w_uq.b], q="pool")
    Qn_c = C.alloc([128, 4, 256], BF16)
    Qr_c = C.alloc([64, 4, 256], BF16)
    cosO = C.alloc([64, 256], F32)
    sinO = C.alloc([64, 256], F32)
    ckvn = C.alloc([128, 2, 512], BF16)
    r2 = C.alloc([128, 512], F32)
    cosT = C.alloc([64, 512], F32)
    sinS = C.alloc([64, 512], F32)
    tmpi = C.alloc([64, 512], I32)
    tmpf = C.alloc([64, 512], F32)
    tmpk = C.alloc([64, 512], F32)
    t1 = C.alloc([64, 512], F32)
    t2 = C.alloc([64, 512], F32)
    krr = C.alloc([64, 512], BF16)
    krsq = C.alloc([64, 512], BF16)
    t3 = C.alloc([64, 512], F32)
    t4 = C.alloc([64, 512], F32)
    knb = None
    knsq = C.alloc([128, 4, 512], BF16)
    onehot = C.alloc([128, 4, 4], BF16)
    ub = C.alloc([128, 4, 512], BF16)
    vb = ub
    knb = ub
    rk = C.alloc([128, 4, 4], F32)
    for tl_ in (wgs, wus, wds):
        for b_ in tl_.b:
            b_.w = None
    w_in.b.w = None
    w_ukv.b.w = None
    w_in_v = W["w_in"].rearrange("(k p) f -> p k f", p=128)
    for cc in range(2):
        C.dma(w_in.ap[:, :, cc * 640:(cc + 1) * 640], w_in_v[:, :, cc * 640:(cc + 1) * 640], [], [w_in.b], q="pool")
    C.dma(w_ukv.ap, W["w_ukv"].rearrange("(k p) f -> p k f", p=128), [], [w_ukv.b], q="pool")
    onehot_f = C.alloc([128, 4, 4], F32)
    C.memset(onehot_f.ap, 0.0, [onehot_f.b])
    for h in range(4):
        C.memset(onehot_f.ap[:, h, h:h + 1], 1.0, [onehot_f.b])
    C.cp(onehot.ap, onehot_f.ap, [onehot_f.b], [onehot.b])
    xTv = xT.rearrange("(k p) t -> p k t", p=128)
    if stop == "A0":
        S.emit()
        return nc
    def post_gen(c, xs):
        t0 = c * 512
        for (ob, ba, bb) in ((0, 0, 1), (1, 3, 2)):
            C.ts(x1own.ap, xs.ap[:, :, ba * 128:(ba + 1) * 128], sel.ap[:, 0:1], None, ALU.mult, None, [xs.b, sel.b], [x1own.b])
            C.stt(x1own.ap, xs.ap[:, :, bb * 128:(bb + 1) * 128], sel.ap[:, 1:2], x1own.ap, ALU.mult, ALU.add, [xs.b, sel.b, x1own.b], [x1own.b])
            C.dma(x1D.ap[:, :, c * 256 + ob * 128:c * 256 + (ob + 1) * 128], x1own.ap, [x1own.b], [x1D.b], q="pool")
            yield
        rstd_from([(xs.ap[:, k, :], 128, xs.b) for k in range(8)], 512, D, sq_t, rstd_t)
        yield
        for k in range(8):
            C.stt(hbf2.ap[:, k, :], xs.ap[:, k, :], gt["g_mix"].ap[:, k:k + 1], rstd_t.ap, ALU.mult, ALU.mult,
                  [xs.b, gt["g_mix"].b, rstd_t.b], [hbf2.b])
            yield
        for i in range(3):
            p = C.ps()
            for k in range(8):
                C.mm(p.ap, w_in.ap[:, k, i * 128:(i + 1) * 128], hbf2.ap[:, k, :], k == 0, k == 7, [w_in.b, hbf2.b], [p.b])
            pv3 = p.ap.rearrange("p (a t) -> p a t", a=1)
            select_own(cqo.ap[:, i:i + 1, :], lambda blk: pv3[:, :, blk * 128:(blk + 1) * 128], [p.b], [cqo.b])
            yield
        rstd_from([(cqo.ap[:, i, :], 128, cqo.b) for i in range(3)], 256, 384, sq_t, r2)
        yield
        for i in range(3):
            C.stt(cqn.ap[:, i, :], cqo.ap[:, i, :], gt["g_q"].ap[:, i:i + 1], r2.ap[:, 0:256], ALU.mult, ALU.mult,
                  [cqo.b, gt["g_q"].b, r2.b], [cqn.b])
            yield
        rope_tables(pos_own, c * 256, 256, cosO, sinO, tmpi, tmpf, tmpk)
        yield
        for h in range(4):
            pn = C.ps()
            pr = C.ps()
            prs = C.ps()
            for k in range(3):
                C.mm(pn.ap[:, 0:256], w_uq.ap[:, k, h * 256:h * 256 + 128], cqn.ap[:, k, :], k == 0, k == 2, [w_uq.b, cqn.b], [pn.b])
            for k in range(3):
                C.mm(pr.ap[0:64, 0:256], w_uq.ap[:, k, h * 256 + 128:h * 256 + 192], cqn.ap[:, k, :], k == 0, k == 2, [w_uq.b, cqn.b], [pr.b])
            for k in range(3):
                C.mm(prs.ap[0:64, 0:256], w_uq.ap[:, k, h * 256 + 192:h * 256 + 256], cqn.ap[:, k, :], k == 0, k == 2, [w_uq.b, cqn.b], [prs.b])
            rstd_from([(pn.ap[:, 0:256], 128, pn.b), (pr.ap[0:64, 0:256], 64, pr.b)], 256, 192, sq_t, r2)
            C.stt(Qn_c.ap[:, h, :], pn.ap[:, 0:256], gt["g_qn"].ap[:, 0:1], r2.ap[:, 0:256], ALU.mult, ALU.mult, [pn.b, gt["g_qn"].b, r2.b], [Qn_c.b])
            C.stt(t1.ap[:, 0:256], pr.ap[0:64, 0:256], gt["g_qr"].ap, cosO.ap, ALU.mult, ALU.mult, [pr.b, gt["g_qr"].b, cosO.b, r2.b], [t1.b])
            C.stt(t2.ap[:, 0:256], prs.ap[0:64, 0:256], gt["g_qrs"].ap, sinO.ap, ALU.mult, ALU.mult, [prs.b, gt["g_qrs"].b, sinO.b, r2.b], [t2.b])
            C.tt(t1.ap[:, 0:256], t1.ap[:, 0:256], t2.ap[:, 0:256], ALU.add, [t1.b, t2.b], [t1.b])
            C.tt(Qr_c.ap[:, h, :], t1.ap[:, 0:256], r2.ap[0:64, 0:256], ALU.mult, [t1.b, r2.b], [Qr_c.b])
            yield
        C.dma(QnD.ap[:, :, c * 256:(c + 1) * 256], Qn_c.ap, [Qn_c.b], [QnD.b], q="pool")
        C.dma(QrD.ap[:, :, c * 256:(c + 1) * 256], Qr_c.ap, [Qr_c.b], [QrD.b], q="pool")
        yield
        pkv = [C.ps(), C.ps()]
        for i in range(2):
            for k in range(8):
                C.mm(pkv[i].ap, w_in.ap[:, k, 384 + i * 128:384 + (i + 1) * 128], hbf2.ap[:, k, :], k == 0, k == 7, [w_in.b, hbf2.b], [pkv[i].b])
        rstd_from([(pkv[i].ap, 128, pkv[i].b) for i in range(2)], 512, 256, sq_t, r2)
        for i in range(2):
            C.stt(ckvn.ap[:, i, :], pkv[i].ap, gt["g_kv"].ap[:, i:i + 1], r2.ap, ALU.mult, ALU.mult,
                  [pkv[i].b, gt["g_kv"].b, r2.b], [ckvn.b])
        yield
        rope_tables(pos_all, t0, 512, cosT, sinS, tmpi, tmpf, tmpk)
        yield
        pk1 = C.ps()
        pk2 = C.ps()
        for k in range(8):
            C.mm(pk1.ap[0:64, :], w_in.ap[:, k, 640:704], hbf2.ap[:, k, :], k == 0, k == 7, [w_in.b, hbf2.b], [pk1.b])
        for k in range(8):
            C.mm(pk2.ap[0:64, :], w_in.ap[:, k, 704:768], hbf2.ap[:, k, :], k == 0, k == 7, [w_in.b, hbf2.b], [pk2.b])
        C.acopy(t3.ap, pk1.ap[0:64, :], [pk1.b], [t3.b])
        C.acopy(t4.ap, pk2.ap[0:64, :], [pk2.b], [t4.b])
        yield
        C.act(krsq.ap, t3.ap, AF.Square, [t3.b], [krsq.b])
        yield
        C.stt(t1.ap, t3.ap, gt["g_kr"].ap, cosT.ap, ALU.mult, ALU.mult, [t3.b, gt["g_kr"].b, cosT.b], [t1.b])
        yield
        C.stt(t2.ap, t4.ap, gt["g_krs"].ap, sinS.ap, ALU.mult, ALU.mult, [t4.b, gt["g_krs"].b, sinS.b], [t2.b])
        yield
        C.tt(krr.ap, t1.ap, t2.ap, ALU.add, [t1.b, t2.b], [krr.b])
        yield
        C.dma(krrD.ap[:, t0:t0 + 512], krr.ap, [krr.b], [krrD.b], q="pool")
        yield
        for i in range(4):
            p = C.ps()
            for k in range(8):
                C.mm(p.ap, w_in.ap[:, k, 768 + i * 128:768 + (i + 1) * 128], hbf2.ap[:, k, :], k == 0, k == 7, [w_in.b, hbf2.b], [p.b])
            C.acopy(ub.ap[:, i, :], p.ap, [p.b], [ub.b])
            yield
        C.dma(uD.ap[:, :, t0:t0 + 512], ub.ap, [ub.b], [uD.b], q="pool")
        yield
        for h in range(4):
            p = C.ps()
            for k in range(2):
                C.mm(p.ap, w_ukv.ap[:, k, h * 128:(h + 1) * 128], ckvn.ap[:, k, :], k == 0, k == 1, [w_ukv.b, ckvn.b], [p.b])
            C.act(knsq.ap[:, h, :], p.ap, AF.Square, [p.b], [knsq.b])
            C.ts(knb.ap[:, h, :], p.ap, gt["g_kn"].ap[:, 0:1], None, ALU.mult, None, [p.b, gt["g_kn"].b, knsq.b], [knb.b])
            yield
        C.dma(KnD.ap[:, :, t0:t0 + 512], knb.ap, [knb.b], [KnD.b], q="pool")
        yield
        pq = C.ps()
        for blk in range(4):
            for h in range(4):
                C.mm(pq.ap[:, blk * 4:blk * 4 + 4], knsq.ap[:, h, blk * 128:(blk + 1) * 128], onehot.ap[:, h, :], h == 0, False,
                     [knsq.b, onehot.b], [pq.b])
            C.mm(pq.ap[:, blk * 4:blk * 4 + 4], krsq.ap[:, blk * 128:(blk + 1) * 128], ones_bf.ap[0:64, 0:4], False, True,
                 [krsq.b, ones_bf.b], [pq.b])
        rkf = rk.ap.rearrange("p a b -> p (a b)")
        C.act(rkf, pq.ap[:, 0:16], AF.Sqrt, [pq.b, eps_t.b], [rk.b], bias=eps_t.ap, scale=1.0 / 192.0)
        yield
        C.recip(rkf, rkf, [rk.b], [rk.b])
        yield
        C.ts(rkf, rkf, 192.0 ** -0.5, None, ALU.mult, None, [rk.b], [rk.b])
        yield
        C.dma(rstdkD.ap[:, c * 4:(c + 1) * 4, :], rk.ap, [rk.b], [rstdkD.b], q="pool")
        yield
        for blk in range(4):
            p = C.ps()
            for k in range(2):
                C.mm(p.ap, ckvn.ap[:, k, blk * 128:(blk + 1) * 128], w_ukv.ap[:, k, 512:1024], k == 0, k == 1, [w_ukv.b, ckvn.b], [p.b])
            C.acopy(vb.ap[:, blk, :], p.ap, [p.b], [vb.b])
            yield
        C.dma(VD.ap[:, c * 4:(c + 1) * 4, :], vb.ap, [vb.b], [VD.b], q="pool")
        yield
        yield

    xs_bufs = [xs, xs_b]

    def pre_gen(c):
        xb = xs_bufs[c % 2]
        C.dma(xb.ap, xTv[:, :, c * 512:(c + 1) * 512], [], [xb.b])
        yield
        for _ in ffn_norm_gen(xb, hbf, "g_ffn1", sq_t, rstd_t):
            yield

    POST_STEP = 2
    for _ in pre_gen(0):
        pass
    prev = None
    for c in range(8):
        xs = xs_bufs[c % 2]
        ffn(xs, hbf, W["ffn1_wg"], W["ffn1_wu"], W["ffn1_wd"], "g_ffn1", sq_t, rstd_t, actT, wgs, wus, wds, silt, scr1, c == 0,
            bg=prev, bgB=(pre_gen(c + 1) if c + 1 < 8 else None), do_norm=False, nstep=POST_STEP, bgB_first=True)
        prev = post_gen(c, xs)
    for _ in prev:
        pass


    if stop == "A":
        S.emit()
        return nc
    C.barrier()
    C.sp = base_sp
    WgB2, WuB2, WdB2, WdB2t = scr2
    for (src, dst) in ((W["ffn2_wg"], WgB2), (W["ffn2_wu"], WuB2)):
        sv = src.rearrange("(k p) f -> p k f", p=128)
        for cc in range(4):
            C.dma(dst.ap[:, :, cc * 688:(cc + 1) * 688], sv[:, :, cc * 688:(cc + 1) * 688], [], list(dst.b), q="pool")
    wd2 = W["ffn2_wd"]
    for d in range(8):
        C.dma(WdB2.ap[:, d, 0:21, :], wd2[0:21 * 128, d * 128:(d + 1) * 128].rearrange("(f p) c -> p f c", p=128),
              [], [WdB2.b[d]], q="pool")
    C.dma(WdB2.ap[0:64, :, 21, :], wd2[21 * 128:DFF, :].rearrange("p (d c) -> p d c", c=128), [], list(WdB2t.b), q="pool")
    ssm_stage(C, G, W, gt, sel, uD, ysD, dbg, eps_t, ones_bf, iota_d, maskE_d, rstd_from, kvec_d)

    if stop == "B":
        S.emit()
        return nc
    ymD = dscr("ymD", [128, 4, NOWN], BF16)
    x3D = dscr("x3D", [128, 8, NOWN], F32)
    C.barrier()
    C.sp = base_sp
    Kn = C.alloc([128, 4, T], BF16)
    krA = C.alloc([64, T], BF16)
    Vv = C.alloc([128, 32, 512], BF16)
    rkA = C.alloc([128, 32, 4], F32)
    amask = C.alloc([128, 8, 512], BF16)
    C.dma(Kn.ap, KnD.ap, [KnD.b], [Kn.b])
    C.dma(krA.ap, krrD.ap, [krrD.b], [krA.b])
    C.dma(Vv.ap, VD.ap, [VD.b], [Vv.b])
    C.dma(rkA.ap, rstdkD.ap, [rstdkD.b], [rkA.b])
    C.dma(amask.ap, amask_d, [], [amask.b], q="pool")
    sq_t = C.alloc([128, 8, 512], BF16)
    r2 = C.alloc([128, 512], F32)
    ymn = C.alloc([128, 4, 512], BF16)
    ymla = C.alloc([128, 4, 512], F32)
    cosT = C.alloc([64, 512], F32)
    sinS = C.alloc([64, 512], F32)
    tmpi = C.alloc([64, 512], I32)
    tmpf = C.alloc([64, 512], F32)
    tmpk = C.alloc([64, 512], F32)
    t1 = C.alloc([64, 512], F32)
    t2 = C.alloc([64, 512], F32)
    Qn2 = [C.alloc([128, 4, 512], BF16) for _ in range(2)]
    Qr2 = [C.alloc([64, 4, 512], BF16) for _ in range(2)]
    PT = C.alloc([128, 6, 512], BF16, nb=6)
    rden = C.alloc([128, 512], F32)
    pti = 0
    for m in range(4):
        o0 = m * 512
        Qn = Qn2[m % 2]
        Qr = Qr2[m % 2]
        if m == 0:
            C.dma(Qn.ap, QnD.ap[:, :, 0:512], [QnD.b], [Qn.b])
            C.dma(Qr.ap, QrD.ap[:, :, 0:512], [QrD.b], [Qr.b])
        if m + 1 < 4:
            C.dma(Qn2[(m + 1) % 2].ap, QnD.ap[:, :, o0 + 512:o0 + 1024], [QnD.b], [Qn2[(m + 1) % 2].b])
            C.dma(Qr2[(m + 1) % 2].ap, QrD.ap[:, :, o0 + 512:o0 + 1024], [QrD.b], [Qr2[(m + 1) % 2].b])
        nkb = 8 * m + 8
        C.reserved = {4, 5, 6, 7}
        for h in range(4):
            po = C.psum[4 + 2 * (h % 2)]
            pd = C.psum[5 + 2 * (h % 2)]
            def score(kb):
                pst = C.ps()
                C.mm(pst.ap, Kn.ap[:, h, kb * 128:(kb + 1) * 128], Qn.ap[:, h, :], True, False, [Kn.b, Qn.b], [pst.b])
                C.mm(pst.ap, krA.ap[:, kb * 128:(kb + 1) * 128], Qr.ap[:, h, :], False, True, [krA.b, Qr.b], [pst.b])
                return pst
            LOOK = 2
            pend = [score(kb) for kb in range(min(LOOK, nkb))]
            for kb in range(nkb):
                pst = pend.pop(0)
                if kb + LOOK < nkb:
                    pend.append(score(kb + LOOK))
                sl = pti % 6
                pti += 1
                C.act(PT.ap[:, sl, :], pst.ap, AF.Exp, [pst.b, rkA.b], [PT.b[sl]], scale=rkA.ap[:, kb, h:h + 1])
                if kb >= 8 * m:
                    C.tt(PT.ap[:, sl, :], PT.ap[:, sl, :], amask.ap[:, kb - 8 * m, :], ALU.mult, [PT.b[sl], amask.b], [PT.b[sl]])
                C.mm(po.ap, Vv.ap[:, kb, h * 128:(h + 1) * 128], PT.ap[:, sl, :], kb == 0, kb == nkb - 1, [Vv.b, PT.b[sl]], [po.b])
                C.mm(pd.ap, ones_bf.ap, PT.ap[:, sl, :], kb == 0, kb == nkb - 1, [ones_bf.b, PT.b[sl]], [pd.b])
            C.recip(rden.ap, pd.ap, [pd.b], [rden.b])
            C.tt(ymla.ap[:, h, :], po.ap, rden.ap, ALU.mult, [po.b, rden.b], [ymla.b])
        C.reserved = set()
        if debug:
            C.dma(dbg["d_ymla"].ap.rearrange("(k p) t -> p k t", p=128)[:, :, o0:o0 + 512], ymla.ap, [ymla.b], [dbg["d_ymla"].b])
        rstd_from([(ymla.ap[:, k, :], 128, ymla.b) for k in range(4)], 512, 512, sq_t, r2)
        for k in range(4):
            C.stt(ymn.ap[:, k, :], ymla.ap[:, k, :], gt["g_om"].ap[:, k:k + 1], r2.ap, ALU.mult, ALU.mult,
                  [ymla.b, gt["g_om"].b, r2.b], [ymn.b])
        C.dma(ymD.ap[:, :, o0:o0 + 512], ymn.ap, [ymn.b], [ymD.b])

    if stop == "C1":
        S.emit()
        return nc
    C.barrier()
    C.sp = base_sp
    w_o = C.alloc([128, 8, 1024], BF16)
    w_xq = C.alloc([128, 8, 512], BF16)
    w_xo = C.alloc([128, 4, 1024], BF16)
    C.dma(w_o.ap, W["w_o"].rearrange("(k p) f -> p k f", p=128), [], [w_o.b], q="pool")
    C.dma(w_xq.ap, W["w_xq"].rearrange("(k p) f -> p k f", p=128), [], [w_xq.b], q="pool")
    C.dma(w_xo.ap, W["w_xo"].rearrange("(k p) f -> p k f", p=128), [], [w_xo.b], q="pool")
    xs = C.alloc([128, 8, 512], F32)
    hbf = C.alloc([128, 8, 512], BF16)
    sq_t = C.alloc([128, 8, 512], BF16)
    rstd_t = C.alloc([128, 512], F32)
    r2 = C.alloc([128, 512], F32)
    ycat = C.alloc([128, 8, 512], BF16, nb=2)
    PT = C.alloc([128, 6, 512], BF16, nb=6)
    rden = C.alloc([128, 512], F32)
    qx = C.alloc([128, 4, 512], BF16)
    ox = C.alloc([128, 4, 512], BF16)
    actT = C.alloc([128, NF, 512], BF16, nb=NF)
    wgs = C.alloc([128, 3, 8, 256], BF16, nb=3)
    wus = C.alloc([128, 3, 8, 256], BF16, nb=3)
    wds = C.alloc([128, 3, NF, 128], BF16, nb=6)
    silt = C.alloc([128, 2, 512], F32, nb=2)
    xs_b = C.alloc([128, 8, 512], F32)
    hbfx = C.alloc([128, 8, 512], BF16)
    xs_bufs = [xs, xs_b]
    pti_box = [0]

    def pre2_gen(m):
        o0 = m * 512
        xs = xs_bufs[m % 2]
        C.dma(xs.ap, x1D.ap[:, :, o0:o0 + 512], [x1D.b], [xs.b])
        C.dma(ycat.ap[:, 0:4, :], ymD.ap[:, :, o0:o0 + 512], [ymD.b], [ycat.b[0]])
        C.dma(ycat.ap[:, 4:8, :], ysD.ap[:, :, o0:o0 + 512], [ysD.b], [ycat.b[1]])
        if debug:
            C.dma(dbg["d_x1own"].ap.rearrange("(k p) t -> p k t", p=128)[:, :, o0:o0 + 512], xs.ap, [xs.b], [dbg["d_x1own"].b])
        yield
        for d in range(8):
            p = C.ps()
            for k in range(8):
                C.mm(p.ap, w_o.ap[:, k, d * 128:(d + 1) * 128], ycat.ap[:, k, :], k == 0, k == 7, [w_o.b, ycat.b[k // 4]], [p.b])
            C.tt(xs.ap[:, d, :], p.ap, xs.ap[:, d, :], ALU.add, [p.b, xs.b], [xs.b])
            yield
        if debug:
            C.dma(dbg["d_x2"].ap.rearrange("(k p) t -> p k t", p=128)[:, :, o0:o0 + 512], xs.ap, [xs.b], [dbg["d_x2"].b])
        rstd_from([(xs.ap[:, k, :], 128, xs.b) for k in range(8)], 512, D, sq_t, rstd_t)
        yield
        for k in range(8):
            C.stt(hbfx.ap[:, k, :], xs.ap[:, k, :], gt["g_xa"].ap[:, k:k + 1], rstd_t.ap, ALU.mult, ALU.mult,
                  [xs.b, gt["g_xa"].b, rstd_t.b], [hbfx.b])
            if k % 2 == 1:
                yield
        for h in range(4):
            p = C.ps()
            for k in range(8):
                C.mm(p.ap, w_xq.ap[:, k, h * 128:(h + 1) * 128], hbfx.ap[:, k, :], k == 0, k == 7, [w_xq.b, hbfx.b], [p.b])
            rstd_from([(p.ap, 128, p.b)], 512, 128, sq_t, r2)
            C.stt(qx.ap[:, h, :], p.ap, gt["g_xq"].ap[:, 0:1], r2.ap, ALU.mult, ALU.mult, [p.b, gt["g_xq"].b, r2.b], [qx.b])
            yield
        for h in range(4):
            po = C.ps()
            pd = C.ps()
            for j in range(2):
                pst = C.ps()
                C.mm(pst.ap, mK.ap[:, h, j * 128:(j + 1) * 128], qx.ap[:, h, :], True, True, [mK.b, qx.b], [pst.b])
                sl = pti_box[0] % 6
                pti_box[0] += 1
                C.act(PT.ap[:, sl, :], pst.ap, AF.Exp, [pst.b], [PT.b[sl]], scale=128.0 ** -0.5)
                C.mm(po.ap, mV.ap[:, j, h * 128:(h + 1) * 128], PT.ap[:, sl, :], j == 0, j == 1, [mV.b, PT.b[sl]], [po.b])
                C.mm(pd.ap, ones_bf.ap, PT.ap[:, sl, :], j == 0, j == 1, [ones_bf.b, PT.b[sl]], [pd.b])
            C.recip(rden.ap, pd.ap, [pd.b], [rden.b])
            C.tt(ox.ap[:, h, :], po.ap, rden.ap, ALU.mult, [po.b, rden.b], [ox.b])
            yield
        for d in range(8):
            p = C.ps()
            for k in range(4):
                C.mm(p.ap, w_xo.ap[:, k, d * 128:(d + 1) * 128], ox.ap[:, k, :], k == 0, k == 3, [w_xo.b, ox.b], [p.b])
            C.tt(xs.ap[:, d, :], p.ap, xs.ap[:, d, :], ALU.add, [p.b, xs.b], [xs.b])
            yield
        if debug:
            C.dma(dbg["d_x3"].ap.rearrange("(k p) t -> p k t", p=128)[:, :, o0:o0 + 512], xs.ap, [xs.b], [dbg["d_x3"].b])
        yield

    for _ in pre2_gen(0):
        pass
    for m in range(4):
        o0 = m * 512
        xs = xs_bufs[m % 2]
        ffn(xs, hbf, W["ffn2_wg"], W["ffn2_wu"], W["ffn2_wd"], "g_ffn2", sq_t, rstd_t, actT, wgs, wus, wds, silt, scr2, False,
            bg=(pre2_gen(m + 1) if m < 3 else None),
            bgB=(ffn_norm_gen(xs_bufs[(m + 1) % 2], hbf, "g_ffn2", sq_t, rstd_t) if m < 3 else None),
            do_norm=(m == 0), nstep=2)
        C.dma(outT.rearrange("(k p) t -> p k t", p=128)[:, :, o0:o0 + 512], xs.ap, [xs.b], [Buf()], q="pool")
    S.emit()
    return nc


def ssm_stage(C, G, W, gt, sel, uD, ysD, dbg, eps_t, ones_bf, iota_d, maskE_d, rstd_from, kvec_d):
    def ld(nm, shp):
        t = C.alloc(shp, F32)
        C.dma(t.ap, G[nm], [], [t.b])
        return t

    iota = C.alloc([128, NJ], F32)
    C.dma(iota.ap, iota_d, [], [iota.b])
    maskE = C.alloc([128, 2], F32)
    C.dma(maskE.ap, maskE_d, [], [maskE.b])
    w_glu = C.alloc([128, 4, 512], BF16)
    C.dma(w_glu.ap, W["w_glu"].rearrange("(k p) f -> p k f", p=128), [], [w_glu.b], q="pool")
    L1r = C.alloc([128, 4, L, 128], BF16)
    L1i = C.alloc([128, 4, L, 128], BF16)
    L3r = C.alloc([128, 16, L, 32], BF16)
    L3i = C.alloc([128, 16, L, 32], BF16)
    FIR = C.alloc([128, 4, L, 128], BF16)
    thr = C.alloc([128, 16], F32)
    RL = C.alloc([128, 16], F32)
    maskO = C.alloc([128, 2], F32)
    kvec = C.alloc([128, L + 1], F32)
    C.dma(kvec.ap, kvec_d, [], [kvec.b])
    hpi = C.alloc([128, 1], F32)
    C.memset(hpi.ap, math.pi / 2, [hpi.b])
    sp_T = C.sp

    K1 = L + 1

    def bc_mid(ap2, nb, n):
        return ap2.rearrange("p (o n) -> p o n", o=1).to_broadcast([128, nb, n])

    def bc_last(ap2, nb, n):
        return ap2.rearrange("p (k o) -> p k o", o=1).to_broadcast([128, nb, n])

    def powers_b(lr, li, ldt, n, Pr, Pi, lrdt, lidt, KB):
        dt = C.alloc([128, n], F32)
        C.act(dt.ap, ldt.ap, AF.Exp, [ldt.b], [dt.b])
        C.tt(lrdt.ap, lr.ap, dt.ap, ALU.mult, [lr.b, dt.b], [lrdt.b])
        C.tt(lidt.ap, li.ap, dt.ap, ALU.mult, [li.b, dt.b], [lidt.b])
        a3 = C.alloc([128, KB, n], F32)
        e3 = C.alloc([128, KB, n], F32)
        kf = C.alloc([128, KB, n], F32)
        ki = C.alloc([128, KB, n], I32)
        sn = C.alloc([128, KB, n], F32)
        cs = C.alloc([128, KB, n], F32)
        for k0 in range(0, K1, KB):
            nb = min(KB, K1 - k0)
            kv = bc_last(kvec.ap[:, k0:k0 + nb], nb, n)
            A3, E3, KF, KI, SN, CS = (t.ap[:, 0:nb, :] for t in (a3, e3, kf, ki, sn, cs))
            C.tt(A3, bc_mid(lidt.ap, nb, n), kv, ALU.mult, [lidt.b, kvec.b], [a3.b])
            C.tt(E3, bc_mid(lrdt.ap, nb, n), kv, ALU.mult, [lrdt.b, kvec.b], [e3.b])
            C.act(E3, E3, AF.Exp, [e3.b], [e3.b])
            for (dst, dt_, shift) in ((SN, sn, 0.0), (CS, cs, math.pi / 2)):
                C.ts(KF, A3, shift, 1.0 / TWO_PI, ALU.add, ALU.mult, [a3.b], [kf.b])
                C.cp(KI, KF, [kf.b], [ki.b])
                C.cp(KF, KI, [ki.b], [kf.b])
                C.stt(KF, KF, -TWO_PI, A3, ALU.mult, ALU.add, [kf.b, a3.b], [kf.b])
                C.ts(KF, KF, math.pi - shift, -math.pi - shift, ALU.min, ALU.max, [kf.b], [kf.b])
                if shift:
                    C.act(dst, KF, AF.Sin, [kf.b, hpi.b], [dt_.b], bias=hpi.ap)
                else:
                    C.act(dst, KF, AF.Sin, [kf.b], [dt_.b])
            C.tt(Pr.ap[:, k0:k0 + nb, :], E3, CS, ALU.mult, [e3.b, cs.b], [Pr.b])
            C.tt(Pi.ap[:, k0:k0 + nb, :], E3, SN, ALU.mult, [e3.b, sn.b], [Pi.b])

    def bbar(lr, li, Pr, Pi, n):
        nr = C.alloc([128, n], F32)
        den = C.alloc([128, n], F32)
        tA = C.alloc([128, n], F32)
        cr = C.alloc([128, n], F32)
        ci = C.alloc([128, n], F32)
        C.ts(nr.ap, Pr.ap[:, 1, :], -1.0, None, ALU.add, None, [Pr.b], [nr.b])
        C.tt(den.ap, lr.ap, lr.ap, ALU.mult, [lr.b], [den.b])
        C.tt(tA.ap, li.ap, li.ap, ALU.mult, [li.b], [tA.b])
        C.tt(den.ap, den.ap, tA.ap, ALU.add, [den.b, tA.b], [den.b])
        C.recip(den.ap, den.ap, [den.b], [den.b])
        C.tt(cr.ap, nr.ap, lr.ap, ALU.mult, [nr.b, lr.b], [cr.b])
        C.tt(tA.ap, Pi.ap[:, 1, :], li.ap, ALU.mult, [Pi.b, li.b], [tA.b])
        C.tt(cr.ap, cr.ap, tA.ap, ALU.add, [cr.b, tA.b], [cr.b])
        C.tt(cr.ap, cr.ap, den.ap, ALU.mult, [cr.b, den.b], [cr.b])
        C.tt(ci.ap, Pi.ap[:, 1, :], lr.ap, ALU.mult, [Pi.b, lr.b], [ci.b])
        C.tt(tA.ap, nr.ap, li.ap, ALU.mult, [nr.b, li.b], [tA.b])
        C.tt(ci.ap, ci.ap, tA.ap, ALU.subtract, [ci.b, tA.b], [ci.b])
        C.tt(ci.ap, ci.ap, den.ap, ALU.mult, [ci.b, den.b], [ci.b])
        return cr, ci

    def cmul(out_r, out_i, ar, ai, br, bi, tmp, reads, wr, wi):
        C.tt(out_r, ar, br, ALU.mult, reads, [wr])
        C.tt(tmp.ap, ai, bi, ALU.mult, reads, [tmp.b])
        C.tt(out_r, out_r, tmp.ap, ALU.subtract, [wr, tmp.b], [wr])
        C.tt(out_i, ar, bi, ALU.mult, reads, [wi])
        C.tt(tmp.ap, ai, br, ALU.mult, reads, [tmp.b])
        C.tt(out_i, out_i, tmp.ap, ALU.add, [wi, tmp.b], [wi])

    n_in = 256
    lr_i = ld("are_in", [128, 4, 64]); li_i = ld("aim_in", [128, 4, 64]); ldt_i = ld("ldt_in", [128, 4, 64])
    br_i = ld("bre_in", [128, 4, 64]); bi_i = ld("bim_in", [128, 4, 64])
    f2 = lambda t: Tl(t.ap.rearrange("p a b -> p (a b)"), t.b)
    lr_i2, li_i2, ldt_i2, br_i2, bi_i2 = f2(lr_i), f2(li_i), f2(ldt_i), f2(br_i), f2(bi_i)
    Pr_i = C.alloc([128, K1, n_in], F32)
    Pi_i = C.alloc([128, K1, n_in], F32)
    lrdt_i = C.alloc([128, n_in], F32)
    lidt_i = C.alloc([128, n_in], F32)
    powers_b(lr_i2, li_i2, ldt_i2, n_in, Pr_i, Pi_i, lrdt_i, lidt_i, 4)
    cr, ci = bbar(lr_i2, li_i2, Pr_i, Pi_i, n_in)
    bbr = C.alloc([128, n_in], F32)
    bbi = C.alloc([128, n_in], F32)
    tA = C.alloc([128, n_in], F32)
    cmul(bbr.ap, bbi.ap, cr.ap, ci.ap, br_i2.ap, bi_i2.ap, tA, [cr.b, ci.b, br_i2.b, bi_i2.b], bbr.b, bbi.b)
    HB = 4
    w1r = C.alloc([128, HB, n_in], F32)
    w1i = C.alloc([128, HB, n_in], F32)
    w1t = C.alloc([128, HB, n_in], F32)
    for p0 in range(0, L, HB):
        cmul(w1r.ap, w1i.ap, Pr_i.ap[:, p0:p0 + HB, :], Pi_i.ap[:, p0:p0 + HB, :], bc_mid(bbr.ap, HB, n_in), bc_mid(bbi.ap, HB, n_in),
             w1t, [Pr_i.b, Pi_i.b, bbr.b, bbi.b], w1r.b, w1i.b)
        w1r4 = w1r.ap.rearrange("p k (c q) -> p c k q", c=4)
        w1i4 = w1i.ap.rearrange("p k (c q) -> p c k q", c=4)
        for e in range(2):
            C.ts(L1r.ap[:, :, p0:p0 + HB, e * 64:(e + 1) * 64], w1r4, maskE.ap[:, e:e + 1], None, ALU.mult, None, [w1r.b, maskE.b], [L1r.b])
            C.ts(L1i.ap[:, :, p0:p0 + HB, e * 64:(e + 1) * 64], w1i4, maskE.ap[:, e:e + 1], None, ALU.mult, None, [w1i.b, maskE.b], [L1i.b])

    C.barrier()
    C.sp = sp_T
    lr_o = ld("are_out", [128, 16]); li_o = ld("aim_out", [128, 16]); ldt_o = ld("ldt_out", [128, 16])
    cr_o = ld("cre_out", [128, 16, 16]); ci_o = ld("cim_out", [128, 16, 16])
    br_o = ld("bre_out", [128, 16, 16]); bi_o = ld("bim_out", [128, 16, 16])
    Pr_o = C.alloc([128, K1, 16], F32)
    Pi_o = C.alloc([128, K1, 16], F32)
    lrdt_o = C.alloc([128, 16], F32)
    lidt_o = C.alloc([128, 16], F32)
    powers_b(lr_o, li_o, ldt_o, 16, Pr_o, Pi_o, lrdt_o, lidt_o, K1)
    cro, cio = bbar(lr_o, li_o, Pr_o, Pi_o, 16)
    C.memset(maskO.ap, 0.0, [maskO.b])
    C.memset(maskO.ap[0:64, 0:1], 1.0, [maskO.b])
    C.memset(maskO.ap[64:128, 1:2], 1.0, [maskO.b])
    bc = lambda ap2: ap2.rearrange("p (q o) -> p q o", o=1).to_broadcast([128, 16, 16])
    bbro = C.alloc([128, 16, 16], F32)
    bbio = C.alloc([128, 16, 16], F32)
    tB0 = C.alloc([128, 16, 16], F32)
    cmul(bbro.ap, bbio.ap, bc(cro.ap), bc(cio.ap), br_o.ap, bi_o.ap, tB0, [cro.b, cio.b, br_o.b, bi_o.b], bbro.b, bbio.b)
    Wsr = C.alloc([128, 16, L, 32], BF16)
    Wsi = C.alloc([128, 16, L, 32], BF16)
    Cer = C.alloc([128, 16, 32], BF16)
    Cei = C.alloc([128, 16, 32], BF16)
    for e in range(2):
        C.ts(Cer.ap[:, :, e * 16:(e + 1) * 16], cr_o.ap, maskO.ap[:, e:e + 1], None, ALU.mult, None, [cr_o.b, maskO.b], [Cer.b])
        C.ts(Cei.ap[:, :, e * 16:(e + 1) * 16], ci_o.ap, maskO.ap[:, e:e + 1], -1.0, ALU.mult, ALU.mult, [ci_o.b, maskO.b], [Cei.b])
    KH = 8
    tr = C.alloc([128, KH, 16, 16], F32)
    ti = C.alloc([128, KH, 16, 16], F32)
    tt_ = C.alloc([128, KH, 16, 16], F32)

    def bq(ap3):
        return ap3.rearrange("p k (q o) -> p k q o", o=1).to_broadcast([128, KH, 16, 16])

    def bk(ap3):
        return ap3.rearrange("p (o q) h -> p o q h", o=1).to_broadcast([128, KH, 16, 16])

    for k0 in range(0, L, KH):
        cmul(tr.ap, ti.ap, bk(cr_o.ap), bk(ci_o.ap), bq(Pr_o.ap[:, k0 + 1:k0 + 1 + KH, :]), bq(Pi_o.ap[:, k0 + 1:k0 + 1 + KH, :]),
             tt_, [cr_o.b, ci_o.b, Pr_o.b, Pi_o.b], tr.b, ti.b)
        for e in range(2):
            o3r = L3r.ap[:, :, k0:k0 + KH, e * 16:(e + 1) * 16].rearrange("p q k h -> p k q h")
            o3i = L3i.ap[:, :, k0:k0 + KH, e * 16:(e + 1) * 16].rearrange("p q k h -> p k q h")
            C.ts(o3r, tr.ap, maskO.ap[:, e:e + 1], None, ALU.mult, None, [tr.b, maskO.b], [L3r.b])
            C.ts(o3i, ti.ap, maskO.ap[:, e:e + 1], -1.0, ALU.mult, ALU.mult, [ti.b, maskO.b], [L3i.b])
        cmul(tr.ap, ti.ap, bk(bbro.ap), bk(bbio.ap), bq(Pr_o.ap[:, k0:k0 + KH, :]), bq(Pi_o.ap[:, k0:k0 + KH, :]),
             tt_, [bbro.b, bbio.b, Pr_o.b, Pi_o.b], tr.b, ti.b)
        for e in range(2):
            o3r = Wsr.ap[:, :, k0:k0 + KH, e * 16:(e + 1) * 16].rearrange("p q k h -> p k q h")
            o3i = Wsi.ap[:, :, k0:k0 + KH, e * 16:(e + 1) * 16].rearrange("p q k h -> p k q h")
            C.ts(o3r, tr.ap, maskO.ap[:, e:e + 1], None, ALU.mult, None, [tr.b, maskO.b], [Wsr.b])
            C.ts(o3i, ti.ap, maskO.ap[:, e:e + 1], None, ALU.mult, None, [ti.b, maskO.b], [Wsi.b])

    C.memset(FIR.ap, 0.0, [FIR.b])
    for q in range(16):
        c, q4 = q // 4, q % 4
        p = C.ps()
        for tau in range(L):
            o = p.ap[32 * q4:32 * q4 + 32, tau * 32:(tau + 1) * 32]
            C.mm(o, Wsr.ap[:, q, tau, :], Cer.ap[:, q, :], True, False, [Wsr.b, Cer.b], [p.b], tile_position=(0, 32 * q4))
            C.mm(o, Wsi.ap[:, q, tau, :], Cei.ap[:, q, :], False, True, [Wsi.b, Cei.b], [p.b], tile_position=(0, 32 * q4))
        C.acopy(FIR.ap[32 * q4:32 * q4 + 32, c, :, 32 * q4:32 * q4 + 32],
                p.ap[32 * q4:32 * q4 + 32, :].rearrange("p (t h) -> p t h", h=32), [p.b], [FIR.b])
    tki = C.alloc([128, 16], I32)
    tkf = C.alloc([128, 16], F32)
    C.ts(thr.ap, lidt_o.ap, float(L), None, ALU.mult, None, [lidt_o.b], [thr.b])
    C.ts(tkf.ap, thr.ap, 1.0 / TWO_PI, None, ALU.mult, None, [thr.b], [tkf.b])
    C.cp(tki.ap, tkf.ap, [tkf.b], [tki.b])
    C.cp(tkf.ap, tki.ap, [tki.b], [tkf.b])
    C.stt(thr.ap, tkf.ap, -TWO_PI, thr.ap, ALU.mult, ALU.add, [tkf.b, thr.b], [thr.b])
    C.act(RL.ap, lrdt_o.ap, AF.Exp, [lrdt_o.b], [RL.b], scale=float(L))

    C.barrier()
    C.sp = sp_T
    yown = C.alloc([128, 4, NOWN], F32)
    sp_U = C.sp
    NJ1 = NJ + 2
    un = C.alloc([128, T], BF16)
    ud2 = [C.alloc([128, L, NJ], BF16) for _ in range(2)]
    tab2 = [(C.alloc([128, 4, NJ], F32), C.alloc([128, 4, NJ], F32)) for _ in range(2)]
    TB = 2
    ang = C.alloc([128, TB, NJ], F32)
    akf = C.alloc([128, TB, NJ], F32)
    aki = C.alloc([128, TB, NJ], I32)
    iota3 = iota.ap.rearrange("p (o j) -> p o j", o=1).to_broadcast([128, TB, NJ])
    zr = C.alloc([128, NJ], F32)
    zi = C.alloc([128, NJ], F32)
    za = C.alloc([128, NJ], F32)
    zb = C.alloc([128, NJ], F32)
    Zr = C.alloc([128, NJ], F32)
    Zi = C.alloc([128, NJ], F32)
    X2 = [(C.alloc([128, 4, NJ1], BF16, nb=4), C.alloc([128, 4, NJ1], BF16, nb=4)) for _ in range(2)]
    yall = C.alloc([128, T], F32)
    yall_kj = yall.ap.rearrange("p (j k) -> p k j", k=L)

    def load_c(c):
        ud = ud2[c % 2]
        tabC, tabS = tab2[c % 2]
        C.dma(un.ap, uD.ap[:, c, :], [uD.b], [un.b])
        C.acopy(ud.ap, un.ap.rearrange("p (j k) -> p k j", k=L), [un.b], [ud.b])
        for cb in range(4 // TB):
            q0 = 4 * c + TB * cb
            thr3 = thr.ap[:, q0:q0 + TB].rearrange("p (q o) -> p q o", o=1).to_broadcast([128, TB, NJ])
            for (dst, shift) in ((tabS, 0.0), (tabC, math.pi / 2)):
                C.tt(ang.ap, iota3, thr3, ALU.mult, [iota.b, thr.b], [ang.b])
                if shift:
                    C.ts(ang.ap, ang.ap, shift, None, ALU.add, None, [ang.b], [ang.b])
                C.ts(akf.ap, ang.ap, 1.0 / TWO_PI, None, ALU.mult, None, [ang.b], [akf.b])
                C.cp(aki.ap, akf.ap, [akf.b], [aki.b])
                C.cp(akf.ap, aki.ap, [aki.b], [akf.b])
                C.stt(akf.ap, akf.ap, -TWO_PI, ang.ap, ALU.mult, ALU.add, [akf.b, ang.b], [akf.b])
                C.clamp_pi(akf.ap, akf.b)
                C.act(dst.ap[:, TB * cb:TB * cb + TB, :], akf.ap, AF.Sin, [akf.b], [dst.b])

    def l12(c, q4):
        ud = ud2[c % 2]
        tabC, tabS = tab2[c % 2]
        Xr, Xi = X2[c % 2]
        q = 4 * c + q4
        pr = C.ps()
        pi = C.ps()
        for k in range(L):
            C.mm(pr.ap[:, 0:NJ], L1r.ap[32 * q4:32 * q4 + 32, c, L - 1 - k, :], ud.ap[32 * q4:32 * q4 + 32, k, :], k == 0, k == L - 1,
                 [L1r.b, ud.b], [pr.b], tile_position=(32 * q4, 0))
        for k in range(L):
            C.mm(pi.ap[:, 0:NJ], L1i.ap[32 * q4:32 * q4 + 32, c, L - 1 - k, :], ud.ap[32 * q4:32 * q4 + 32, k, :], k == 0, k == L - 1,
                 [L1i.b, ud.b], [pi.b], tile_position=(32 * q4, 0))
        cosJ = Tl(tabC.ap[:, q4, :], tabC.b)
        sinJ = Tl(tabS.ap[:, q4, :], tabS.b)
        C.tt(zr.ap, pr.ap[:, 0:NJ], cosJ.ap, ALU.mult, [pr.b, cosJ.b], [zr.b])
        C.tt(za.ap, pi.ap[:, 0:NJ], sinJ.ap, ALU.mult, [pi.b, sinJ.b], [za.b])
        C.tt(zr.ap, zr.ap, za.ap, ALU.add, [zr.b, za.b], [zr.b])
        C.tt(zi.ap, pi.ap[:, 0:NJ], cosJ.ap, ALU.mult, [pi.b, cosJ.b], [zi.b])
        C.tt(zb.ap, pr.ap[:, 0:NJ], sinJ.ap, ALU.mult, [pr.b, sinJ.b], [zb.b])
        C.tt(zi.ap, zi.ap, zb.ap, ALU.subtract, [zi.b, zb.b], [zi.b])
        Rb = RL.ap[:, q:q + 1].to_broadcast([128, NJ])
        C.S.add("dve", lambda e, Rb=Rb: e.tensor_tensor_scan(out=Zr.ap, data0=Rb, data1=zr.ap, initial=0.0, op0=ALU.mult, op1=ALU.add),
                reads=[RL.b, zr.b], writes=[Zr.b])
        C.S.add("dve", lambda e, Rb=Rb: e.tensor_tensor_scan(out=Zi.ap, data0=Rb, data1=zi.ap, initial=0.0, op0=ALU.mult, op1=ALU.add),
                reads=[RL.b, zi.b], writes=[Zi.b])
        C.memset(Xr.ap[:, q4, 0:2], 0.0, [Xr.b[q4]])
        C.memset(Xi.ap[:, q4, 0:2], 0.0, [Xi.b[q4]])
        C.tt(za.ap, Zr.ap, cosJ.ap, ALU.mult, [Zr.b, cosJ.b], [za.b])
        C.tt(zb.ap, Zi.ap, sinJ.ap, ALU.mult, [Zi.b, sinJ.b], [zb.b])
        C.tt(Xr.ap[:, q4, 1:NJ + 1], za.ap, zb.ap, ALU.subtract, [za.b, zb.b], [Xr.b[q4]])
        C.tt(za.ap, Zi.ap, cosJ.ap, ALU.mult, [Zi.b, cosJ.b], [za.b])
        C.tt(zb.ap, Zr.ap, sinJ.ap, ALU.mult, [Zr.b, sinJ.b], [zb.b])
        C.tt(Xi.ap[:, q4, 1:NJ + 1], za.ap, zb.ap, ALU.add, [za.b, zb.b], [Xi.b[q4]])

    def l3(c, k):
        ud = ud2[c % 2]
        Xr, Xi = X2[c % 2]
        p = C.ps()
        for q4 in range(4):
            q = 4 * c + q4
            o = p.ap[32 * q4:32 * q4 + 32, 0:NJ]
            tp = (0, 32 * q4)
            C.mm(o, L3r.ap[:, q, k, :], Xr.ap[:, q4, 0:NJ], True, False, [L3r.b, Xr.b[q4]], [p.b], tile_position=tp)
            C.mm(o, L3i.ap[:, q, k, :], Xi.ap[:, q4, 0:NJ], False, False, [L3i.b, Xi.b[q4]], [p.b], tile_position=tp)
            for kp in range(k + 1):
                C.mm(o, FIR.ap[:, c, k - kp, 32 * q4:32 * q4 + 32], ud.ap[:, kp, :], False, kp == k, [FIR.b, ud.b], [p.b], tile_position=tp)
        C.stt(yall_kj[:, k, :], ud.ap[:, k, :], gt["d_fm"].ap[:, c:c + 1], p.ap[:, 0:NJ], ALU.mult, ALU.add,
              [ud.b, gt["d_fm"].b, p.b], [yall.b])

    load_c(0)
    for q4 in range(4):
        l12(0, q4)
    for c in range(4):
        if c + 1 < 4:
            load_c(c + 1)
        for i in range(4):
            if c + 1 < 4:
                l12(c + 1, i)
            for k in range(4 * i, 4 * i + 4):
                l3(c, k)
        ya4 = yall.ap.rearrange("p (ch b t) -> p ch b t", b=4, t=128)
        yo4 = yown.ap[:, c, :].rearrange("p (ch b t) -> p ch b t", b=2, t=128)
        for (ob, ba, bb) in ((0, 0, 1), (1, 3, 2)):
            C.ts(yo4[:, :, ob, :], ya4[:, :, ba, :], sel.ap[:, 0:1], None, ALU.mult, None, [yall.b, sel.b], [yown.b])
            C.stt(yo4[:, :, ob, :], ya4[:, :, bb, :], sel.ap[:, 1:2], yo4[:, :, ob, :], ALU.mult, ALU.add, [yall.b, sel.b, yown.b], [yown.b])
    C.barrier()
    C.sp = sp_U
    gg = C.alloc([128, 4, 512], F32)
    gb = C.alloc([128, 4, 512], BF16)
    ta = C.alloc([128, 512], F32)
    tb = C.alloc([128, 512], F32)
    sq_t = C.alloc([128, 8, 512], BF16)
    r2 = C.alloc([128, 512], F32)
    yn = C.alloc([128, 4, 512], BF16)
    CG = 2.0 * math.sqrt(2.0 / math.pi)
    for m in range(4):
        o0 = m * 512
        for k in range(4):
            y = yown.ap[:, k, o0:o0 + 512]
            C.tt(ta.ap, y, y, ALU.mult, [yown.b], [ta.b])
            C.ts(ta.ap, ta.ap, 0.044715, 1.0, ALU.mult, ALU.add, [ta.b], [ta.b])
            C.tt(ta.ap, ta.ap, y, ALU.mult, [ta.b, yown.b], [ta.b])
            C.act(tb.ap, ta.ap, AF.Sigmoid, [ta.b], [tb.b], scale=CG)
            C.tt(gg.ap[:, k, :], tb.ap, y, ALU.mult, [tb.b, yown.b], [gg.b])
            C.cp(gb.ap[:, k, :], gg.ap[:, k, :], [gg.b], [gb.b])
        for d in range(4):
            p = C.ps()
            for k in range(4):
                C.mm(p.ap, w_glu.ap[:, k, d * 128:(d + 1) * 128], gb.ap[:, k, :], k == 0, k == 3, [w_glu.b, gb.b], [p.b])
            C.act(tb.ap, p.ap, AF.Sigmoid, [p.b, gt["b_glu"].b], [tb.b], bias=gt["b_glu"].ap[:, d:d + 1], scale=1.0)
            C.tt(gg.ap[:, d, :], gg.ap[:, d, :], tb.ap, ALU.mult, [gg.b, tb.b], [gg.b])
        if dbg:
            C.dma(dbg["d_yssm"].ap.rearrange("(k p) t -> p k t", p=128)[:, :, o0:o0 + 512], gg.ap, [gg.b], [dbg["d_yssm"].b])
        rstd_from([(gg.ap[:, k, :], 128, gg.b) for k in range(4)], 512, 512, sq_t, r2)
        for k in range(4):
            C.stt(yn.ap[:, k, :], gg.ap[:, k, :], gt["g_os"].ap[:, k:k + 1], r2.ap, ALU.mult, ALU.mult,
                  [gg.b, gt["g_os"].b, r2.b], [yn.b])
        C.dma(ysD.ap[:, :, o0:o0 + 512], yn.ap, [yn.b], [ysD.b])


def own_blocks(j):
    out = []
    for c in range(8):
        out += [4 * c + (0 if j == 0 else 1), 4 * c + (3 if j == 0 else 2)]
    return out


def make_in_maps(inp):
    f32 = np.float32
    A = lambda a: np.ascontiguousarray(a)
    fm = lambda g: A(np.asarray(g, f32).reshape(-1, 128).T)
    col = lambda g: A(np.asarray(g, f32).reshape(-1, 1))
    sw = np.concatenate([np.arange(32, 64), np.arange(0, 32)])
    w_in = np.asarray(inp["w_in"][0], f32)
    w_in2 = A(np.concatenate([w_in[:, 0:640], w_in[:, 640:704], w_in[:, 640:704][:, sw], w_in[:, 704:1216]], axis=1))
    w_uq = np.asarray(inp["mla_w_uq"][0], f32).reshape(384, 4, 192)
    w_uq2 = A(np.concatenate([w_uq[:, :, 0:128], w_uq[:, :, 128:192], w_uq[:, :, 128:192][:, :, sw]], axis=2).reshape(384, 1024))
    w_ukv = np.asarray(inp["mla_w_ukv"][0], f32).reshape(256, 4, 256)
    w_ukv2 = A(np.concatenate([w_ukv[:, :, 0:128].reshape(256, 512), w_ukv[:, :, 128:256].reshape(256, 512)], axis=1))
    gq = np.asarray(inp["mla_qk_norm_q"][0], f32)
    gk = np.asarray(inp["mla_qk_norm_k"][0], f32)
    a_re = np.asarray(inp["ssm_a_re"][0], f32); a_im = np.asarray(inp["ssm_a_im"][0], f32)
    ldt = np.asarray(inp["ssm_log_dt"][0], f32)
    b_re = np.asarray(inp["ssm_b_re"][0], f32); b_im = np.asarray(inp["ssm_b_im"][0], f32)
    c_re = np.asarray(inp["ssm_c_re"][0], f32); c_im = np.asarray(inp["ssm_c_im"][0], f32)

    def in_side_gp(a):
        v = a.reshape(4, 4, 2, 64)
        v = np.transpose(v, (1, 2, 0, 3))
        return A(np.broadcast_to(v[:, :, None], (4, 2, 16, 4, 64)).reshape(128, 4, 64))

    def in_side_b(b):
        v = b.reshape(4, 4, 2, 64, 16)
        return A(np.transpose(v, (1, 2, 4, 0, 3)).reshape(128, 4, 64))

    def out_side_gp(a):
        v = a.reshape(16, 2, 64)
        return A(np.transpose(v, (1, 2, 0)).reshape(128, 16))

    def out_side_c(cc):
        v = cc.reshape(16, 2, 16, 64)
        return A(np.transpose(v, (1, 3, 0, 2)).reshape(128, 16, 16))

    def out_side_b(b):
        v = b.reshape(16, 2, 64, 16)
        return A(np.transpose(v, (1, 2, 0, 3)).reshape(128, 16, 16))

    ldt_gp = np.broadcast_to(ldt[:, None], (32, 64))
    common = {
        "ffn1_wg": A(inp["ffn1_w_gate"][0]), "ffn1_wu": A(inp["ffn1_w_up"][0]), "ffn1_wd": A(inp["ffn1_w_down"][0]),
        "ffn2_wg": A(inp["ffn2_w_gate"][0]), "ffn2_wu": A(inp["ffn2_w_up"][0]), "ffn2_wd": A(inp["ffn2_w_down"][0]),
        "w_in": w_in2, "w_uq": w_uq2, "w_ukv": w_ukv2,
        "w_glu": A(inp["ssm_w_glu"][0]), "w_o": A(inp["w_o"][0]), "w_xq": A(inp["xattn_w_q"][0]),
        "w_xkv": A(inp["xattn_w_kv"][0]), "w_xo": A(inp["xattn_w_o"][0]),
        "g_ffn1": fm(inp["ffn1_norm"][0]), "g_mix": fm(inp["mix_norm"][0]), "g_q": fm(inp["mla_q_norm"][0]),
        "g_kv": fm(inp["mla_kv_norm"][0]),
        "g_qn": col(gq[0:128]), "g_qr": col(gq[128:192]), "g_qrs": col(gq[128:192][sw]),
        "g_kn": col(gk[0:128]), "g_kr": col(gk[128:192]), "g_krs": col(gk[128:192][sw]),
        "g_om": fm(inp["out_norm_mla"][0]), "g_os": fm(inp["out_norm_ssm"][0]), "g_xa": fm(inp["xattn_norm"][0]),
        "g_mem": fm(inp["mem_norm"][0]), "g_xq": col(inp["xattn_q_norm"][0]), "g_xk": col(inp["xattn_k_norm"][0]),
        "g_ffn2": fm(inp["ffn2_norm"][0]), "b_glu": fm(inp["ssm_b_glu"][0]), "d_fm": fm(np.asarray(inp["ssm_d"][0], f32).reshape(-1)),
        "are_in": in_side_gp(a_re), "aim_in": in_side_gp(a_im), "ldt_in": in_side_gp(ldt_gp),
        "bre_in": in_side_b(b_re), "bim_in": in_side_b(b_im),
        "are_out": out_side_gp(a_re), "aim_out": out_side_gp(a_im), "ldt_out": out_side_gp(ldt_gp),
        "cre_out": out_side_c(c_re), "cim_out": out_side_c(c_im),
        "bre_out": out_side_b(b_re), "bim_out": out_side_b(b_im),
    }
    d = np.arange(64)
    invf = (10000.0 ** (-(d % 32).astype(np.float64) / 32.0)).astype(f32).reshape(64, 1)
    invf = (np.float32(10000.0) ** (-(np.arange(32, dtype=f32)) / np.float32(32))).astype(f32)
    invf = A(np.concatenate([invf, invf]).reshape(64, 1))
    sgn = A(np.where(d < 32, -1.0, 1.0).astype(f32).reshape(64, 1))
    iota = A(np.broadcast_to(np.arange(1, NJ + 1, dtype=f32)[None, :], (128, NJ)))
    r = np.arange(128)
    ee = (r // 16) % 2
    maskE = A(np.stack([(ee == 0), (ee == 1)], axis=1).astype(f32))
    kvec = A(np.broadcast_to(np.arange(0, L + 1, dtype=f32)[None, :], (128, L + 1)))
    common.update({"invf": invf, "sgn": sgn, "iota": iota, "maskE": maskE, "kvec": kvec})
    x = np.asarray(inp["x"], f32)
    mem = np.asarray(inp["mem"], f32)
    pos = np.asarray(inp["positions"]).astype(np.int32)
    maps = []
    for core in range(8):
        b, j = core // 2, core % 2
        ob = own_blocks(j)
        own_tok = np.concatenate([np.arange(g * 128, (g + 1) * 128) for g in ob])
        qg = np.array(ob[0:4])
        qpos = (qg[:, None] * 128 + np.arange(128)[None, :]).reshape(-1)
        am = np.zeros((128, 8, 512), f32)
        for a in range(8):
            kpos = a * 128 + np.arange(128)
            am[:, a, :] = (kpos[:, None] <= qpos[None, :]).astype(f32)
        m = dict(common)
        m.update({
            "xT": A(x[b].T), "memT": A(mem[b].T),
            "pos_all": A(np.broadcast_to(pos[b][None, :], (64, T))),
            "pos_own": A(np.broadcast_to(pos[b][own_tok][None, :], (64, NOWN))),
            "sel": A(np.broadcast_to(np.array([1.0, 0.0] if j == 0 else [0.0, 1.0], f32)[None, :], (128, 2))),
            "amask": am,
        })
        maps.append(m)
    return maps


_NC_CACHE = {}


def kernel(**inputs):
    if "nc" not in _NC_CACHE:
        _NC_CACHE["nc"] = build(False)
    nc = _NC_CACHE["nc"]
    maps = make_in_maps(inputs)
    res = run_bass_kernel_spmd(nc, maps, core_ids=list(range(8)))
    out = np.zeros((4, T, D), np.float32)
    for core in range(8):
        b, j = core // 2, core % 2
        ob = own_blocks(j)
        o = np.asarray(res.results[core]["outT"], np.float32)
        for i, g in enumerate(ob):
            out[b, g * 128:(g + 1) * 128, :] = o[:, i * 128:(i + 1) * 128].T
    return out
```

```python
import math
import contextlib
import numpy as np
import ml_dtypes
import concourse.bass as bass
import concourse.mybir as mybir
from concourse.bass_utils import run_bass_kernel_spmd

F32 = mybir.dt.float32
BF16 = mybir.dt.bfloat16
I32 = mybir.dt.int32
U8 = mybir.dt.uint8
ALU = mybir.AluOpType
AF = mybir.ActivationFunctionType

D = 1024
T = 4096
NOWN = 2048
DFF = 2752
NF = 22
EPS = 1e-6
L = 16
NJ = T // L
TWO_PI = 2.0 * math.pi


class Buf:
    __slots__ = ("w", "r")

    def __init__(self, w=None):
        self.w = w
        self.r = []


class Op:
    __slots__ = ("idx", "eng", "fn", "deps", "dma", "slot", "ticket", "inc", "waits")

    def __init__(self, idx, eng, fn, dma):
        self.idx = idx
        self.eng = eng
        self.fn = fn
        self.deps = {}
        self.dma = dma
        self.slot = None
        self.ticket = None
        self.inc = False
        self.waits = []


class Sched:
    NSLOT = 12

    def __init__(self, nc):
        self.nc = nc
        self.ops = []
        self.slot_ctr = {}
        self.bar = None
        self.touched = set()

    def add(self, eng, fn, reads=(), writes=(), dma=False):
        op = Op(len(self.ops), eng, fn, dma)
        for b in reads:
            if b.w is not None:
                op.deps[b.w] = "raw"
        for b in writes:
            if b.w is not None and b.w not in op.deps:
                op.deps[b.w] = "waw"
            for r in b.r:
                if r not in op.deps:
                    op.deps[r] = "war"
        for b in reads:
            b.r.append(op.idx)
            self.touched.add(b)
        for b in writes:
            b.w = op.idx
            b.r = []
            self.touched.add(b)
        if dma:
            c = self.slot_ctr.get(eng, 0)
            op.slot = (eng, c % self.NSLOT)
            self.slot_ctr[eng] = c + 1
        self.ops.append(op)
        return op

    def emit(self):
        nc = self.nc
        ops = self.ops
        for op in ops:
            best = {}
            for p, kind in op.deps.items():
                P = ops[p]
                if P.dma:
                    key = ("dma",) + P.slot
                else:
                    if P.eng == op.eng and not op.dma:
                        if P.eng == "pe":
                            continue
                    key = ("eng", P.eng)
                if key not in best or best[key] < p:
                    best[key] = p
            op.waits = sorted(best.items(), key=lambda kv: kv[1])
            for _, p in op.waits:
                ops[p].inc = True
        cnt = {}
        for op in ops:
            if op.dma:
                key = ("dma",) + op.slot
                cnt[key] = cnt.get(key, 0) + 1
                op.ticket = 16 * cnt[key]
            elif op.inc:
                key = ("eng", op.eng)
                cnt[key] = cnt.get(key, 0) + 1
                op.ticket = cnt[key]
        keys = sorted(set(cnt.keys()), key=str)
        sems = {}
        with contextlib.ExitStack() as es:
            for k in keys:
                sems[k] = es.enter_context(nc.semaphore("s_" + "_".join(str(x) for x in k)))
            block = es.enter_context(nc.Block())
            by_eng = {}
            for op in ops:
                by_eng.setdefault(op.eng, []).append(op)
            engmap = {"pe": "tensor", "act": "scalar", "dve": "vector", "pool": "gpsimd", "sp": "sync"}

            def make(elist):
                def body(e):
                    seen = {}
                    for op in elist:
                        for key, p in op.waits:
                            v = ops[p].ticket
                            if seen.get(key, 0) < v:
                                e.wait_ge(sems[key], v)
                                seen[key] = v
                        if op.dma:
                            key = ("dma",) + op.slot
                            prev = op.ticket - 16
                            if prev > 0 and seen.get(key, 0) < prev:
                                e.wait_ge(sems[key], prev)
                                seen[key] = prev
                            op.fn(e).then_inc(sems[key], 16)
                        else:
                            ins = op.fn(e)
                            if op.inc:
                                ins.then_inc(sems[("eng", op.eng)], 1)
                    for op in elist:
                        if op.dma:
                            key = ("dma",) + op.slot
                            if seen.get(key, 0) < op.ticket:
                                e.wait_ge(sems[key], op.ticket)
                                seen[key] = op.ticket
                return body

            for engname, elist in by_eng.items():
                getattr(block, engmap[engname])(make(elist))


class Tl:
    def __init__(self, ap, b):
        self.ap = ap
        self.b = b

    def __getitem__(self, k):
        return self.ap[k]


def _dtsize(dt):
    return {F32: 4, BF16: 2, I32: 4, U8: 1}[dt]


class Ctx:
    ARENA = 212480

    def __init__(self, nc):
        self.nc = nc
        self.S = Sched(nc)
        self.arena = nc.alloc_sbuf_tensor("arena", [128, self.ARENA], U8)
        self.sp = 0
        self.barrier_idx = None
        self.psum = [Tl(nc.alloc_psum_tensor("pb%d" % i, [128, 512], F32)[:, :], Buf()) for i in range(8)]
        self.pi = 0
        self.reserved = set()

    def alloc(self, shape, dt, nb=1):
        n = 1
        for s in shape[1:]:
            n *= s
        nbytes = (n * _dtsize(dt) + 63) // 64 * 64
        assert self.sp + nbytes <= self.ARENA, ("SBUF overflow", self.sp, nbytes)
        ap = self.arena[:, self.sp:self.sp + n * _dtsize(dt)].bitcast(dt)
        self.sp += nbytes
        if len(shape) == 3:
            ap = ap.rearrange("p (a b) -> p a b", b=shape[2])
        elif len(shape) == 4:
            ap = ap.rearrange("p (a b c) -> p a b c", b=shape[2], c=shape[3])
        if shape[0] < 128:
            ap = ap[0:shape[0]]
        if nb == 1:
            return Tl(ap, Buf(self.barrier_idx))
        return Tl(ap, [Buf(self.barrier_idx) for _ in range(nb)])

    def ps(self):
        while (self.pi % 8) in self.reserved:
            self.pi += 1
        t = self.psum[self.pi % 8]
        self.pi += 1
        return t

    def barrier(self):
        S = self.S
        bufs = list(S.touched)
        dummy = self._dummy
        op = S.add("dve", lambda e: e.memset(dummy.ap, 0.0), reads=bufs, writes=bufs + [dummy.b])
        self.barrier_idx = op.idx
        S.touched = set()
        for t in self.psum:
            t.b.w = op.idx
            t.b.r = []
        return op.idx

    def mm(self, out, lhsT, rhs, start, stop, reads, writes, **kw):
        self.S.add("pe", lambda e: e.matmul(out, lhsT=lhsT, rhs=rhs, start=start, stop=stop, **kw),
                   reads=reads, writes=writes)

    def act(self, out, in_, func, reads, writes, bias=None, scale=None):
        kw = {}
        if bias is not None:
            kw["bias"] = bias
        if scale is not None:
            kw["scale"] = scale
        self.S.add("act", lambda e: e.activation(out=out, in_=in_, func=func, **kw), reads=reads, writes=writes)

    def ts(self, out, in0, s1, s2, op0, op1, reads, writes, eng="dve"):
        if op1 is None:
            self.S.add(eng, lambda e: e.tensor_scalar(out=out, in0=in0, scalar1=s1, scalar2=None, op0=op0),
                       reads=reads, writes=writes)
        else:
            self.S.add(eng, lambda e: e.tensor_scalar(out=out, in0=in0, scalar1=s1, scalar2=s2, op0=op0, op1=op1),
                       reads=reads, writes=writes)

    def stt(self, out, in0, scalar, in1, op0, op1, reads, writes, eng="dve"):
        self.S.add(eng, lambda e: e.scalar_tensor_tensor(out=out, in0=in0, scalar=scalar, in1=in1, op0=op0, op1=op1),
                   reads=reads, writes=writes)

    def tt(self, out, in0, in1, op, reads, writes, eng="dve"):
        self.S.add(eng, lambda e: e.tensor_tensor(out=out, in0=in0, in1=in1, op=op), reads=reads, writes=writes)

    def cp(self, out, in_, reads, writes, eng="dve"):
        self.S.add(eng, lambda e: e.tensor_copy(out=out, in_=in_), reads=reads, writes=writes)

    def memset(self, out, val, writes, eng="dve"):
        self.S.add(eng, lambda e: e.memset(out, val), writes=writes)

    def acopy(self, out, in_, reads, writes):
        self.S.add("act", lambda e: e.copy(out=out, in_=in_), reads=reads, writes=writes)

    def clamp_pi(self, ap, b):
        self.ts(ap, ap, math.pi, -math.pi, ALU.min, ALU.max, [b], [b])

    def recip(self, out, in_, reads, writes):
        self.S.add("dve", lambda e: e.reciprocal(out=out, in_=in_), reads=reads, writes=writes)

    def dma(self, out, in_, reads, writes, q="sp"):
        self.S.add(q, lambda e: e.dma_start(out=out, in_=in_), reads=reads, writes=writes, dma=True)


def _bl(x):
    return x if isinstance(x, (list, tuple)) else [x]


def build(debug=False, stop=None):
    nc = bass.Bass("TRN2", target_bir_lowering=False)
    C = Ctx(nc)
    S = C.S

    def din(name, shape, dt=F32):
        return nc.dram_tensor(name, list(shape), dt, kind="ExternalInput").ap()

    def dscr(name, shape, dt):
        return Tl(nc.dram_tensor(name, list(shape), dt, kind="Internal").ap(), Buf())

    xT = din("xT", [D, T])
    memT = din("memT", [D, 256])
    pos_all = din("pos_all", [64, T], I32)
    pos_own = din("pos_own", [64, NOWN], I32)
    sel_d = din("sel", [128, 2])
    amask_d = din("amask", [128, 8, 512])
    invf_d = din("invf", [64, 1])
    sgn_d = din("sgn", [64, 1])
    iota_d = din("iota", [128, NJ])
    maskE_d = din("maskE", [128, 2])
    kvec_d = din("kvec", [128, L + 1])
    W = {}
    for nm, shp in [("ffn1_wg", [D, DFF]), ("ffn1_wu", [D, DFF]), ("ffn1_wd", [DFF, D]),
                    ("ffn2_wg", [D, DFF]), ("ffn2_wu", [D, DFF]), ("ffn2_wd", [DFF, D]),
                    ("w_in", [D, 1280]), ("w_uq", [384, 1024]), ("w_ukv", [256, 1024]),
                    ("w_glu", [512, 512]), ("w_o", [D, D]), ("w_xq", [D, 512]), ("w_xkv", [D, 1024]),
                    ("w_xo", [512, D])]:
        W[nm] = din(nm, shp)
    G = {}
    for nm, shp in [("g_ffn1", [128, 8]), ("g_mix", [128, 8]), ("g_q", [128, 3]), ("g_kv", [128, 2]),
                    ("g_qn", [128, 1]), ("g_qr", [64, 1]), ("g_qrs", [64, 1]),
                    ("g_kn", [128, 1]), ("g_kr", [64, 1]), ("g_krs", [64, 1]),
                    ("g_om", [128, 4]), ("g_os", [128, 4]), ("g_xa", [128, 8]), ("g_mem", [128, 8]),
                    ("g_xq", [128, 1]), ("g_xk", [128, 1]), ("g_ffn2", [128, 8]),
                    ("b_glu", [128, 4]), ("d_fm", [128, 4]),
                    ("are_in", [128, 4, 64]), ("aim_in", [128, 4, 64]), ("ldt_in", [128, 4, 64]),
                    ("bre_in", [128, 4, 64]), ("bim_in", [128, 4, 64]),
                    ("are_out", [128, 16]), ("aim_out", [128, 16]), ("ldt_out", [128, 16]),
                    ("cre_out", [128, 16, 16]), ("cim_out", [128, 16, 16]),
                    ("bre_out", [128, 16, 16]), ("bim_out", [128, 16, 16])]:
        G[nm] = din(nm, shp)
    outT = nc.dram_tensor("outT", [D, NOWN], F32, kind="ExternalOutput").ap()
    dbg = {}
    if debug:
        for nm, shp in [("d_x1own", [D, NOWN]), ("d_yssm", [512, NOWN]), ("d_ymla", [512, NOWN]), ("d_x2", [D, NOWN]),
                        ("d_x3", [D, NOWN])]:
            dbg[nm] = Tl(nc.dram_tensor(nm, shp, F32, kind="ExternalOutput").ap(), Buf())

    def wscr(tag):
        a = Tl(nc.dram_tensor("WgB" + tag, [128, 8, DFF], BF16, kind="Internal").ap(), [Buf() for _ in range(11)])
        b = Tl(nc.dram_tensor("WuB" + tag, [128, 8, DFF], BF16, kind="Internal").ap(), [Buf() for _ in range(11)])
        c = Tl(nc.dram_tensor("WdB" + tag, [128, 8, NF, 128], BF16, kind="Internal").ap(), [Buf() for _ in range(8)])
        ct = Tl(c.ap, [Buf() for _ in range(8)])
        return (a, b, c, ct)
    scr1 = wscr("1")
    scr2 = wscr("2")
    KnD = dscr("KnD", [128, 4, T], BF16)
    krrD = dscr("krrD", [64, T], BF16)
    VD = dscr("VD", [128, 32, 512], BF16)
    rstdkD = dscr("rstdkD", [128, 32, 4], F32)
    uD = dscr("uD", [128, 4, T], BF16)
    QnD = dscr("QnD", [128, 4, NOWN], BF16)
    QrD = dscr("QrD", [64, 4, NOWN], BF16)
    x1D = dscr("x1D", [128, 8, NOWN], F32)
    ysD = dscr("ysD", [128, 4, NOWN], BF16)

    C._dummy = C.alloc([128, 8], F32)
    ones_bf = C.alloc([128, 128], BF16)
    eps_t = C.alloc([128, 1], F32)
    sel = C.alloc([128, 2], F32)
    gt = {}
    for nm in ["g_ffn1", "g_mix", "g_q", "g_kv", "g_qn", "g_qr", "g_qrs", "g_kn", "g_kr", "g_krs", "g_om", "g_os",
               "g_xa", "g_mem", "g_xq", "g_xk", "g_ffn2", "b_glu", "d_fm"]:
        shp = [int(s) for s in G[nm].shape]
        gt[nm] = C.alloc(shp, F32)
        C.dma(gt[nm].ap, G[nm], [], [gt[nm].b])
    invf = C.alloc([64, 1], F32)
    sgn = C.alloc([64, 1], F32)
    C.dma(invf.ap, invf_d, [], [invf.b])
    C.dma(sgn.ap, sgn_d, [], [sgn.b])
    C.dma(sel.ap, sel_d, [], [sel.b])
    C.memset(ones_bf.ap, 1.0, [ones_bf.b])
    C.memset(eps_t.ap, EPS, [eps_t.b])
    base_sp = C.sp

    def rstd_from(srcs, N, Dn, sq_t, rstd_t):
        ps = C.ps()
        n = len(srcs)
        for i, (ap, K, bufs) in enumerate(srcs):
            C.act(sq_t.ap[0:K, i, 0:N], ap, AF.Square, _bl(bufs), [sq_t.b])
        for i, (ap, K, bufs) in enumerate(srcs):
            C.mm(ps.ap[:, 0:N], ones_bf.ap[0:K, :], sq_t.ap[0:K, i, 0:N], i == 0, i == n - 1, [ones_bf.b, sq_t.b], [ps.b])
        C.act(rstd_t.ap[:, 0:N], ps.ap[:, 0:N], AF.Sqrt, [ps.b, eps_t.b], [rstd_t.b], bias=eps_t.ap, scale=1.0 / Dn)
        C.recip(rstd_t.ap[:, 0:N], rstd_t.ap[:, 0:N], [rstd_t.b], [rstd_t.b])

    def rope_tables(pos_ap_dram, col0, N, cosT, sinS, tmpi, tmpf, tmpk):
        C.dma(tmpi.ap[:, 0:N], pos_ap_dram[:, col0:col0 + N], [], [tmpi.b])
        C.cp(tmpf.ap[:, 0:N], tmpi.ap[:, 0:N], [tmpi.b], [tmpf.b])
        C.ts(tmpf.ap[:, 0:N], tmpf.ap[:, 0:N], invf.ap, None, ALU.mult, None, [tmpf.b, invf.b], [tmpf.b])
        C.ts(tmpk.ap[:, 0:N], tmpf.ap[:, 0:N], 1.0 / TWO_PI, None, ALU.mult, None, [tmpf.b], [tmpk.b])
        C.cp(tmpi.ap[:, 0:N], tmpk.ap[:, 0:N], [tmpk.b], [tmpi.b])
        C.cp(tmpk.ap[:, 0:N], tmpi.ap[:, 0:N], [tmpi.b], [tmpk.b])
        C.stt(tmpk.ap[:, 0:N], tmpk.ap[:, 0:N], -TWO_PI, tmpf.ap[:, 0:N], ALU.mult, ALU.add, [tmpk.b, tmpf.b], [tmpk.b])
        C.clamp_pi(tmpk.ap[:, 0:N], tmpk.b)
        C.act(sinS.ap[:, 0:N], tmpk.ap[:, 0:N], AF.Sin, [tmpk.b], [sinS.b])
        C.ts(sinS.ap[:, 0:N], sinS.ap[:, 0:N], sgn.ap, None, ALU.mult, None, [sinS.b, sgn.b], [sinS.b])
        C.ts(tmpf.ap[:, 0:N], tmpf.ap[:, 0:N], math.pi / 2, None, ALU.add, None, [tmpf.b], [tmpf.b])
        C.ts(tmpk.ap[:, 0:N], tmpf.ap[:, 0:N], 1.0 / TWO_PI, None, ALU.mult, None, [tmpf.b], [tmpk.b])
        C.cp(tmpi.ap[:, 0:N], tmpk.ap[:, 0:N], [tmpk.b], [tmpi.b])
        C.cp(tmpk.ap[:, 0:N], tmpi.ap[:, 0:N], [tmpi.b], [tmpk.b])
        C.stt(tmpk.ap[:, 0:N], tmpk.ap[:, 0:N], -TWO_PI, tmpf.ap[:, 0:N], ALU.mult, ALU.add, [tmpk.b, tmpf.b], [tmpk.b])
        C.clamp_pi(tmpk.ap[:, 0:N], tmpk.b)
        C.act(cosT.ap[:, 0:N], tmpk.ap[:, 0:N], AF.Sin, [tmpk.b], [cosT.b])

    def ffn_norm_gen(xs, hbf, gname, sq_t, rstd_t):
        N = 512
        rstd_from([(xs.ap[:, k, :], 128, xs.b) for k in range(8)], N, D, sq_t, rstd_t)
        yield
        for k in range(8):
            C.stt(hbf.ap[:, k, :], xs.ap[:, k, :], gt[gname].ap[:, k:k + 1], rstd_t.ap[:, 0:N], ALU.mult, ALU.mult,
                  [xs.b, gt[gname].b, rstd_t.b], [hbf.b])
            if k % 2 == 1:
                yield

    def ffn(xs, hbf, wg_d, wu_d, wd_d, gname, sq_t, rstd_t, actT, wgs, wus, wds, silt, scr, first, bg=None, bgB=None, do_norm=True, nstep=2, bgB_first=False):
        N = 512
        if do_norm:
            for _ in ffn_norm_gen(xs, hbf, gname, sq_t, rstd_t):
                pass
        wgv = wg_d.rearrange("(k p) f -> p k f", p=128)
        wuv = wu_d.rearrange("(k p) f -> p k f", p=128)
        NG = 11
        WgB, WuB, WdB, WdBt = scr

        def load_wd(d):
            slot = d % 3
            if first:
                C.dma(wds.ap[:, slot, 0:21, :], wd_d[0:21 * 128, d * 128:(d + 1) * 128].rearrange("(f p) d -> p f d", p=128),
                      [], [wds.b[slot]], q="pool")
                C.dma(wds.ap[0:64, slot, 21, :], wd_d[21 * 128:DFF, d * 128:(d + 1) * 128], [], [wds.b[slot + 3]], q="pool")
                C.dma(WdB.ap[:, d, 0:21, :], wds.ap[:, slot, 0:21, :], [wds.b[slot]], [WdB.b[d]])
                C.dma(WdB.ap[0:64, d, 21, :], wds.ap[0:64, slot, 21, :], [wds.b[slot + 3]], [WdBt.b[d]])
            else:
                C.dma(wds.ap[:, slot, 0:21, :], WdB.ap[:, d, 0:21, :], [WdB.b[d]], [wds.b[slot]], q="sp")
                C.dma(wds.ap[0:64, slot, 21, :], WdB.ap[0:64, d, 21, :], [WdBt.b[d]], [wds.b[slot + 3]], q="sp")

        for g in range(NG):
            if g in (5, 7, 9):
                load_wd((g - 5) // 2)
            c0 = g * 256
            cw = min(256, DFF - c0)
            slot = g % 3
            if first:
                C.dma(wgs.ap[:, slot, :, 0:cw], wgv[:, :, c0:c0 + cw], [], [wgs.b[slot]], q="pool")
                C.dma(wus.ap[:, slot, :, 0:cw], wuv[:, :, c0:c0 + cw], [], [wus.b[slot]], q="pool")
                C.dma(WgB.ap[:, :, c0:c0 + cw], wgs.ap[:, slot, :, 0:cw], [wgs.b[slot]], [WgB.b[g]])
                C.dma(WuB.ap[:, :, c0:c0 + cw], wus.ap[:, slot, :, 0:cw], [wus.b[slot]], [WuB.b[g]])
            else:
                C.dma(wgs.ap[:, slot, :, 0:cw], WgB.ap[:, :, c0:c0 + cw], [WgB.b[g]], [wgs.b[slot]], q="sp")
                C.dma(wus.ap[:, slot, :, 0:cw], WuB.ap[:, :, c0:c0 + cw], [WuB.b[g]], [wus.b[slot]], q="sp")
            for ff in range(2):
                f = 2 * g + ff
                if f >= NF:
                    break
                fs = min(128, DFF - f * 128)
                pg = C.ps()
                pu = C.ps()
                for k in range(8):
                    C.mm(pg.ap[0:fs, :], wgs.ap[:, slot, k, ff * 128:ff * 128 + fs], hbf.ap[:, k, :], k == 0, k == 7,
                         [wgs.b[slot], hbf.b], [pg.b])
                for k in range(8):
                    C.mm(pu.ap[0:fs, :], wus.ap[:, slot, k, ff * 128:ff * 128 + fs], hbf.ap[:, k, :], k == 0, k == 7,
                         [wus.b[slot], hbf.b], [pu.b])
                C.act(silt.ap[0:fs, f % 2, :], pg.ap[0:fs, :], AF.Silu, [pg.b], [silt.b[f % 2]])
                C.tt(actT.ap[0:fs, f, :], pu.ap[0:fs, :], silt.ap[0:fs, f % 2, :], ALU.mult, [pu.b, silt.b[f % 2]], [actT.b[f]])
                if bg is not None:
                    for _ in range(nstep):
                        next(bg, None)
        for d in range(8):
            slot = d % 3
            if d + 3 < 8:
                pass
            po = C.ps()
            for f in range(NF):
                fs = min(128, DFF - f * 128)
                C.mm(po.ap, wds.ap[0:fs, slot, f, :], actT.ap[0:fs, f, :], f == 0, f == NF - 1,
                     [wds.b[slot + (3 if f == NF - 1 else 0)], actT.b[f]], [po.b])
            C.stt(xs.ap[:, d, :], po.ap, 0.5, xs.ap[:, d, :], ALU.mult, ALU.add, [po.b, xs.b], [xs.b])
            if d + 3 < 8:
                load_wd(d + 3)
            if bgB is not None and bgB_first:
                next(bgB, None)
            bg_done = True
            if bg is not None:
                for _ in range(4):
                    if next(bg, "END") == "END":
                        bg = None
                        break
                bg_done = bg is None
            if bgB is not None and bg_done and not bgB_first:
                next(bgB, None)
        if bg is not None:
            for _ in bg:
                pass
        if bgB is not None:
            for _ in bgB:
                pass

    def select_own(out3, src_fn, reads, writes):
        C.ts(out3[:, :, 0:128], src_fn(0), sel.ap[:, 0:1], None, ALU.mult, None, reads + [sel.b], writes)
        C.stt(out3[:, :, 0:128], src_fn(1), sel.ap[:, 1:2], out3[:, :, 0:128], ALU.mult, ALU.add, reads + [sel.b] + writes, writes)
        C.ts(out3[:, :, 128:256], src_fn(3), sel.ap[:, 0:1], None, ALU.mult, None, reads + [sel.b], writes)
        C.stt(out3[:, :, 128:256], src_fn(2), sel.ap[:, 1:2], out3[:, :, 128:256], ALU.mult, ALU.add, reads + [sel.b] + writes, writes)

    mK = C.alloc([128, 4, 256], BF16)
    mV = C.alloc([128, 2, 512], BF16)
    base_sp = C.sp
    C.barrier()
    if True:
        mx = C.alloc([128, 8, 256], F32)
        mh = C.alloc([128, 8, 256], BF16)
        sq_t = C.alloc([128, 8, 512], BF16)
        rstd_t = C.alloc([128, 512], F32)
        wkv = C.alloc([128, 8, 1024], BF16)
        C.dma(mx.ap, memT.rearrange("(k p) t -> p k t", p=128), [], [mx.b])
        C.dma(wkv.ap, W["w_xkv"].rearrange("(k p) f -> p k f", p=128), [], [wkv.b], q="pool")
        rstd_from([(mx.ap[:, k, :], 128, mx.b) for k in range(8)], 256, D, sq_t, rstd_t)
        for k in range(8):
            C.stt(mh.ap[:, k, :], mx.ap[:, k, :], gt["g_mem"].ap[:, k:k + 1], rstd_t.ap[:, 0:256], ALU.mult, ALU.mult,
                  [mx.b, gt["g_mem"].b, rstd_t.b], [mh.b])
        rk = C.alloc([128, 512], F32)
        for h in range(4):
            pk = C.ps()
            for k in range(8):
                C.mm(pk.ap[:, 0:256], wkv.ap[:, k, h * 128:(h + 1) * 128], mh.ap[:, k, :], k == 0, k == 7, [wkv.b, mh.b], [pk.b])
            rstd_from([(pk.ap[:, 0:256], 128, pk.b)], 256, 128, sq_t, rk)
            C.stt(mK.ap[:, h, :], pk.ap[:, 0:256], gt["g_xk"].ap[:, 0:1], rk.ap[:, 0:256], ALU.mult, ALU.mult,
                  [pk.b, gt["g_xk"].b, rk.b], [mK.b])
        for j in range(2):
            pv = C.ps()
            for k in range(8):
                C.mm(pv.ap, mh.ap[:, k, j * 128:(j + 1) * 128], wkv.ap[:, k, 512:1024], k == 0, k == 7, [wkv.b, mh.b], [pv.b])
            C.acopy(mV.ap[:, j, :], pv.ap, [pv.b], [mV.b])

    if stop == "M":
        S.emit()
        return nc
    C.barrier()
    C.sp = base_sp
    xs = C.alloc([128, 8, 512], F32)
    xs_b = C.alloc([128, 8, 512], F32)
    hbf = C.alloc([128, 8, 512], BF16)
    hbf2 = C.alloc([128, 8, 512], BF16)
    sq_t = C.alloc([128, 8, 512], BF16)
    rstd_t = C.alloc([128, 512], F32)
    actT = C.alloc([128, NF, 512], BF16, nb=NF)
    wgs = C.alloc([128, 3, 8, 256], BF16, nb=3)
    wus = C.alloc([128, 3, 8, 256], BF16, nb=3)
    wds = C.alloc([128, 3, NF, 128], BF16, nb=6)
    silt = C.alloc([128, 2, 512], F32, nb=2)
    w_in = C.alloc([128, 8, 1280], BF16)
    w_ukv = C.alloc([128, 2, 1024], BF16)
    x1own = C.alloc([128, 8, 128], F32)
    cqo = C.alloc([128, 3, 256], F32)
    cqn = C.alloc([128, 3, 256], BF16)
    w_uq = C.alloc([128, 3, 1024], BF16)
    w_uq.b.w = None
    C.dma(w_uq.ap, W["w_uq"].rearrange("(k p) f -> p k f", p=128), [], [w_uq.b], q="pool")
    Qn_c = C.alloc([128, 4, 256], BF16)
    Qr_c = C.alloc([64, 4, 256], BF16)
    cosO = C.alloc([64, 256], F32)
    sinO = C.alloc([64, 256], F32)
    ckvn = C.alloc([128, 2, 512], BF16)
    r2 = C.alloc([128, 512], F32)
    cosT = C.alloc([64, 512], F32)
    sinS = C.alloc([64, 512], F32)
    tmpi = C.alloc([64, 512], I32)
    tmpf = C.alloc([64, 512], F32)
    tmpk = C.alloc([64, 512], F32)
    t1 = C.alloc([64, 512], F32)
    t2 = C.alloc([64, 512], F32)
    krr = C.alloc([64, 512], BF16)
    krsq = C.alloc([64, 512], BF16)
    t3 = C.alloc([64, 512], F32)
    t4 = C.alloc([64, 512], F32)
    knb = None
    knsq = C.alloc([128, 4, 512], BF16)
    onehot = C.alloc([128, 4, 4], BF16)
    ub = C.alloc([128, 4, 512], BF16)
    vb = ub
    knb = ub
    rk = C.alloc([128, 4, 4], F32)
    for tl_ in (wgs, wus, wds):
        for b_ in tl_.b:
            b_.w = None
    w_in.b.w = None
    w_ukv.b.w = None
    w_in_v = W["w_in"].rearrange("(k p) f -> p k f", p=128)
    for cc in range(2):
        C.dma(w_in.ap[:, :, cc * 640:(cc + 1) * 640], w_in_v[:, :, cc * 640:(cc + 1) * 640], [], [w_in.b], q="pool")
    C.dma(w_ukv.ap, W["w_ukv"].rearrange("(k p) f -> p k f", p=128), [], [w_ukv.b], q="pool")
    onehot_f = C.alloc([128, 4, 4], F32)
    C.memset(onehot_f.ap, 0.0, [onehot_f.b])
    for h in range(4):
        C.memset(onehot_f.ap[:, h, h:h + 1], 1.0, [onehot_f.b])
    C.cp(onehot.ap, onehot_f.ap, [onehot_f.b], [onehot.b])
    xTv = xT.rearrange("(k p) t -> p k t", p=128)
    if stop == "A0":
        S.emit()
        return nc
    def post_gen(c, xs):
        t0 = c * 512
        for (ob, ba, bb) in ((0, 0, 1), (1, 3, 2)):
            C.ts(x1own.ap, xs.ap[:, :, ba * 128:(ba + 1) * 128], sel.ap[:, 0:1], None, ALU.mult, None, [xs.b, sel.b], [x1own.b])
            C.stt(x1own.ap, xs.ap[:, :, bb * 128:(bb + 1) * 128], sel.ap[:, 1:2], x1own.ap, ALU.mult, ALU.add, [xs.b, sel.b, x1own.b], [x1own.b])
            C.dma(x1D.ap[:, :, c * 256 + ob * 128:c * 256 + (ob + 1) * 128], x1own.ap, [x1own.b], [x1D.b], q="pool")
            yield
        rstd_from([(xs.ap[:, k, :], 128, xs.b) for k in range(8)], 512, D, sq_t, rstd_t)
        yield
        for k in range(8):
            C.stt(hbf2.ap[:, k, :], xs.ap[:, k, :], gt["g_mix"].ap[:, k:k + 1], rstd_t.ap, ALU.mult, ALU.mult,
                  [xs.b, gt["g_mix"].b, rstd_t.b], [hbf2.b])
            yield
        for i in range(3):
            p = C.ps()
            for k in range(8):
                C.mm(p.ap, w_in.ap[:, k, i * 128:(i + 1) * 128], hbf2.ap[:, k, :], k == 0, k == 7, [w_in.b, hbf2.b], [p.b])
            pv3 = p.ap.rearrange("p (a t) -> p a t", a=1)
            select_own(cqo.ap[:, i:i + 1, :], lambda blk: pv3[:, :, blk * 128:(blk + 1) * 128], [p.b], [cqo.b])
            yield
        rstd_from([(cqo.ap[:, i, :], 128, cqo.b) for i in range(3)], 256, 384, sq_t, r2)
        yield
        for i in range(3):
            C.stt(cqn.ap[:, i, :], cqo.ap[:, i, :], gt["g_q"].ap[:, i:i + 1], r2.ap[:, 0:256], ALU.mult, ALU.mult,
                  [cqo.b, gt["g_q"].b, r2.b], [cqn.b])
            yield
        rope_tables(pos_own, c * 256, 256, cosO, sinO, tmpi, tmpf, tmpk)
        yield
        for h in range(4):
            pn = C.ps()
            pr = C.ps()
            prs = C.ps()
            for k in range(3):
                C.mm(pn.ap[:, 0:256], w_uq.ap[:, k, h * 256:h * 256 + 128], cqn.ap[:, k, :], k == 0, k == 2, [w_uq.b, cqn.b], [pn.b])
            for k in range(3):
                C.mm(pr.ap[0:64, 0:256], w_uq.ap[:, k, h * 256 + 128:h * 256 + 192], cqn.ap[:, k, :], k == 0, k == 2, [w_uq.b, cqn.b], [pr.b])
            for k in range(3):
                C.mm(prs.ap[0:64, 0:256], w_uq.ap[:, k, h * 256 + 192:h * 256 + 256], cqn.ap[:, k, :], k == 0, k == 2, [w_uq.b, cqn.b], [prs.b])
            rstd_from([(pn.ap[:, 0:256], 128, pn.b), (pr.ap[0:64, 0:256], 64, pr.b)], 256, 192, sq_t, r2)
            C.stt(Qn_c.ap[:, h, :], pn.ap[:, 0:256], gt["g_qn"].ap[:, 0:1], r2.ap[:, 0:256], ALU.mult, ALU.mult, [pn.b, gt["g_qn"].b, r2.b], [Qn_c.b])
            C.stt(t1.ap[:, 0:256], pr.ap[0:64, 0:256], gt["g_qr"].ap, cosO.ap, ALU.mult, ALU.mult, [pr.b, gt["g_qr"].b, cosO.b, r2.b], [t1.b])
            C.stt(t2.ap[:, 0:256], prs.ap[0:64, 0:256], gt["g_qrs"].ap, sinO.ap, ALU.mult, ALU.mult, [prs.b, gt["g_qrs"].b, sinO.b, r2.b], [t2.b])
            C.tt(t1.ap[:, 0:256], t1.ap[:, 0:256], t2.ap[:, 0:256], ALU.add, [t1.b, t2.b], [t1.b])
            C.tt(Qr_c.ap[:, h, :], t1.ap[:, 0:256], r2.ap[0:64, 0:256], ALU.mult, [t1.b, r2.b], [Qr_c.b])
            yield
        C.dma(QnD.ap[:, :, c * 256:(c + 1) * 256], Qn_c.ap, [Qn_c.b], [QnD.b], q="pool")
        C.dma(QrD.ap[:, :, c * 256:(c + 1) * 256], Qr_c.ap, [Qr_c.b], [QrD.b], q="pool")
        yield
        pkv = [C.ps(), C.ps()]
        for i in range(2):
            for k in range(8):
                C.mm(pkv[i].ap, w_in.ap[:, k, 384 + i * 128:384 + (i + 1) * 128], hbf2.ap[:, k, :], k == 0, k == 7, [w_in.b, hbf2.b], [pkv[i].b])
        rstd_from([(pkv[i].ap, 128, pkv[i].b) for i in range(2)], 512, 256, sq_t, r2)
        for i in range(2):
            C.stt(ckvn.ap[:, i, :], pkv[i].ap, gt["g_kv"].ap[:, i:i + 1], r2.ap, ALU.mult, ALU.mult,
                  [pkv[i].b, gt["g_kv"].b, r2.b], [ckvn.b])
        yield
        rope_tables(pos_all, t0, 512, cosT, sinS, tmpi, tmpf, tmpk)
        yield
        pk1 = C.ps()
        pk2 = C.ps()
        for k in range(8):
            C.mm(pk1.ap[0:64, :], w_in.ap[:, k, 640:704], hbf2.ap[:, k, :], k == 0, k == 7, [w_in.b, hbf2.b], [pk1.b])
        for k in range(8):
            C.mm(pk2.ap[0:64, :], w_in.ap[:, k, 704:768], hbf2.ap[:, k, :], k == 0, k == 7, [w_in.b, hbf2.b], [pk2.b])
        C.acopy(t3.ap, pk1.ap[0:64, :], [pk1.b], [t3.b])
        C.acopy(t4.ap, pk2.ap[0:64, :], [pk2.b], [t4.b])
        yield
        C.act(krsq.ap, t3.ap, AF.Square, [t3.b], [krsq.b])
        yield
        C.stt(t1.ap, t3.ap, gt["g_kr"].ap, cosT.ap, ALU.mult, ALU.mult, [t3.b, gt["g_kr"].b, cosT.b], [t1.b])
        yield
        C.stt(t2.ap, t4.ap, gt["g_krs"].ap, sinS.ap, ALU.mult, ALU.mult, [t4.b, gt["g_krs"].b, sinS.b], [t2.b])
        yield
        C.tt(krr.ap, t1.ap, t2.ap, ALU.add, [t1.b, t2.b], [krr.b])
        yield
        C.dma(krrD.ap[:, t0:t0 + 512], krr.ap, [krr.b], [krrD.b], q="pool")
        yield
        for i in range(4):
            p = C.ps()
            for k in range(8):
                C.mm(p.ap, w_in.ap[:, k, 768 + i * 128:768 + (i + 1) * 128], hbf2.ap[:, k, :], k == 0, k == 7, [w_in.b, hbf2.b], [p.b])
            C.acopy(ub.ap[:, i, :], p.ap, [p.b], [ub.b])
            yield
        C.dma(uD.ap[:, :, t0:t0 + 512], ub.ap, [ub.b], [uD.b], q="pool")
        yield
        for h in range(4):
            p = C.ps()
            for k in range(2):
                C.mm(p.ap, w_ukv.ap[:, k, h * 128:(h + 1) * 128], ckvn.ap[:, k, :], k == 0, k == 1, [w_ukv.b, ckvn.b], [p.b])
            C.act(knsq.ap[:, h, :], p.ap, AF.Square, [p.b], [knsq.b])
            C.ts(knb.ap[:, h, :], p.ap, gt["g_kn"].ap[:, 0:1], None, ALU.mult, None, [p.b, gt["g_kn"].b, knsq.b], [knb.b])
            yield
        C.dma(KnD.ap[:, :, t0:t0 + 512], knb.ap, [knb.b], [KnD.b], q="pool")
        yield
        pq = C.ps()
        for blk in range(4):
            for h in range(4):
                C.mm(pq.ap[:, blk * 4:blk * 4 + 4], knsq.ap[:, h, blk * 128:(blk + 1) * 128], onehot.ap[:, h, :], h == 0, False,
                     [knsq.b, onehot.b], [pq.b])
            C.mm(pq.ap[:, blk * 4:blk * 4 + 4], krsq.ap[:, blk * 128:(blk + 1) * 128], ones_bf.ap[0:64, 0:4], False, True,
                 [krsq.b, ones_bf.b], [pq.b])
        rkf = rk.ap.rearrange("p a b -> p (a b)")
        C.act(rkf, pq.ap[:, 0:16], AF.Sqrt, [pq.b, eps_t.b], [rk.b], bias=eps_t.ap, scale=1.0 / 192.0)
        yield
        C.recip(rkf, rkf, [rk.b], [rk.b])
        yield
        C.ts(rkf, rkf, 192.0 ** -0.5, None, ALU.mult, None, [rk.b], [rk.b])
        yield
        C.dma(rstdkD.ap[:, c * 4:(c + 1) * 4, :], rk.ap, [rk.b], [rstdkD.b], q="pool")
        yield
        for blk in range(4):
            p = C.ps()
            for k in range(2):
                C.mm(p.ap, ckvn.ap[:, k, blk * 128:(blk + 1) * 128], w_ukv.ap[:, k, 512:1024], k == 0, k == 1, [w_ukv.b, ckvn.b], [p.b])
            C.acopy(vb.ap[:, blk, :], p.ap, [p.b], [vb.b])
            yield
        C.dma(VD.ap[:, c * 4:(c + 1) * 4, :], vb.ap, [vb.b], [VD.b], q="pool")
        yield
        yield

    xs_bufs = [xs, xs_b]

    def pre_gen(c):
        xb = xs_bufs[c % 2]
        C.dma(xb.ap, xTv[:, :, c * 512:(c + 1) * 512], [], [xb.b])
        yield
        for _ in ffn_norm_gen(xb, hbf, "g_ffn1", sq_t, rstd_t):
            yield

    POST_STEP = 1
    for _ in pre_gen(0):
        pass
    prev = None
    for c in range(8):
        xs = xs_bufs[c % 2]
        ffn(xs, hbf, W["ffn1_wg"], W["ffn1_wu"], W["ffn1_wd"], "g_ffn1", sq_t, rstd_t, actT, wgs, wus, wds, silt, scr1, c == 0,
            bg=prev, bgB=(pre_gen(c + 1) if c + 1 < 8 else None), do_norm=False, nstep=POST_STEP, bgB_first=True)
        prev = post_gen(c, xs)
    for _ in prev:
        pass


    if stop == "A":
        S.emit()
        return nc
    C.barrier()
    C.sp = base_sp
    WgB2, WuB2, WdB2, WdB2t = scr2
    for (src, dst) in ((W["ffn2_wg"], WgB2), (W["ffn2_wu"], WuB2)):
        sv = src.rearrange("(k p) f -> p k f", p=128)
        for cc in range(4):
            C.dma(dst.ap[:, :, cc * 688:(cc + 1) * 688], sv[:, :, cc * 688:(cc + 1) * 688], [], list(dst.b), q="pool")
    wd2 = W["ffn2_wd"]
    for d in range(8):
        C.dma(WdB2.ap[:, d, 0:21, :], wd2[0:21 * 128, d * 128:(d + 1) * 128].rearrange("(f p) c -> p f c", p=128),
              [], [WdB2.b[d]], q="pool")
    C.dma(WdB2.ap[0:64, :, 21, :], wd2[21 * 128:DFF, :].rearrange("p (d c) -> p d c", c=128), [], list(WdB2t.b), q="pool")
    ssm_stage(C, G, W, gt, sel, uD, ysD, dbg, eps_t, ones_bf, iota_d, maskE_d, rstd_from, kvec_d)

    if stop == "B":
        S.emit()
        return nc
    ymD = dscr("ymD", [128, 4, NOWN], BF16)
    x3D = dscr("x3D", [128, 8, NOWN], F32)
    C.barrier()
    C.sp = base_sp
    Kn = C.alloc([128, 4, T], BF16)
    krA = C.alloc([64, T], BF16)
    Vv = C.alloc([128, 32, 512], BF16)
    rkA = C.alloc([128, 32, 4], F32)
    amask = C.alloc([128, 8, 512], BF16)
    C.dma(Kn.ap, KnD.ap, [KnD.b], [Kn.b])
    C.dma(krA.ap, krrD.ap, [krrD.b], [krA.b])
    C.dma(Vv.ap, VD.ap, [VD.b], [Vv.b])
    C.dma(rkA.ap, rstdkD.ap, [rstdkD.b], [rkA.b])
    C.dma(amask.ap, amask_d, [], [amask.b], q="pool")
    sq_t = C.alloc([128, 8, 512], BF16)
    r2 = C.alloc([128, 512], F32)
    ymn = C.alloc([128, 4, 512], BF16)
    ymla = C.alloc([128, 4, 512], F32)
    cosT = C.alloc([64, 512], F32)
    sinS = C.alloc([64, 512], F32)
    tmpi = C.alloc([64, 512], I32)
    tmpf = C.alloc([64, 512], F32)
    tmpk = C.alloc([64, 512], F32)
    t1 = C.alloc([64, 512], F32)
    t2 = C.alloc([64, 512], F32)
    Qn2 = [C.alloc([128, 4, 512], BF16) for _ in range(2)]
    Qr2 = [C.alloc([64, 4, 512], BF16) for _ in range(2)]
    PT = C.alloc([128, 6, 512], BF16, nb=6)
    rden = C.alloc([128, 512], F32)
    pti = 0
    for m in range(4):
        o0 = m * 512
        Qn = Qn2[m % 2]
        Qr = Qr2[m % 2]
        if m == 0:
            C.dma(Qn.ap, QnD.ap[:, :, 0:512], [QnD.b], [Qn.b])
            C.dma(Qr.ap, QrD.ap[:, :, 0:512], [QrD.b], [Qr.b])
        if m + 1 < 4:
            C.dma(Qn2[(m + 1) % 2].ap, QnD.ap[:, :, o0 + 512:o0 + 1024], [QnD.b], [Qn2[(m + 1) % 2].b])
            C.dma(Qr2[(m + 1) % 2].ap, QrD.ap[:, :, o0 + 512:o0 + 1024], [QrD.b], [Qr2[(m + 1) % 2].b])
        nkb = 8 * m + 8
        C.reserved = {4, 5, 6, 7}
        for h in range(4):
            po = C.psum[4 + 2 * (h % 2)]
            pd = C.psum[5 + 2 * (h % 2)]
            def score(kb):
                pst = C.ps()
                C.mm(pst.ap, Kn.ap[:, h, kb * 128:(kb + 1) * 128], Qn.ap[:, h, :], True, False, [Kn.b, Qn.b], [pst.b])
                C.mm(pst.ap, krA.ap[:, kb * 128:(kb + 1) * 128], Qr.ap[:, h, :], False, True, [krA.b, Qr.b], [pst.b])
                return pst
            LOOK = 2
            pend = [score(kb) for kb in range(min(LOOK, nkb))]
            for kb in range(nkb):
                pst = pend.pop(0)
                if kb + LOOK < nkb:
                    pend.append(score(kb + LOOK))
                sl = pti % 6
                pti += 1
                C.act(PT.ap[:, sl, :], pst.ap, AF.Exp, [pst.b, rkA.b], [PT.b[sl]], scale=rkA.ap[:, kb, h:h + 1])
                if kb >= 8 * m:
                    C.tt(PT.ap[:, sl, :], PT.ap[:, sl, :], amask.ap[:, kb - 8 * m, :], ALU.mult, [PT.b[sl], amask.b], [PT.b[sl]])
                C.mm(po.ap, Vv.ap[:, kb, h * 128:(h + 1) * 128], PT.ap[:, sl, :], kb == 0, kb == nkb - 1, [Vv.b, PT.b[sl]], [po.b])
                C.mm(pd.ap, ones_bf.ap, PT.ap[:, sl, :], kb == 0, kb == nkb - 1, [ones_bf.b, PT.b[sl]], [pd.b])
            C.recip(rden.ap, pd.ap, [pd.b], [rden.b])
            C.tt(ymla.ap[:, h, :], po.ap, rden.ap, ALU.mult, [po.b, rden.b], [ymla.b])
        C.reserved = set()
        if debug:
            C.dma(dbg["d_ymla"].ap.rearrange("(k p) t -> p k t", p=128)[:, :, o0:o0 + 512], ymla.ap, [ymla.b], [dbg["d_ymla"].b])
        rstd_from([(ymla.ap[:, k, :], 128, ymla.b) for k in range(4)], 512, 512, sq_t, r2)
        for k in range(4):
            C.stt(ymn.ap[:, k, :], ymla.ap[:, k, :], gt["g_om"].ap[:, k:k + 1], r2.ap, ALU.mult, ALU.mult,
                  [ymla.b, gt["g_om"].b, r2.b], [ymn.b])
        C.dma(ymD.ap[:, :, o0:o0 + 512], ymn.ap, [ymn.b], [ymD.b])

    if stop == "C1":
        S.emit()
        return nc
    C.barrier()
    C.sp = base_sp
    w_o = C.alloc([128, 8, 1024], BF16)
    w_xq = C.alloc([128, 8, 512], BF16)
    w_xo = C.alloc([128, 4, 1024], BF16)
    C.dma(w_o.ap, W["w_o"].rearrange("(k p) f -> p k f", p=128), [], [w_o.b], q="pool")
    C.dma(w_xq.ap, W["w_xq"].rearrange("(k p) f -> p k f", p=128), [], [w_xq.b], q="pool")
    C.dma(w_xo.ap, W["w_xo"].rearrange("(k p) f -> p k f", p=128), [], [w_xo.b], q="pool")
    xs = C.alloc([128, 8, 512], F32)
    hbf = C.alloc([128, 8, 512], BF16)
    sq_t = C.alloc([128, 8, 512], BF16)
    rstd_t = C.alloc([128, 512], F32)
    r2 = C.alloc([128, 512], F32)
    ycat = C.alloc([128, 8, 512], BF16, nb=2)
    PT = C.alloc([128, 6, 512], BF16, nb=6)
    rden = C.alloc([128, 512], F32)
    qx = C.alloc([128, 4, 512], BF16)
    ox = C.alloc([128, 4, 512], BF16)
    actT = C.alloc([128, NF, 512], BF16, nb=NF)
    wgs = C.alloc([128, 3, 8, 256], BF16, nb=3)
    wus = C.alloc([128, 3, 8, 256], BF16, nb=3)
    wds = C.alloc([128, 3, NF, 128], BF16, nb=6)
    silt = C.alloc([128, 2, 512], F32, nb=2)
    xs_b = C.alloc([128, 8, 512], F32)
    hbfx = C.alloc([128, 8, 512], BF16)
    xs_bufs = [xs, xs_b]
    pti_box = [0]

    def pre2_gen(m):
        o0 = m * 512
        xs = xs_bufs[m % 2]
        C.dma(xs.ap, x1D.ap[:, :, o0:o0 + 512], [x1D.b], [xs.b])
        C.dma(ycat.ap[:, 0:4, :], ymD.ap[:, :, o0:o0 + 512], [ymD.b], [ycat.b[0]])
        C.dma(ycat.ap[:, 4:8, :], ysD.ap[:, :, o0:o0 + 512], [ysD.b], [ycat.b[1]])
        if debug:
            C.dma(dbg["d_x1own"].ap.rearrange("(k p) t -> p k t", p=128)[:, :, o0:o0 + 512], xs.ap, [xs.b], [dbg["d_x1own"].b])
        yield
        for d in range(8):
            p = C.ps()
            for k in range(8):
                C.mm(p.ap, w_o.ap[:, k, d * 128:(d + 1) * 128], ycat.ap[:, k, :], k == 0, k == 7, [w_o.b, ycat.b[k // 4]], [p.b])
            C.tt(xs.ap[:, d, :], p.ap, xs.ap[:, d, :], ALU.add, [p.b, xs.b], [xs.b])
            yield
        if debug:
            C.dma(dbg["d_x2"].ap.rearrange("(k p) t -> p k t", p=128)[:, :, o0:o0 + 512], xs.ap, [xs.b], [dbg["d_x2"].b])
        rstd_from([(xs.ap[:, k, :], 128, xs.b) for k in range(8)], 512, D, sq_t, rstd_t)
        yield
        for k in range(8):
            C.stt(hbfx.ap[:, k, :], xs.ap[:, k, :], gt["g_xa"].ap[:, k:k + 1], rstd_t.ap, ALU.mult, ALU.mult,
                  [xs.b, gt["g_xa"].b, rstd_t.b], [hbfx.b])
            if k % 2 == 1:
                yield
        for h in range(4):
            p = C.ps()
            for k in range(8):
                C.mm(p.ap, w_xq.ap[:, k, h * 128:(h + 1) * 128], hbfx.ap[:, k, :], k == 0, k == 7, [w_xq.b, hbfx.b], [p.b])
            rstd_from([(p.ap, 128, p.b)], 512, 128, sq_t, r2)
            C.stt(qx.ap[:, h, :], p.ap, gt["g_xq"].ap[:, 0:1], r2.ap, ALU.mult, ALU.mult, [p.b, gt["g_xq"].b, r2.b], [qx.b])
            yield
        for h in range(4):
            po = C.ps()
            pd = C.ps()
            for j in range(2):
                pst = C.ps()
                C.mm(pst.ap, mK.ap[:, h, j * 128:(j + 1) * 128], qx.ap[:, h, :], True, True, [mK.b, qx.b], [pst.b])
                sl = pti_box[0] % 6
                pti_box[0] += 1
                C.act(PT.ap[:, sl, :], pst.ap, AF.Exp, [pst.b], [PT.b[sl]], scale=128.0 ** -0.5)
                C.mm(po.ap, mV.ap[:, j, h * 128:(h + 1) * 128], PT.ap[:, sl, :], j == 0, j == 1, [mV.b, PT.b[sl]], [po.b])
                C.mm(pd.ap, ones_bf.ap, PT.ap[:, sl, :], j == 0, j == 1, [ones_bf.b, PT.b[sl]], [pd.b])
            C.recip(rden.ap, pd.ap, [pd.b], [rden.b])
            C.tt(ox.ap[:, h, :], po.ap, rden.ap, ALU.mult, [po.b, rden.b], [ox.b])
            yield
        for d in range(8):
            p = C.ps()
            for k in range(4):
                C.mm(p.ap, w_xo.ap[:, k, d * 128:(d + 1) * 128], ox.ap[:, k, :], k == 0, k == 3, [w_xo.b, ox.b], [p.b])
            C.tt(xs.ap[:, d, :], p.ap, xs.ap[:, d, :], ALU.add, [p.b, xs.b], [xs.b])
            yield
        if debug:
            C.dma(dbg["d_x3"].ap.rearrange("(k p) t -> p k t", p=128)[:, :, o0:o0 + 512], xs.ap, [xs.b], [dbg["d_x3"].b])
        yield

    for _ in pre2_gen(0):
        pass
    for m in range(4):
        o0 = m * 512
        xs = xs_bufs[m % 2]
        ffn(xs, hbf, W["ffn2_wg"], W["ffn2_wu"], W["ffn2_wd"], "g_ffn2", sq_t, rstd_t, actT, wgs, wus, wds, silt, scr2, False,
            bg=(pre2_gen(m + 1) if m < 3 else None),
            bgB=(ffn_norm_gen(xs_bufs[(m + 1) % 2], hbf, "g_ffn2", sq_t, rstd_t) if m < 3 else None),
            do_norm=(m == 0), nstep=2)
        C.dma(outT.rearrange("(k p) t -> p k t", p=128)[:, :, o0:o0 + 512], xs.ap, [xs.b], [Buf()], q="pool")
    S.emit()
    return nc


def ssm_stage(C, G, W, gt, sel, uD, ysD, dbg, eps_t, ones_bf, iota_d, maskE_d, rstd_from, kvec_d):
    def ld(nm, shp):
        t = C.alloc(shp, F32)
        C.dma(t.ap, G[nm], [], [t.b])
        return t

    iota = C.alloc([128, NJ], F32)
    C.dma(iota.ap, iota_d, [], [iota.b])
    maskE = C.alloc([128, 2], F32)
    C.dma(maskE.ap, maskE_d, [], [maskE.b])
    w_glu = C.alloc([128, 4, 512], BF16)
    C.dma(w_glu.ap, W["w_glu"].rearrange("(k p) f -> p k f", p=128), [], [w_glu.b], q="pool")
    L1r = C.alloc([128, 4, L, 128], BF16)
    L1i = C.alloc([128, 4, L, 128], BF16)
    L3r = C.alloc([128, 16, L, 32], BF16)
    L3i = C.alloc([128, 16, L, 32], BF16)
    FIR = C.alloc([128, 4, L, 128], BF16)
    thr = C.alloc([128, 16], F32)
    RL = C.alloc([128, 16], F32)
    maskO = C.alloc([128, 2], F32)
    kvec = C.alloc([128, L + 1], F32)
    C.dma(kvec.ap, kvec_d, [], [kvec.b])
    hpi = C.alloc([128, 1], F32)
    C.memset(hpi.ap, math.pi / 2, [hpi.b])
    sp_T = C.sp

    K1 = L + 1

    def bc_mid(ap2, nb, n):
        return ap2.rearrange("p (o n) -> p o n", o=1).to_broadcast([128, nb, n])

    def bc_last(ap2, nb, n):
        return ap2.rearrange("p (k o) -> p k o", o=1).to_broadcast([128, nb, n])

    def powers_b(lr, li, ldt, n, Pr, Pi, lrdt, lidt, KB):
        dt = C.alloc([128, n], F32)
        C.act(dt.ap, ldt.ap, AF.Exp, [ldt.b], [dt.b])
        C.tt(lrdt.ap, lr.ap, dt.ap, ALU.mult, [lr.b, dt.b], [lrdt.b])
        C.tt(lidt.ap, li.ap, dt.ap, ALU.mult, [li.b, dt.b], [lidt.b])
        a3 = C.alloc([128, KB, n], F32)
        e3 = C.alloc([128, KB, n], F32)
        kf = C.alloc([128, KB, n], F32)
        ki = C.alloc([128, KB, n], I32)
        sn = C.alloc([128, KB, n], F32)
        cs = C.alloc([128, KB, n], F32)
        for k0 in range(0, K1, KB):
            nb = min(KB, K1 - k0)
            kv = bc_last(kvec.ap[:, k0:k0 + nb], nb, n)
            A3, E3, KF, KI, SN, CS = (t.ap[:, 0:nb, :] for t in (a3, e3, kf, ki, sn, cs))
            C.tt(A3, bc_mid(lidt.ap, nb, n), kv, ALU.mult, [lidt.b, kvec.b], [a3.b])
            C.tt(E3, bc_mid(lrdt.ap, nb, n), kv, ALU.mult, [lrdt.b, kvec.b], [e3.b])
            C.act(E3, E3, AF.Exp, [e3.b], [e3.b])
            for (dst, dt_, shift) in ((SN, sn, 0.0), (CS, cs, math.pi / 2)):
                C.ts(KF, A3, shift, 1.0 / TWO_PI, ALU.add, ALU.mult, [a3.b], [kf.b])
                C.cp(KI, KF, [kf.b], [ki.b])
                C.cp(KF, KI, [ki.b], [kf.b])
                C.stt(KF, KF, -TWO_PI, A3, ALU.mult, ALU.add, [kf.b, a3.b], [kf.b])
                C.ts(KF, KF, math.pi - shift, -math.pi - shift, ALU.min, ALU.max, [kf.b], [kf.b])
                if shift:
                    C.act(dst, KF, AF.Sin, [kf.b, hpi.b], [dt_.b], bias=hpi.ap)
                else:
                    C.act(dst, KF, AF.Sin, [kf.b], [dt_.b])
            C.tt(Pr.ap[:, k0:k0 + nb, :], E3, CS, ALU.mult, [e3.b, cs.b], [Pr.b])
            C.tt(Pi.ap[:, k0:k0 + nb, :], E3, SN, ALU.mult, [e3.b, sn.b], [Pi.b])

    def bbar(lr, li, Pr, Pi, n):
        nr = C.alloc([128, n], F32)
        den = C.alloc([128, n], F32)
        tA = C.alloc([128, n], F32)
        cr = C.alloc([128, n], F32)
        ci = C.alloc([128, n], F32)
        C.ts(nr.ap, Pr.ap[:, 1, :], -1.0, None, ALU.add, None, [Pr.b], [nr.b])
        C.tt(den.ap, lr.ap, lr.ap, ALU.mult, [lr.b], [den.b])
        C.tt(tA.ap, li.ap, li.ap, ALU.mult, [li.b], [tA.b])
        C.tt(den.ap, den.ap, tA.ap, ALU.add, [den.b, tA.b], [den.b])
        C.recip(den.ap, den.ap, [den.b], [den.b])
        C.tt(cr.ap, nr.ap, lr.ap, ALU.mult, [nr.b, lr.b], [cr.b])
        C.tt(tA.ap, Pi.ap[:, 1, :], li.ap, ALU.mult, [Pi.b, li.b], [tA.b])
        C.tt(cr.ap, cr.ap, tA.ap, ALU.add, [cr.b, tA.b], [cr.b])
        C.tt(cr.ap, cr.ap, den.ap, ALU.mult, [cr.b, den.b], [cr.b])
        C.tt(ci.ap, Pi.ap[:, 1, :], lr.ap, ALU.mult, [Pi.b, lr.b], [ci.b])
        C.tt(tA.ap, nr.ap, li.ap, ALU.mult, [nr.b, li.b], [tA.b])
        C.tt(ci.ap, ci.ap, tA.ap, ALU.subtract, [ci.b, tA.b], [ci.b])
        C.tt(ci.ap, ci.ap, den.ap, ALU.mult, [ci.b, den.b], [ci.b])
        return cr, ci

    def cmul(out_r, out_i, ar, ai, br, bi, tmp, reads, wr, wi):
        C.tt(out_r, ar, br, ALU.mult, reads, [wr])
        C.tt(tmp.ap, ai, bi, ALU.mult, reads, [tmp.b])
        C.tt(out_r, out_r, tmp.ap, ALU.subtract, [wr, tmp.b], [wr])
        C.tt(out_i, ar, bi, ALU.mult, reads, [wi])
        C.tt(tmp.ap, ai, br, ALU.mult, reads, [tmp.b])
        C.tt(out_i, out_i, tmp.ap, ALU.add, [wi, tmp.b], [wi])

    n_in = 256
    lr_i = ld("are_in", [128, 4, 64]); li_i = ld("aim_in", [128, 4, 64]); ldt_i = ld("ldt_in", [128, 4, 64])
    br_i = ld("bre_in", [128, 4, 64]); bi_i = ld("bim_in", [128, 4, 64])
    f2 = lambda t: Tl(t.ap.rearrange("p a b -> p (a b)"), t.b)
    lr_i2, li_i2, ldt_i2, br_i2, bi_i2 = f2(lr_i), f2(li_i), f2(ldt_i), f2(br_i), f2(bi_i)
    Pr_i = C.alloc([128, K1, n_in], F32)
    Pi_i = C.alloc([128, K1, n_in], F32)
    lrdt_i = C.alloc([128, n_in], F32)
    lidt_i = C.alloc([128, n_in], F32)
    powers_b(lr_i2, li_i2, ldt_i2, n_in, Pr_i, Pi_i, lrdt_i, lidt_i, 4)
    cr, ci = bbar(lr_i2, li_i2, Pr_i, Pi_i, n_in)
    bbr = C.alloc([128, n_in], F32)
    bbi = C.alloc([128, n_in], F32)
    tA = C.alloc([128, n_in], F32)
    cmul(bbr.ap, bbi.ap, cr.ap, ci.ap, br_i2.ap, bi_i2.ap, tA, [cr.b, ci.b, br_i2.b, bi_i2.b], bbr.b, bbi.b)
    HB = 4
    w1r = C.alloc([128, HB, n_in], F32)
    w1i = C.alloc([128, HB, n_in], F32)
    w1t = C.alloc([128, HB, n_in], F32)
    for p0 in range(0, L, HB):
        cmul(w1r.ap, w1i.ap, Pr_i.ap[:, p0:p0 + HB, :], Pi_i.ap[:, p0:p0 + HB, :], bc_mid(bbr.ap, HB, n_in), bc_mid(bbi.ap, HB, n_in),
             w1t, [Pr_i.b, Pi_i.b, bbr.b, bbi.b], w1r.b, w1i.b)
        w1r4 = w1r.ap.rearrange("p k (c q) -> p c k q", c=4)
        w1i4 = w1i.ap.rearrange("p k (c q) -> p c k q", c=4)
        for e in range(2):
            C.ts(L1r.ap[:, :, p0:p0 + HB, e * 64:(e + 1) * 64], w1r4, maskE.ap[:, e:e + 1], None, ALU.mult, None, [w1r.b, maskE.b], [L1r.b])
            C.ts(L1i.ap[:, :, p0:p0 + HB, e * 64:(e + 1) * 64], w1i4, maskE.ap[:, e:e + 1], None, ALU.mult, None, [w1i.b, maskE.b], [L1i.b])

    C.barrier()
    C.sp = sp_T
    lr_o = ld("are_out", [128, 16]); li_o = ld("aim_out", [128, 16]); ldt_o = ld("ldt_out", [128, 16])
    cr_o = ld("cre_out", [128, 16, 16]); ci_o = ld("cim_out", [128, 16, 16])
    br_o = ld("bre_out", [128, 16, 16]); bi_o = ld("bim_out", [128, 16, 16])
    Pr_o = C.alloc([128, K1, 16], F32)
    Pi_o = C.alloc([128, K1, 16], F32)
    lrdt_o = C.alloc([128, 16], F32)
    lidt_o = C.alloc([128, 16], F32)
    powers_b(lr_o, li_o, ldt_o, 16, Pr_o, Pi_o, lrdt_o, lidt_o, K1)
    cro, cio = bbar(lr_o, li_o, Pr_o, Pi_o, 16)
    C.memset(maskO.ap, 0.0, [maskO.b])
    C.memset(maskO.ap[0:64, 0:1], 1.0, [maskO.b])
    C.memset(maskO.ap[64:128, 1:2], 1.0, [maskO.b])
    bc = lambda ap2: ap2.rearrange("p (q o) -> p q o", o=1).to_broadcast([128, 16, 16])
    bbro = C.alloc([128, 16, 16], F32)
    bbio = C.alloc([128, 16, 16], F32)
    tB0 = C.alloc([128, 16, 16], F32)
    cmul(bbro.ap, bbio.ap, bc(cro.ap), bc(cio.ap), br_o.ap, bi_o.ap, tB0, [cro.b, cio.b, br_o.b, bi_o.b], bbro.b, bbio.b)
    Wsr = C.alloc([128, 16, L, 32], BF16)
    Wsi = C.alloc([128, 16, L, 32], BF16)
    Cer = C.alloc([128, 16, 32], BF16)
    Cei = C.alloc([128, 16, 32], BF16)
    for e in range(2):
        C.ts(Cer.ap[:, :, e * 16:(e + 1) * 16], cr_o.ap, maskO.ap[:, e:e + 1], None, ALU.mult, None, [cr_o.b, maskO.b], [Cer.b])
        C.ts(Cei.ap[:, :, e * 16:(e + 1) * 16], ci_o.ap, maskO.ap[:, e:e + 1], -1.0, ALU.mult, ALU.mult, [ci_o.b, maskO.b], [Cei.b])
    KH = 8
    tr = C.alloc([128, KH, 16, 16], F32)
    ti = C.alloc([128, KH, 16, 16], F32)
    tt_ = C.alloc([128, KH, 16, 16], F32)

    def bq(ap3):
        return ap3.rearrange("p k (q o) -> p k q o", o=1).to_broadcast([128, KH, 16, 16])

    def bk(ap3):
        return ap3.rearrange("p (o q) h -> p o q h", o=1).to_broadcast([128, KH, 16, 16])

    for k0 in range(0, L, KH):
        cmul(tr.ap, ti.ap, bk(cr_o.ap), bk(ci_o.ap), bq(Pr_o.ap[:, k0 + 1:k0 + 1 + KH, :]), bq(Pi_o.ap[:, k0 + 1:k0 + 1 + KH, :]),
             tt_, [cr_o.b, ci_o.b, Pr_o.b, Pi_o.b], tr.b, ti.b)
        for e in range(2):
            o3r = L3r.ap[:, :, k0:k0 + KH, e * 16:(e + 1) * 16].rearrange("p q k h -> p k q h")
            o3i = L3i.ap[:, :, k0:k0 + KH, e * 16:(e + 1) * 16].rearrange("p q k h -> p k q h")
            C.ts(o3r, tr.ap, maskO.ap[:, e:e + 1], None, ALU.mult, None, [tr.b, maskO.b], [L3r.b])
            C.ts(o3i, ti.ap, maskO.ap[:, e:e + 1], -1.0, ALU.mult, ALU.mult, [ti.b, maskO.b], [L3i.b])
        cmul(tr.ap, ti.ap, bk(bbro.ap), bk(bbio.ap), bq(Pr_o.ap[:, k0:k0 + KH, :]), bq(Pi_o.ap[:, k0:k0 + KH, :]),
             tt_, [bbro.b, bbio.b, Pr_o.b, Pi_o.b], tr.b, ti.b)
        for e in range(2):
            o3r = Wsr.ap[:, :, k0:k0 + KH, e * 16:(e + 1) * 16].rearrange("p q k h -> p k q h")
            o3i = Wsi.ap[:, :, k0:k0 + KH, e * 16:(e + 1) * 16].rearrange("p q k h -> p k q h")
            C.ts(o3r, tr.ap, maskO.ap[:, e:e + 1], None, ALU.mult, None, [tr.b, maskO.b], [Wsr.b])
            C.ts(o3i, ti.ap, maskO.ap[:, e:e + 1], None, ALU.mult, None, [ti.b, maskO.b], [Wsi.b])

    C.memset(FIR.ap, 0.0, [FIR.b])
    for q in range(16):
        c, q4 = q // 4, q % 4
        p = C.ps()
        for tau in range(L):
            o = p.ap[32 * q4:32 * q4 + 32, tau * 32:(tau + 1) * 32]
            C.mm(o, Wsr.ap[:, q, tau, :], Cer.ap[:, q, :], True, False, [Wsr.b, Cer.b], [p.b], tile_position=(0, 32 * q4))
            C.mm(o, Wsi.ap[:, q, tau, :], Cei.ap[:, q, :], False, True, [Wsi.b, Cei.b], [p.b], tile_position=(0, 32 * q4))
        C.acopy(FIR.ap[32 * q4:32 * q4 + 32, c, :, 32 * q4:32 * q4 + 32],
                p.ap[32 * q4:32 * q4 + 32, :].rearrange("p (t h) -> p t h", h=32), [p.b], [FIR.b])
    tki = C.alloc([128, 16], I32)
    tkf = C.alloc([128, 16], F32)
    C.ts(thr.ap, lidt_o.ap, float(L), None, ALU.mult, None, [lidt_o.b], [thr.b])
    C.ts(tkf.ap, thr.ap, 1.0 / TWO_PI, None, ALU.mult, None, [thr.b], [tkf.b])
    C.cp(tki.ap, tkf.ap, [tkf.b], [tki.b])
    C.cp(tkf.ap, tki.ap, [tki.b], [tkf.b])
    C.stt(thr.ap, tkf.ap, -TWO_PI, thr.ap, ALU.mult, ALU.add, [tkf.b, thr.b], [thr.b])
    C.act(RL.ap, lrdt_o.ap, AF.Exp, [lrdt_o.b], [RL.b], scale=float(L))

    C.barrier()
    C.sp = sp_T
    yown = C.alloc([128, 4, NOWN], F32)
    sp_U = C.sp
    NJ1 = NJ + 2
    un = C.alloc([128, T], BF16)
    ud2 = [C.alloc([128, L, NJ], BF16) for _ in range(2)]
    tab2 = [(C.alloc([128, 4, NJ], F32), C.alloc([128, 4, NJ], F32)) for _ in range(2)]
    TB = 2
    ang = C.alloc([128, TB, NJ], F32)
    akf = C.alloc([128, TB, NJ], F32)
    aki = C.alloc([128, TB, NJ], I32)
    iota3 = iota.ap.rearrange("p (o j) -> p o j", o=1).to_broadcast([128, TB, NJ])
    zr = C.alloc([128, NJ], F32)
    zi = C.alloc([128, NJ], F32)
    za = C.alloc([128, NJ], F32)
    zb = C.alloc([128, NJ], F32)
    Zr = C.alloc([128, NJ], F32)
    Zi = C.alloc([128, NJ], F32)
    X2 = [(C.alloc([128, 4, NJ1], BF16, nb=4), C.alloc([128, 4, NJ1], BF16, nb=4)) for _ in range(2)]
    yall = C.alloc([128, T], F32)
    yall_kj = yall.ap.rearrange("p (j k) -> p k j", k=L)

    def load_c(c):
        ud = ud2[c % 2]
        tabC, tabS = tab2[c % 2]
        C.dma(un.ap, uD.ap[:, c, :], [uD.b], [un.b])
        C.acopy(ud.ap, un.ap.rearrange("p (j k) -> p k j", k=L), [un.b], [ud.b])
        for cb in range(4 // TB):
            q0 = 4 * c + TB * cb
            thr3 = thr.ap[:, q0:q0 + TB].rearrange("p (q o) -> p q o", o=1).to_broadcast([128, TB, NJ])
            for (dst, shift) in ((tabS, 0.0), (tabC, math.pi / 2)):
                C.tt(ang.ap, iota3, thr3, ALU.mult, [iota.b, thr.b], [ang.b])
                if shift:
                    C.ts(ang.ap, ang.ap, shift, None, ALU.add, None, [ang.b], [ang.b])
                C.ts(akf.ap, ang.ap, 1.0 / TWO_PI, None, ALU.mult, None, [ang.b], [akf.b])
                C.cp(aki.ap, akf.ap, [akf.b], [aki.b])
                C.cp(akf.ap, aki.ap, [aki.b], [akf.b])
                C.stt(akf.ap, akf.ap, -TWO_PI, ang.ap, ALU.mult, ALU.add, [akf.b, ang.b], [akf.b])
                C.clamp_pi(akf.ap, akf.b)
                C.act(dst.ap[:, TB * cb:TB * cb + TB, :], akf.ap, AF.Sin, [akf.b], [dst.b])

    def l12(c, q4):
        ud = ud2[c % 2]
        tabC, tabS = tab2[c % 2]
        Xr, Xi = X2[c % 2]
        q = 4 * c + q4
        pr = C.ps()
        pi = C.ps()
        for k in range(L):
            C.mm(pr.ap[:, 0:NJ], L1r.ap[32 * q4:32 * q4 + 32, c, L - 1 - k, :], ud.ap[32 * q4:32 * q4 + 32, k, :], k == 0, k == L - 1,
                 [L1r.b, ud.b], [pr.b], tile_position=(32 * q4, 0))
        for k in range(L):
            C.mm(pi.ap[:, 0:NJ], L1i.ap[32 * q4:32 * q4 + 32, c, L - 1 - k, :], ud.ap[32 * q4:32 * q4 + 32, k, :], k == 0, k == L - 1,
                 [L1i.b, ud.b], [pi.b], tile_position=(32 * q4, 0))
        cosJ = Tl(tabC.ap[:, q4, :], tabC.b)
        sinJ = Tl(tabS.ap[:, q4, :], tabS.b)
        C.tt(zr.ap, pr.ap[:, 0:NJ], cosJ.ap, ALU.mult, [pr.b, cosJ.b], [zr.b])
        C.tt(za.ap, pi.ap[:, 0:NJ], sinJ.ap, ALU.mult, [pi.b, sinJ.b], [za.b])
        C.tt(zr.ap, zr.ap, za.ap, ALU.add, [zr.b, za.b], [zr.b])
        C.tt(zi.ap, pi.ap[:, 0:NJ], cosJ.ap, ALU.mult, [pi.b, cosJ.b], [zi.b])
        C.tt(zb.ap, pr.ap[:, 0:NJ], sinJ.ap, ALU.mult, [pr.b, sinJ.b], [zb.b])
        C.tt(zi.ap, zi.ap, zb.ap, ALU.subtract, [zi.b, zb.b], [zi.b])
        Rb = RL.ap[:, q:q + 1].to_broadcast([128, NJ])
        C.S.add("dve", lambda e, Rb=Rb: e.tensor_tensor_scan(out=Zr.ap, data0=Rb, data1=zr.ap, initial=0.0, op0=ALU.mult, op1=ALU.add),
                reads=[RL.b, zr.b], writes=[Zr.b])
        C.S.add("dve", lambda e, Rb=Rb: e.tensor_tensor_scan(out=Zi.ap, data0=Rb, data1=zi.ap, initial=0.0, op0=ALU.mult, op1=ALU.add),
                reads=[RL.b, zi.b], writes=[Zi.b])
        C.memset(Xr.ap[:, q4, 0:2], 0.0, [Xr.b[q4]])
        C.memset(Xi.ap[:, q4, 0:2], 0.0, [Xi.b[q4]])
        C.tt(za.ap, Zr.ap, cosJ.ap, ALU.mult, [Zr.b, cosJ.b], [za.b])
        C.tt(zb.ap, Zi.ap, sinJ.ap, ALU.mult, [Zi.b, sinJ.b], [zb.b])
        C.tt(Xr.ap[:, q4, 1:NJ + 1], za.ap, zb.ap, ALU.subtract, [za.b, zb.b], [Xr.b[q4]])
        C.tt(za.ap, Zi.ap, cosJ.ap, ALU.mult, [Zi.b, cosJ.b], [za.b])
        C.tt(zb.ap, Zr.ap, sinJ.ap, ALU.mult, [Zr.b, sinJ.b], [zb.b])
        C.tt(Xi.ap[:, q4, 1:NJ + 1], za.ap, zb.ap, ALU.add, [za.b, zb.b], [Xi.b[q4]])

    def l3(c, k):
        ud = ud2[c % 2]
        Xr, Xi = X2[c % 2]
        p = C.ps()
        for q4 in range(4):
            q = 4 * c + q4
            o = p.ap[32 * q4:32 * q4 + 32, 0:NJ]
            tp = (0, 32 * q4)
            C.mm(o, L3r.ap[:, q, k, :], Xr.ap[:, q4, 0:NJ], True, False, [L3r.b, Xr.b[q4]], [p.b], tile_position=tp)
            C.mm(o, L3i.ap[:, q, k, :], Xi.ap[:, q4, 0:NJ], False, False, [L3i.b, Xi.b[q4]], [p.b], tile_position=tp)
            for kp in range(k + 1):
                C.mm(o, FIR.ap[:, c, k - kp, 32 * q4:32 * q4 + 32], ud.ap[:, kp, :], False, kp == k, [FIR.b, ud.b], [p.b], tile_position=tp)
        C.stt(yall_kj[:, k, :], ud.ap[:, k, :], gt["d_fm"].ap[:, c:c + 1], p.ap[:, 0:NJ], ALU.mult, ALU.add,
              [ud.b, gt["d_fm"].b, p.b], [yall.b])

    load_c(0)
    for q4 in range(4):
        l12(0, q4)
    for c in range(4):
        if c + 1 < 4:
            load_c(c + 1)
        for i in range(4):
            if c + 1 < 4:
                l12(c + 1, i)
            for k in range(4 * i, 4 * i + 4):
                l3(c, k)
        ya4 = yall.ap.rearrange("p (ch b t) -> p ch b t", b=4, t=128)
        yo4 = yown.ap[:, c, :].rearrange("p (ch b t) -> p ch b t", b=2, t=128)
        for (ob, ba, bb) in ((0, 0, 1), (1, 3, 2)):
            C.ts(yo4[:, :, ob, :], ya4[:, :, ba, :], sel.ap[:, 0:1], None, ALU.mult, None, [yall.b, sel.b], [yown.b])
            C.stt(yo4[:, :, ob, :], ya4[:, :, bb, :], sel.ap[:, 1:2], yo4[:, :, ob, :], ALU.mult, ALU.add, [yall.b, sel.b, yown.b], [yown.b])
    C.barrier()
    C.sp = sp_U
    gg = C.alloc([128, 4, 512], F32)
    gb = C.alloc([128, 4, 512], BF16)
    ta = C.alloc([128, 512], F32)
    tb = C.alloc([128, 512], F32)
    sq_t = C.alloc([128, 8, 512], BF16)
    r2 = C.alloc([128, 512], F32)
    yn = C.alloc([128, 4, 512], BF16)
    CG = 2.0 * math.sqrt(2.0 / math.pi)
    for m in range(4):
        o0 = m * 512
        for k in range(4):
            y = yown.ap[:, k, o0:o0 + 512]
            C.tt(ta.ap, y, y, ALU.mult, [yown.b], [ta.b])
            C.ts(ta.ap, ta.ap, 0.044715, 1.0, ALU.mult, ALU.add, [ta.b], [ta.b])
            C.tt(ta.ap, ta.ap, y, ALU.mult, [ta.b, yown.b], [ta.b])
            C.act(tb.ap, ta.ap, AF.Sigmoid, [ta.b], [tb.b], scale=CG)
            C.tt(gg.ap[:, k, :], tb.ap, y, ALU.mult, [tb.b, yown.b], [gg.b])
            C.cp(gb.ap[:, k, :], gg.ap[:, k, :], [gg.b], [gb.b])
        for d in range(4):
            p = C.ps()
            for k in range(4):
                C.mm(p.ap, w_glu.ap[:, k, d * 128:(d + 1) * 128], gb.ap[:, k, :], k == 0, k == 3, [w_glu.b, gb.b], [p.b])
            C.act(tb.ap, p.ap, AF.Sigmoid, [p.b, gt["b_glu"].b], [tb.b], bias=gt["b_glu"].ap[:, d:d + 1], scale=1.0)
            C.tt(gg.ap[:, d, :], gg.ap[:, d, :], tb.ap, ALU.mult, [gg.b, tb.b], [gg.b])
        if dbg:
            C.dma(dbg["d_yssm"].ap.rearrange("(k p) t -> p k t", p=128)[:, :, o0:o0 + 512], gg.ap, [gg.b], [dbg["d_yssm"].b])
        rstd_from([(gg.ap[:, k, :], 128, gg.b) for k in range(4)], 512, 512, sq_t, r2)
        for k in range(4):
            C.stt(yn.ap[:, k, :], gg.ap[:, k, :], gt["g_os"].ap[:, k:k + 1], r2.ap, ALU.mult, ALU.mult,
                  [gg.b, gt["g_os"].b, r2.b], [yn.b])
        C.dma(ysD.ap[:, :, o0:o0 + 512], yn.ap, [yn.b], [ysD.b])


def own_blocks(j):
    out = []
    for c in range(8):
        out += [4 * c + (0 if j == 0 else 1), 4 * c + (3 if j == 0 else 2)]
    return out


def make_in_maps(inp):
    f32 = np.float32
    A = lambda a: np.ascontiguousarray(a)
    fm = lambda g: A(np.asarray(g, f32).reshape(-1, 128).T)
    col = lambda g: A(np.asarray(g, f32).reshape(-1, 1))
    sw = np.concatenate([np.arange(32, 64), np.arange(0, 32)])
    w_in = np.asarray(inp["w_in"][0], f32)
    w_in2 = A(np.concatenate([w_in[:, 0:640], w_in[:, 640:704], w_in[:, 640:704][:, sw], w_in[:, 704:1216]], axis=1))
    w_uq = np.asarray(inp["mla_w_uq"][0], f32).reshape(384, 4, 192)
    w_uq2 = A(np.concatenate([w_uq[:, :, 0:128], w_uq[:, :, 128:192], w_uq[:, :, 128:192][:, :, sw]], axis=2).reshape(384, 1024))
    w_ukv = np.asarray(inp["mla_w_ukv"][0], f32).reshape(256, 4, 256)
    w_ukv2 = A(np.concatenate([w_ukv[:, :, 0:128].reshape(256, 512), w_ukv[:, :, 128:256].reshape(256, 512)], axis=1))
    gq = np.asarray(inp["mla_qk_norm_q"][0], f32)
    gk = np.asarray(inp["mla_qk_norm_k"][0], f32)
    a_re = np.asarray(inp["ssm_a_re"][0], f32); a_im = np.asarray(inp["ssm_a_im"][0], f32)
    ldt = np.asarray(inp["ssm_log_dt"][0], f32)
    b_re = np.asarray(inp["ssm_b_re"][0], f32); b_im = np.asarray(inp["ssm_b_im"][0], f32)
    c_re = np.asarray(inp["ssm_c_re"][0], f32); c_im = np.asarray(inp["ssm_c_im"][0], f32)

    def in_side_gp(a):
        v = a.reshape(4, 4, 2, 64)
        v = np.transpose(v, (1, 2, 0, 3))
        return A(np.broadcast_to(v[:, :, None], (4, 2, 16, 4, 64)).reshape(128, 4, 64))

    def in_side_b(b):
        v = b.reshape(4, 4, 2, 64, 16)
        return A(np.transpose(v, (1, 2, 4, 0, 3)).reshape(128, 4, 64))

    def out_side_gp(a):
        v = a.reshape(16, 2, 64)
        return A(np.transpose(v, (1, 2, 0)).reshape(128, 16))

    def out_side_c(cc):
        v = cc.reshape(16, 2, 16, 64)
        return A(np.transpose(v, (1, 3, 0, 2)).reshape(128, 16, 16))

    def out_side_b(b):
        v = b.reshape(16, 2, 64, 16)
        return A(np.transpose(v, (1, 2, 0, 3)).reshape(128, 16, 16))

    ldt_gp = np.broadcast_to(ldt[:, None], (32, 64))
    common = {
        "ffn1_wg": A(inp["ffn1_w_gate"][0]), "ffn1_wu": A(inp["ffn1_w_up"][0]), "ffn1_wd": A(inp["ffn1_w_down"][0]),
        "ffn2_wg": A(inp["ffn2_w_gate"][0]), "ffn2_wu": A(inp["ffn2_w_up"][0]), "ffn2_wd": A(inp["ffn2_w_down"][0]),
        "w_in": w_in2, "w_uq": w_uq2, "w_ukv": w_ukv2,
        "w_glu": A(inp["ssm_w_glu"][0]), "w_o": A(inp["w_o"][0]), "w_xq": A(inp["xattn_w_q"][0]),
        "w_xkv": A(inp["xattn_w_kv"][0]), "w_xo": A(inp["xattn_w_o"][0]),
        "g_ffn1": fm(inp["ffn1_norm"][0]), "g_mix": fm(inp["mix_norm"][0]), "g_q": fm(inp["mla_q_norm"][0]),
        "g_kv": fm(inp["mla_kv_norm"][0]),
        "g_qn": col(gq[0:128]), "g_qr": col(gq[128:192]), "g_qrs": col(gq[128:192][sw]),
        "g_kn": col(gk[0:128]), "g_kr": col(gk[128:192]), "g_krs": col(gk[128:192][sw]),
        "g_om": fm(inp["out_norm_mla"][0]), "g_os": fm(inp["out_norm_ssm"][0]), "g_xa": fm(inp["xattn_norm"][0]),
        "g_mem": fm(inp["mem_norm"][0]), "g_xq": col(inp["xattn_q_norm"][0]), "g_xk": col(inp["xattn_k_norm"][0]),
        "g_ffn2": fm(inp["ffn2_norm"][0]), "b_glu": fm(inp["ssm_b_glu"][0]), "d_fm": fm(np.asarray(inp["ssm_d"][0], f32).reshape(-1)),
        "are_in": in_side_gp(a_re), "aim_in": in_side_gp(a_im), "ldt_in": in_side_gp(ldt_gp),
        "bre_in": in_side_b(b_re), "bim_in": in_side_b(b_im),
        "are_out": out_side_gp(a_re), "aim_out": out_side_gp(a_im), "ldt_out": out_side_gp(ldt_gp),
        "cre_out": out_side_c(c_re), "cim_out": out_side_c(c_im),
        "bre_out": out_side_b(b_re), "bim_out": out_side_b(b_im),
    }
    d = np.arange(64)
    invf = (10000.0 ** (-(d % 32).astype(np.float64) / 32.0)).astype(f32).reshape(64, 1)
    invf = (np.float32(10000.0) ** (-(np.arange(32, dtype=f32)) / np.float32(32))).astype(f32)
    invf = A(np.concatenate([invf, invf]).reshape(64, 1))
    sgn = A(np.where(d < 32, -1.0, 1.0).astype(f32).reshape(64, 1))
    iota = A(np.broadcast_to(np.arange(1, NJ + 1, dtype=f32)[None, :], (128, NJ)))
    r = np.arange(128)
    ee = (r // 16) % 2
    maskE = A(np.stack([(ee == 0), (ee == 1)], axis=1).astype(f32))
    kvec = A(np.broadcast_to(np.arange(0, L + 1, dtype=f32)[None, :], (128, L + 1)))
    common.update({"invf": invf, "sgn": sgn, "iota": iota, "maskE": maskE, "kvec": kvec})
    x = np.asarray(inp["x"], f32)
    mem = np.asarray(inp["mem"], f32)
    pos = np.asarray(inp["positions"]).astype(np.int32)
    maps = []
    for core in range(8):
        b, j = core // 2, core % 2
        ob = own_blocks(j)
        own_tok = np.concatenate([np.arange(g * 128, (g + 1) * 128) for g in ob])
        qg = np.array(ob[0:4])
        qpos = (qg[:, None] * 128 + np.arange(128)[None, :]).reshape(-1)
        am = np.zeros((128, 8, 512), f32)
        for a in range(8):
            kpos = a * 128 + np.arange(128)
            am[:, a, :] = (kpos[:, None] <= qpos[None, :]).astype(f32)
        m = dict(common)
        m.update({
            "xT": A(x[b].T), "memT": A(mem[b].T),
            "pos_all": A(np.broadcast_to(pos[b][None, :], (64, T))),
            "pos_own": A(np.broadcast_to(pos[b][own_tok][None, :], (64, NOWN))),
            "sel": A(np.broadcast_to(np.array([1.0, 0.0] if j == 0 else [0.0, 1.0], f32)[None, :], (128, 2))),
            "amask": am,
        })
        maps.append(m)
    return maps


_NC_CACHE = {}


def kernel(**inputs):
    if "nc" not in _NC_CACHE:
        _NC_CACHE["nc"] = build(False)
    nc = _NC_CACHE["nc"]
    maps = make_in_maps(inputs)
    res = run_bass_kernel_spmd(nc, maps, core_ids=list(range(8)))
    out = np.zeros((4, T, D), np.float32)
    for core in range(8):
        b, j = core // 2, core % 2
        ob = own_blocks(j)
        o = np.asarray(res.results[core]["outT"], np.float32)
        for i, g in enumerate(ob):
            out[b, g * 128:(g + 1) * 128, :] = o[:, i * 128:(i + 1) * 128].T
    return out
```
